# Optimizing a Trainium2 kernel written in Bass

```python
import functools
import jax, jax.numpy as jnp
from jax import lax
import numpy as np

D_MODEL = 1024
BATCH = 4
SEQ = 8192
DEPTH = 1
DEC_BATCH = 128
DEC_SEQ = 8
PAST_LEN = 16384
PAGE_SIZE = 128

MLA_HEADS = 8
QK_NOPE = 64
QK_ROPE = 32
V_HEAD = 64
KV_LORA = 128
Q_LORA = 256
MLA_WIDTH = MLA_HEADS * V_HEAD
ROPE_THETA = 10000.0
Q_BLOCK = 128
GLA_HEADS = 4
GLA_DK = 64
GLA_DV = 128
GLA_WIDTH = GLA_HEADS * GLA_DV
GLA_GATE_RANK = 16
GLA_TAU = 16.0
GLA_CHUNK = 64
P_DIM = 256
NORM_EPS = 1e-6
MIX_WIDTH = MLA_WIDTH + GLA_WIDTH
IN_SIZES = (Q_LORA, KV_LORA + QK_ROPE, MLA_WIDTH, GLA_HEADS * GLA_DK, GLA_HEADS * GLA_DK,
            GLA_WIDTH, GLA_GATE_RANK, GLA_WIDTH)
IN_DIM = Q_LORA + KV_LORA + QK_ROPE + MLA_WIDTH + 2 * GLA_HEADS * GLA_DK + GLA_WIDTH + GLA_GATE_RANK + GLA_WIDTH

kernel_name = 'hymba_mla_gla_decode_step'


def rmsnorm(x, g):
    xf = x.astype(jnp.float32)
    y = xf * lax.rsqrt(jnp.mean(xf * xf, axis=-1, keepdims=True) + NORM_EPS)
    return (y * g.astype(jnp.float32)).astype(x.dtype)


def rope_tables(pos):
    inv = 1.0 / (ROPE_THETA ** (jnp.arange(0, QK_ROPE, 2, dtype=jnp.float32) / QK_ROPE))
    ang = pos[:, None] * inv[None, :]
    return jnp.cos(ang), jnp.sin(ang)


def apply_rope(x, cos, sin):
    x1, x2 = jnp.split(x.astype(jnp.float32), 2, axis=-1)
    return jnp.concatenate([x1 * cos - x2 * sin, x1 * sin + x2 * cos], axis=-1).astype(x.dtype)


def mla_prompt_attention(q_nope, q_rope, c_kv, k_rope, w_uk, w_uv):
    B, S, H, _ = q_nope.shape
    scale = (QK_NOPE + QK_ROPE) ** -0.5
    k_nope = jnp.einsum('bsc,chd->bshd', c_kv, w_uk)
    v = jnp.einsum('bsc,chd->bshd', c_kv, w_uv)
    nb = S // Q_BLOCK
    qn = q_nope.reshape(B, nb, Q_BLOCK, H, QK_NOPE).transpose(1, 0, 2, 3, 4)
    qr = q_rope.reshape(B, nb, Q_BLOCK, H, QK_ROPE).transpose(1, 0, 2, 3, 4)
    kpos = jnp.arange(S)

    def block(args):
        qn_b, qr_b, start = args
        s = (jnp.einsum('bqhd,bkhd->bhqk', qn_b, k_nope)
             + jnp.einsum('bqhr,bkr->bhqk', qr_b, k_rope)).astype(jnp.float32) * scale
        qpos = start + jnp.arange(Q_BLOCK)
        s = jnp.where(kpos[None, :] <= qpos[:, None], s, -jnp.inf)
        p = jax.nn.softmax(s, axis=-1).astype(v.dtype)
        return jnp.einsum('bhqk,bkhd->bqhd', p, v)

    out = lax.map(block, (qn, qr, jnp.arange(nb) * Q_BLOCK))
    return out.transpose(1, 0, 2, 3, 4).reshape(B, S, H, V_HEAD)


def mla_decode_attention(q_nope, q_rope, c_new, kr_new, w_uk, w_uv, ckv_pool, kr_pool, page_table):
    T = q_nope.shape[1]
    scale = (QK_NOPE + QK_ROPE) ** -0.5
    past_len = page_table.shape[1] * ckv_pool.shape[1]
    q_lat = jnp.einsum('bthd,chd->bthc', q_nope, w_uk)
    kpos = jnp.arange(past_len + T)
    qpos = past_len + jnp.arange(T)
    mask = kpos[None, :] <= qpos[:, None]

    def one(args):
        ql, qr, pages, cn, krn = args
        c = jnp.concatenate([ckv_pool[pages].reshape(-1, KV_LORA), cn.astype(ckv_pool.dtype)], axis=0)
        kr = jnp.concatenate([kr_pool[pages].reshape(-1, QK_ROPE), krn.astype(kr_pool.dtype)], axis=0)
        s = (jnp.einsum('thc,lc->htl', ql, c)
             + jnp.einsum('thr,lr->htl', qr, kr)).astype(jnp.float32) * scale
        s = jnp.where(mask[None], s, -jnp.inf)
        p = jax.nn.softmax(s, axis=-1).astype(c.dtype)
        return jnp.einsum('htl,lc->thc', p, c)

    o_lat = lax.map(one, (q_lat, q_rope, page_table, c_new, kr_new))
    return jnp.einsum('bthc,chd->bthd', o_lat, w_uv.astype(o_lat.dtype))


def gla_chunked(q, k, v, log_a, s0, chunk):
    B, T, H, DK = q.shape
    DV = v.shape[-1]
    n = T // chunk
    f32 = jnp.float32
    q = q.astype(f32) * (DK ** -0.5)

    def to_chunks(a):
        return a.astype(f32).reshape(B, n, chunk, H, a.shape[-1]).transpose(1, 0, 3, 2, 4)

    tri = jnp.tril(jnp.ones((chunk, chunk), dtype=bool))

    def step(S, inp):
        qc, kc, vc, gc = inp
        b = jnp.cumsum(gc, axis=-2)
        o_inter = jnp.einsum('bhtk,bhkv->bhtv', qc * jnp.exp(b), S)
        diff = b[:, :, :, None, :] - b[:, :, None, :, :]
        decay = jnp.exp(jnp.where(tri[None, None, :, :, None], diff, -jnp.inf))
        A = jnp.einsum('bhtk,bhsk,bhtsk->bhts', qc, kc, decay)
        o = o_inter + jnp.einsum('bhts,bhsv->bhtv', A, vc)
        b_last = b[:, :, -1:, :]
        S = (jnp.exp(b_last[:, :, 0, :])[..., None] * S
             + jnp.einsum('bhsk,bhsv->bhkv', kc * jnp.exp(b_last - b), vc))
        return S, o

    S, o = lax.scan(step, s0.astype(f32), (to_chunks(q), to_chunks(k), to_chunks(v), to_chunks(log_a)))
    return o.transpose(1, 0, 3, 2, 4).reshape(B, T, H, DV), S


def hybrid_layer(h, p_l, pos, gla_s0, gla_chunk, mla_attend, g_mix_norm, w_in, g_qnorm, w_qup,
                 g_kvnorm, w_uk, w_uv, w_gla_a2, b_gla_a, g_gla_onorm, w_out, w_ple_gate, w_ple_proj):
    B, T, _ = h.shape
    xn = rmsnorm(h, g_mix_norm)
    z = xn @ w_in
    bounds = np.cumsum(IN_SIZES)[:-1].tolist()
    q_c, kv_c, gate_mla, q_g, k_g, v_g, a_g, gate_gla = jnp.split(z, bounds, axis=-1)
    cos, sin = rope_tables(pos)
    q = (rmsnorm(q_c, g_qnorm) @ w_qup).reshape(B, T, MLA_HEADS, QK_NOPE + QK_ROPE)
    q_nope = q[..., :QK_NOPE]
    q_rope = apply_rope(q[..., QK_NOPE:], cos[:, None, :], sin[:, None, :])
    c_kv = rmsnorm(kv_c[..., :KV_LORA], g_kvnorm)
    k_rope = apply_rope(kv_c[..., KV_LORA:], cos, sin)
    o_mla = mla_attend(q_nope, q_rope, c_kv, k_rope, w_uk, w_uv).reshape(B, T, MLA_WIDTH).astype(h.dtype)
    o_mla = o_mla * jax.nn.silu(gate_mla)
    gq = q_g.reshape(B, T, GLA_HEADS, GLA_DK)
    gk = k_g.reshape(B, T, GLA_HEADS, GLA_DK)
    gv = v_g.reshape(B, T, GLA_HEADS, GLA_DV)
    log_a = (jax.nn.log_sigmoid((a_g @ w_gla_a2 + b_gla_a).astype(jnp.float32)) / GLA_TAU
             ).reshape(B, T, GLA_HEADS, GLA_DK)
    o_gla, s_new = gla_chunked(gq, gk, gv, log_a, gla_s0, gla_chunk)
    o_gla = rmsnorm(o_gla.astype(h.dtype), g_gla_onorm).reshape(B, T, GLA_WIDTH) * jax.nn.silu(gate_gla)
    h = h + jnp.concatenate([o_mla, o_gla], axis=-1) @ w_out
    h = h + jax.nn.sigmoid(h @ w_ple_gate) * (p_l @ w_ple_proj)
    return h, c_kv, k_rope, s_new.astype(h.dtype)


def setup_inputs(seed: int = 0) -> dict:
    key = jax.random.key(seed)
    ks = jax.random.split(key, 32)
    n_pages = PAST_LEN // PAGE_SIZE
    n_pool = (DEC_BATCH * n_pages * 5) // 4
    f32 = jnp.float32

    def nrm(k, shape, scale):
        return jax.random.normal(k, shape, f32) * scale

    def gain(k, shape):
        return 1.0 + 0.05 * jax.random.normal(k, shape, f32)

    page_table = jax.random.permutation(ks[7], n_pool)[:DEC_BATCH * n_pages].reshape(DEC_BATCH, n_pages).astype(jnp.int32)
    return {
        'x_prompt': nrm(ks[0], (BATCH, SEQ, D_MODEL), 1.0),
        'x_sample': nrm(ks[1], (DEC_BATCH, DEC_SEQ, D_MODEL), 1.0),
        'p_prompt': nrm(ks[2], (DEPTH, BATCH, SEQ, P_DIM), 1.0),
        'p_sample': nrm(ks[3], (DEPTH, DEC_BATCH, DEC_SEQ, P_DIM), 1.0),
        'cache_ckv': nrm(ks[4], (DEPTH, n_pool, PAGE_SIZE, KV_LORA), 1.0),
        'cache_krope': nrm(ks[5], (DEPTH, n_pool, PAGE_SIZE, QK_ROPE), 1.0),
        'state_gla': nrm(ks[6], (DEPTH, DEC_BATCH, GLA_HEADS, GLA_DK, GLA_DV), 1.0),
        'page_table': page_table,
        'g_mix_norm': gain(ks[8], (DEPTH, D_MODEL)),
        'w_in': nrm(ks[9], (DEPTH, D_MODEL, IN_DIM), D_MODEL ** -0.5),
        'g_qnorm': gain(ks[10], (DEPTH, Q_LORA)),
        'w_qup': nrm(ks[11], (DEPTH, Q_LORA, MLA_HEADS * (QK_NOPE + QK_ROPE)), Q_LORA ** -0.5),
        'g_kvnorm': gain(ks[12], (DEPTH, KV_LORA)),
        'w_uk': nrm(ks[13], (DEPTH, KV_LORA, MLA_HEADS, QK_NOPE), KV_LORA ** -0.5),
        'w_uv': nrm(ks[14], (DEPTH, KV_LORA, MLA_HEADS, V_HEAD), KV_LORA ** -0.5),
        'w_gla_a2': nrm(ks[15], (DEPTH, GLA_GATE_RANK, GLA_HEADS * GLA_DK), GLA_GATE_RANK ** -0.5),
        'b_gla_a': nrm(ks[16], (DEPTH, GLA_HEADS * GLA_DK), 0.5),
        'g_gla_onorm': gain(ks[17], (DEPTH, GLA_DV)),
        'w_out': nrm(ks[18], (DEPTH, MIX_WIDTH, D_MODEL), MIX_WIDTH ** -0.5),
        'w_ple_gate': nrm(ks[19], (DEPTH, D_MODEL, D_MODEL), D_MODEL ** -0.5),
        'w_ple_proj': nrm(ks[20], (DEPTH, P_DIM, D_MODEL), P_DIM ** -0.5),
        'g_final': gain(ks[21], (D_MODEL,)),
    }


def reference(x_prompt, x_sample, p_prompt, p_sample, cache_ckv, cache_krope, state_gla, page_table,
              g_mix_norm, w_in, g_qnorm, w_qup, g_kvnorm, w_uk, w_uv, w_gla_a2, b_gla_a, g_gla_onorm,
              w_out, w_ple_gate, w_ple_proj, g_final):
    B, S, _ = x_prompt.shape
    T = x_sample.shape[1]
    past_len = page_table.shape[1] * cache_ckv.shape[2]
    pos_p = jnp.arange(S, dtype=jnp.float32)
    pos_s = past_len + jnp.arange(T, dtype=jnp.float32)
    s0_prompt = jnp.zeros((B, GLA_HEADS, GLA_DK, GLA_DV), jnp.float32)
    h_p, h_s = x_prompt, x_sample
    ckv_p, kr_p, st_p, ckv_s, kr_s, st_s = [], [], [], [], [], []
    for i in range(DEPTH):
        w = (g_mix_norm[i], w_in[i], g_qnorm[i], w_qup[i], g_kvnorm[i], w_uk[i], w_uv[i],
             w_gla_a2[i], b_gla_a[i], g_gla_onorm[i], w_out[i], w_ple_gate[i], w_ple_proj[i])
        h_p, c1, r1, s1 = hybrid_layer(h_p, p_prompt[i], pos_p, s0_prompt, min(GLA_CHUNK, S),
                                       mla_prompt_attention, *w)
        decode_attend = functools.partial(mla_decode_attention, ckv_pool=cache_ckv[i],
                                          kr_pool=cache_krope[i], page_table=page_table)
        h_s, c2, r2, s2 = hybrid_layer(h_s, p_sample[i], pos_s, state_gla[i], T, decode_attend, *w)
        ckv_p.append(c1); kr_p.append(r1); st_p.append(s1)
        ckv_s.append(c2); kr_s.append(r2); st_s.append(s2)
    y_prompt = rmsnorm(h_p, g_final)
    y_sample = rmsnorm(h_s, g_final)
    return (y_prompt, y_sample, jnp.stack(ckv_p), jnp.stack(kr_p), jnp.stack(st_p),
            jnp.stack(ckv_s), jnp.stack(kr_s), jnp.stack(st_s))
```

```python
import contextlib
import numpy as np
import concourse.bass as bass
import concourse.mybir as mybir
from concourse.bass_utils import run_bass_kernel_spmd

F32 = mybir.dt.float32
BF16 = mybir.dt.bfloat16
I32 = mybir.dt.int32
AF = mybir.ActivationFunctionType
ALU = mybir.AluOpType
AX = mybir.AxisListType

D = 1024
NCH = 8
SC = 96.0 ** -0.5
NEG = -30000.0
C_KV, C_KG, C_VG, C_A, C_QC, C_GM, C_QG, C_GG = 0, 160, 416, 928, 944, 1200, 1712, 1968


class Buf:
    __slots__ = ("name", "last_w", "readers", "dsem", "dcount")

    def __init__(self, name):
        self.name = name
        self.last_w = None
        self.readers = []
        self.dsem = None
        self.dcount = 0


class Op:
    __slots__ = ("eng", "fn", "deps", "is_dma", "done", "needed", "waits", "idx")

    def __init__(self, eng, fn, is_dma):
        self.eng = eng
        self.fn = fn
        self.deps = []
        self.is_dma = is_dma
        self.done = None
        self.needed = False
        self.waits = []


class Tl:
    __slots__ = ("t", "b")

    def __init__(self, t, b):
        self.t = t
        self.b = b


class Prog:
    ENGS = ("pe", "act", "dve", "pool", "sp")
    EPOCH = 12000

    def __init__(self, nc, es):
        self.nc = nc
        self.es = es
        self.ops = {e: [] for e in self.ENGS}
        self.order = []
        self.out_dmas = []

    def sem(self, name):
        return self.es.enter_context(self.nc.semaphore(name))

    def tile(self, name, shape, dtype, psum=False):
        if psum:
            t = self.es.enter_context(self.nc.psum_tensor(name, shape, dtype))
        else:
            t = self.es.enter_context(self.nc.sbuf_tensor(name, shape, dtype))
        return Tl(t, Buf(name))

    def _add(self, eng, fn, reads, writes, is_dma, dsem_buf=None):
        op = Op(eng, fn, is_dma)
        deps = []
        for b in reads:
            if b.last_w is not None:
                deps.append((b.last_w, "raw"))
        for b in writes:
            if b.last_w is not None:
                deps.append((b.last_w, "waw"))
            for r in b.readers:
                deps.append((r, "war"))
        for d, kind in deps:
            if d is op:
                continue
            same = (d.eng == eng) and (not d.is_dma) and (not is_dma)
            if same and (kind != "raw" or eng == "pe"):
                continue
            op.deps.append(d)
            d.needed = True
        for b in reads:
            b.readers.append(op)
        for b in writes:
            b.last_w = op
            b.readers = []
        if is_dma:
            if dsem_buf.dsem is None:
                dsem_buf.dsem = self.sem("d_" + dsem_buf.name)
            dsem_buf.dcount += 16
            op.done = (dsem_buf.dsem, dsem_buf.dcount)
        self.ops[eng].append(op)
        self.order.append(op)
        return op

    def pe(self, fn, reads, writes):
        return self._add("pe", fn, reads, writes, False)

    def act(self, fn, reads, writes):
        return self._add("act", fn, reads, writes, False)

    def dve(self, fn, reads, writes):
        return self._add("dve", fn, reads, writes, False)

    def pool(self, fn, reads, writes):
        return self._add("pool", fn, reads, writes, False)

    def dma(self, q, fn, reads, writes, sb, is_out=False):
        op = self._add(q, fn, reads, writes, True, sb)
        if is_out:
            self.out_dmas.append(op)
        return op

    def barrier(self):
        lasts = [self.ops[e][-1] for e in self.ENGS if self.ops[e]]
        dmas = [o for o in self.order if o.is_dma]
        for e in self.ENGS:
            op = Op(e, lambda eng: eng.nop(), False)
            for d in lasts + dmas:
                if d.is_dma or d.eng != e:
                    op.deps.append(d)
                    d.needed = True
            self.ops[e].append(op)
            self.order.append(op)

    def finalize(self):
        esems = {}
        for e in self.ENGS:
            cnt = 0
            ep = 0
            cur = None
            for op in self.ops[e]:
                if op.is_dma or not op.needed:
                    continue
                if cur is None or cnt >= self.EPOCH:
                    cur = self.sem(f"e_{e}_{ep}")
                    ep += 1
                    cnt = 0
                cnt += 1
                op.done = (cur, cnt)
        fin = Op("sp", None, False)
        last = {}
        for o in self.out_dmas:
            s, v = o.done
            k = id(s)
            if k not in last or last[k][1] < v:
                last[k] = (s, v)
        fin.waits = list(last.values())
        waited = {e: {} for e in self.ENGS}
        for op in self.order:
            w = {}
            for d in op.deps:
                s, v = d.done
                k = id(s)
                if k not in w or w[k][1] < v:
                    w[k] = (s, v)
            wm = waited[op.eng]
            for k, (s, v) in w.items():
                if wm.get(k, 0) >= v:
                    continue
                wm[k] = v
                op.waits.append((s, v))
        self.fin = fin

    def emit(self, block):
        nc = self.nc

        def run(e, eng):
            for op in self.ops[e]:
                for s, v in op.waits:
                    eng.wait_ge(s, v)
                ins = op.fn(eng)
                if op.is_dma:
                    ins.then_inc(op.done[0], 16)
                elif op.needed:
                    ins.then_inc(op.done[0], 1)
            if e == "sp":
                for s, v in self.fin.waits:
                    eng.wait_ge(s, v)

        @block.tensor
        def _(eng):
            run("pe", eng)

        @block.scalar
        def _(eng):
            run("act", eng)

        @block.vector
        def _(eng):
            run("dve", eng)

        @block.gpsimd
        def _(eng):
            run("pool", eng)

        @block.sync
        def _(eng):
            run("sp", eng)


def build(NS, NG8, NPOOL):
    SEQL = 256 * NS
    NKB = 2 * NS
    NOWN = NS * 128
    nc = bass.Bass("TRN2", target_bir_lowering=False)
    es = contextlib.ExitStack()
    P = Prog(nc, es)

    def din(name, shape, dt=F32):
        return nc.dram_tensor(name, list(shape), dt, kind="ExternalInput").ap()

    def dout(name, shape, dt=F32):
        return nc.dram_tensor(name, list(shape), dt, kind="ExternalOutput").ap()

    x_all = din("x_all", [SEQL, D]); x_own = din("x_own", [NOWN, D]); p_own = din("p_own", [NOWN, 256])
    x_smp = din("x_smp", [128, D]); p_smp = din("p_smp", [128, 256])
    cs_all = din("cs_all", [SEQL, 64]); cs_own = din("cs_own", [NOWN, 64]); cs_smp = din("cs_smp", [128, 64])
    w_in_d = din("w_in_r", [D, 2480]); w_qup_d = din("w_qup_r", [256, 768]); w_ukT_d = din("w_ukT", [64, 1024])
    w_uv_d = din("w_uv", [128, 512]); w_a2_d = din("w_a2", [16, 256]); w_out_d = din("w_out", [D, D])
    w_pg_d = din("w_pg", [D, D]); w_pp_d = din("w_pp", [256, D])
    g_mix_d = din("g_mix", [1, D]); g_q_d = din("g_q", [1, 256]); g_kv_d = din("g_kv", [1, 128])
    g_on_d = din("g_on", [1, 128]); b_a_d = din("b_a", [1, 256]); g_fin_d = din("g_fin", [1, D])
    mk4_d = din("mk4", [128, 512]); par_d = din("par", [128, 2])
    gc_d = din("gla_c", [128, 6 * 128])
    sx_d = din("smp_x", [128, 2 + 8 + 16])
    m2_d = din("m2", [128, 1024])
    dmask_d = din("dmask", [64, 16 * 128])
    idb_d = din("ident_b", [128, 128]); idf_d = din("ident_f", [128, 128])
    ptl_d = din("ptl", [128, 16 * NG8], I32); pm16_d = din("pm16", [128, 1], I32)
    pool_c = din("pool_c", [NPOOL * 16, 1024]); pool_r = din("pool_r", [NPOOL * 16, 256])
    st_in = din("st_in", [16, 4, 64, 128])

    y_own = dout("y_own", [NOWN, D]); y_smp = dout("y_smp", [128, D])
    ckv_o = dout("ckv_o", [SEQL, 128]); kr_o = dout("kr_o", [SEQL, 32]); stp_o = dout("stp_o", [4, 64, 128])
    ckvs_o = dout("ckvs_o", [128, 128]); krs_o = dout("krs_o", [128, 32]); sts_o = dout("sts_o", [16, 4, 64, 128])

    T = P.tile
    W_in = T("W_in", [128, NCH, 2480], BF16); W_qup = T("W_qup", [128, 2, 768], BF16)
    W_ukT = T("W_ukT", [64, 1024], BF16); W_uv = T("W_uv", [128, 512], BF16); W_a2 = T("W_a2", [16, 256], BF16)
    W_out = T("W_out", [128, NCH, D], BF16); W_pg = T("W_pg", [128, NCH, D], BF16); W_pp = T("W_pp", [128, 2, D], BF16)
    G_mix = T("G_mix", [128, D], F32); G_fin = T("G_fin", [128, D], F32); G_q = T("G_q", [128, 256], F32)
    G_kv = T("G_kv", [128, 128], F32); G_on = T("G_on", [128, 128], F32); B_a = T("B_a", [128, 256], F32)
    PAR = T("PAR", [128, 2], F32); GC = T("GC", [128, 768], F32)
    SX = T("SX", [128, 26], F32)
    IDB = T("IDB", [128, 128], BF16); IDF = T("IDF", [128, 128], F32)
    ONESB = T("ONESB", [128, 1], BF16); ONESF = T("ONESF", [1, 128], F32)
    CMAX = T("CMAX", [128, 1], F32)
    WB = Buf("weights")

    def ld(q, dst, src, dap=None):
        P.dma(q, lambda e: e.dma_start(out=(dst.t[:] if dap is None else dap), in_=src), [], [dst.b], WB)

    for c in range(NCH):
        ld("pool", W_in, w_in_d[c * 128:(c + 1) * 128, :], W_in.t[:, c, :])
        ld("pool", W_out, w_out_d[c * 128:(c + 1) * 128, :], W_out.t[:, c, :])
        ld("pool", W_pg, w_pg_d[c * 128:(c + 1) * 128, :], W_pg.t[:, c, :])
    for c in range(2):
        ld("pool", W_qup, w_qup_d[c * 128:(c + 1) * 128, :], W_qup.t[:, c, :])
        ld("pool", W_pp, w_pp_d[c * 128:(c + 1) * 128, :], W_pp.t[:, c, :])
    ld("pool", W_ukT, w_ukT_d); ld("pool", W_uv, w_uv_d); ld("pool", W_a2, w_a2_d)
    ld("pool", IDB, idb_d)
    for dst, src in ((G_mix, g_mix_d), (G_fin, g_fin_d), (G_q, g_q_d), (G_kv, g_kv_d), (G_on, g_on_d), (B_a, b_a_d)):
        ld("sp", dst, src.partition_broadcast(128))
    for dst, src in ((PAR, par_d), (GC, gc_d), (SX, sx_d), (IDF, idf_d)):
        ld("sp", dst, src)
    P.pool(lambda e: e.memset(ONESB.t[:], 1.0), [], [ONESB.b])
    P.pool(lambda e: e.memset(ONESF.t[:], 1.0), [], [ONESF.b])
    CT = T("CT", [128, 128], F32)
    P.dve(lambda e: e.tensor_tensor(out=CT.t[:], in0=G_kv.t[:], in1=G_kv.t[:], op=ALU.mult), [G_kv.b, WB], [CT.b])
    P.dve(lambda e: e.reduce_max(out=CMAX.t[:], in_=CT.t[:], axis=AX.X), [CT.b], [CMAX.b])
    P.act(lambda e: e.activation(out=CMAX.t[:], in_=CMAX.t[:], func=AF.Ln, scale=128.0), [CMAX.b], [CMAX.b])
    P.act(lambda e: e.activation(out=CMAX.t[:], in_=CMAX.t[:], func=AF.Exp, scale=0.5), [CMAX.b], [CMAX.b])

    RMAX2 = T("RMAX2", [128, 1], F32)
    P.dve(lambda e: e.memset(RMAX2.t[:], 0.0), [], [RMAX2.b])

    PZ2 = T("pz", [128, 1024], F32, True); PS2 = T("ps", [128, 1024], F32, True)
    pz = [Tl(PZ2.t[:, 0:512], Buf("pz0")), Tl(PZ2.t[:, 512:1024], Buf("pz1"))]
    ps = [Tl(PS2.t[:, 0:512], Buf("ps0")), Tl(PS2.t[:, 512:1024], Buf("ps1"))]
    pa = [T("pa0", [128, 512], F32, True), T("pa1", [128, 512], F32, True)]
    ptp = T("ptp", [128, 1024], BF16, True)
    ppt = T("ppt", [128, 1024], BF16, True)
    ppth = [Tl(ppt.t[:, 0:512], Buf("ppt_a")), Tl(ppt.t[:, 512:1024], Buf("ppt_b"))]

    XO = T("XO", [128, D], F32); PIN = T("PIN", [128, 256], F32)
    CSO = T("CSO", [128, 64], F32)
    JUNK = T("JUNK", [128, D], BF16)
    XN = T("XN", [128, D], BF16); XNT = T("XNT", [128, NCH, 128], BF16)
    SS = T("SS", [128, 1], F32); RS = T("RS", [128, 1], F32)
    SS8 = T("SS8", [128, 8], F32); RS8 = T("RS8", [128, 8], F32)
    CKF = T("CKF", [128, 128], F32); CKB = T("CKB", [128, 128], BF16)
    KRF = T("KRF", [128, 32], F32); KRB = T("KRB", [128, 96], BF16); RT1 = T("RT1", [128, 256], F32); RT2 = T("RT2", [128, 256], F32)
    NR2 = T("NR2", [128, 1], F32)
    AGT = T("AGT", [16, 128], BF16)
    T1 = T("T1", [128, 256], F32); LSB = T("LSB", [128, 256], F32); ERB = T("ERB", [128, 256], F32)
    KD = T("KD", [128, 256], BF16); VB = T("VB", [128, 512], BF16); EBL = T("EBL", [128, 8], F32)
    EBT = T("EBT", [64, 512], F32); ENBT = T("ENBT", [64, 512], F32)
    QET = T("QET", [64, 512], BF16); KET = T("KET", [64, 512], BF16); ATM = T("ATM", [128, 512], BF16)
    SOWNB = T("SOWNB", [64, 4, 128], BF16)
    OG = T("OG", [128, 512], F32)
    SIG = T("SIG", [128, D], F32)
    CAT = T("CAT", [128, D], BF16); CATT = T("CATT", [128, NCH, 128], BF16); H1T = CATT
    QCN = T("QCN", [128, 256], BF16); QCNT = T("QCNT", [128, 2, 128], BF16)
    QNT = T("QNT", [64, 1024], BF16); QLT = T("QLT", [128, 1024], BF16)
    QRB = T("QRB", [128, 768], BF16); QRT = T("QRT", [96, 1024], BF16)
    NL2 = T("NL2", [128, 8], F32); NRQ = T("NRQ", [128, 8], F32); NEGM = T("NEGM", [128, 8], F32)
    R1 = T("R1", [1, 1], F32); RM = T("RM", [128, 1], F32)
    SM = T("SM", [128, 512], F32)
    PB = [T(f"PB{k}", [128, 512], BF16) for k in range(2)]
    PT = [T(f"PT{k}", [128, 512], BF16) for k in range(2)]
    LR = T("LR", [128, 8], F32)
    OLAT = T("OLAT", [128, 8, 128], BF16); OLT = T("OLT", [128, 1024], BF16); QSQ = OLT
    H1 = XO; H1B = JUNK; H2 = XO; YO = SIG
    PBF = T("PBF", [128, 256], BF16); PTT = T("PTT", [128, 2, 128], BF16)
    es_main = P.es
    es_p = contextlib.ExitStack(); P.es = es_p
    NGM = max(1, (NKB + 3) // 4); NCC = (NGM + 2) // 3
    CKVT = T("CKVT", [128, SEQL], BF16); KRT = T("KRT", [96, NCC * 512], BF16); CKVN = T("CKVN", [128, NKB, 128], BF16)
    kvb = [Buf(f"kv{j}") for j in range(NKB)]
    ST = [T(f"ST{k}", [64, 4, 128], F32) for k in range(3)]
    MK4 = T("MK4", [128, 512], BF16)
    XS1 = T("XS", [128, D], F32); XS = [XS1, XS1]
    CSS1 = T("CSS", [128, 64], F32); CSS = [CSS1, CSS1]
    LSUM = T("LSUM", [128, 8, NGM], F32)
    P.es = es_main
    ld("pool", MK4, mk4_d)
    P.dve(lambda e: e.memset(ST[0].t[:], 0.0), [], [ST[0].b])
    Z = type("Z", (), {})()

    def alloc_sample():
        Z.PGC = [T(f"PGC{k}", [128, 8, 128], F32) for k in range(2)]; Z.PGR = [T(f"PGR{k}", [128, 8, 32], F32) for k in range(2)]
        Z.PGCB = [T(f"PGCB{k}", [128, 8, 128], BF16) for k in range(2)]; Z.PGRB = [T(f"PGRB{k}", [128, 8, 32], BF16) for k in range(2)]
        Z.CTS = [T(f"CTS{k}", [128, 512], BF16) for k in range(1)]; Z.KTS = [T(f"KTS{k}", [32, 512], BF16) for k in range(1)]
        Z.CKVT_S = T("CKVT_S", [128, 128], BF16); Z.KRT_S = T("KRT_S", [32, 128], BF16); Z.CKVN_S = T("CKVN_S", [128, 128], BF16)
        Z.MRUN = T("MRUN", [64, 1], F32); Z.MNEW = T("MNEW", [64, 1], F32); Z.NMN = T("NMN", [64, 1], F32)
        Z.ALP = T("ALP", [64, 1], F32); Z.LRUN = T("LRUN", [64, 1], F32); Z.LG = T("LG", [64, 1], F32); Z.GMX = T("GMX", [64, 1], F32)
        Z.ACC = T("ACC", [64, 128], F32); Z.ACCB = T("ACCB", [64, 128], BF16)
        s0 = T("S0F", [128, 8, 128], F32); Z.S0F = [s0, s0]; Z.S0B = T("S0B", [128, 8, 128], BF16)
        Z.L2 = T("L2", [128, 2, 64], F32); Z.EBS = T("EBS", [128, 8], F32)
        Z.QEX = T("QEX", [128, 8, 128], BF16); Z.KDX = T("KDX", [128, 16, 64], BF16)
        Z.QEN = T("QEN", [128, 2, 64], BF16)
        Z.SNEW = T("SNEW", [128, 8, 128], F32)
        Z.M2 = T("M2", [128, 1024], BF16); Z.DMASK = T("DMASK", [64, 2048], BF16)
        Z.PTL = T("PTL", [128, 16 * NG8], I32); Z.PM16 = T("PM16", [128, 1], I32); Z.IDX = T("IDX", [128, 16 * NG8], I32)
        ld("pool", Z.M2, m2_d); ld("pool", Z.DMASK, dmask_d); ld("sp", Z.PTL, ptl_d); ld("sp", Z.PM16, pm16_d)
        P.dve(lambda e: e.tensor_scalar(out=Z.IDX.t[:], in0=Z.PTL.t[:], scalar1=16.0, scalar2=Z.PM16.t[:, 0:1],
                                        op0=ALU.mult, op1=ALU.add), [Z.PTL.b, Z.PM16.b, WB], [Z.IDX.b])

    ident = IDB.t[:]

    def transposes(src, dst, nch, cp="act"):
        for c in range(nch):
            P.pe(lambda e, c=c: e.transpose(out=ptp.t[:, c * 128:(c + 1) * 128], in_=src.t[:, c * 128:(c + 1) * 128],
                                            identity=ident), [src.b, IDB.b], [ptp.b])
        f = (lambda e: e.activation(out=dst.t[:].rearrange("p c t -> p (c t)"), in_=ptp.t[:, 0:nch * 128], func=AF.Copy)) \
            if cp == "act" else (lambda e: e.tensor_copy(out=dst.t[:].rearrange("p c t -> p (c t)"), in_=ptp.t[:, 0:nch * 128]))
        (P.act if cp == "act" else P.dve)(f, [ptp.b], [dst.b])

    def linear(out_ps, xT, nch, W, c0, n):
        for c in range(nch):
            P.pe(lambda e, c=c: e.matmul(out_ps.t[:, 0:n], lhsT=xT.t[:, c, :], rhs=W.t[:, c, c0:c0 + n],
                                         start=(c == 0), stop=(c == nch - 1)), [xT.b, W.b, WB], [out_ps.b])

    def linearT(out_ap, out_b, W, c0, m, xT, nch):
        for c in range(nch):
            P.pe(lambda e, c=c: e.matmul(out_ap, lhsT=W.t[:, c, c0:c0 + m], rhs=xT.t[:, c, :],
                                         start=(c == 0), stop=(c == nch - 1)), [xT.b, W.b, WB], [out_b])

    def sumsq(src_ap, src_b, n, out_ap, out_b):
        P.dve(lambda e: e.scalar_tensor_tensor(out=JUNK.t[:, 0:n], in0=src_ap, scalar=1.0, in1=src_ap, op0=ALU.mult,
                                               op1=ALU.mult, accum_out=out_ap), [src_b], [JUNK.b, out_b])

    def powm(out_ap, out_b, in_ap, in_b, scale, bias, power, tmp_ap):
        P.act(lambda e: e.activation(out=tmp_ap, in_=in_ap, func=AF.Ln, bias=bias, scale=scale), [in_b], [out_b])
        P.act(lambda e: e.activation(out=out_ap, in_=tmp_ap, func=AF.Exp, scale=power), [out_b], [out_b])

    def rms_prep(xt):
        sumsq(xt.t[:], xt.b, D, SS.t[:], SS.b)
        powm(RS.t[:], RS.b, SS.t[:], SS.b, 1.0 / D, 1e-6, -0.5, RS.t[:])

    def front(xt):
        rms_prep(xt)
        P.dve(lambda e: e.scalar_tensor_tensor(out=XN.t[:], in0=xt.t[:], scalar=RS.t[:, 0:1], in1=G_mix.t[:],
                                               op0=ALU.mult, op1=ALU.mult), [xt.b, RS.b, G_mix.b, WB], [XN.b])
        transposes(XN, XNT, NCH)

    def rope(src_ap, src_b, nh, cst, outs):
        n = nh * 32
        cs = cst.t[:, 0:32].unsqueeze(1).to_broadcast([128, nh, 32])
        sa = cst.t[:, 32:48].unsqueeze(1).to_broadcast([128, nh, 16])
        sb_ = cst.t[:, 48:64].unsqueeze(1).to_broadcast([128, nh, 16])
        a = RT1.t[:, 0:n].rearrange("p (h r) -> p h r", r=32)
        b = RT2.t[:, 0:n].rearrange("p (h r) -> p h r", r=32)
        P.dve(lambda e: e.tensor_tensor(out=a, in0=src_ap, in1=cs, op=ALU.mult), [src_b, cst.b], [RT1.b])
        P.dve(lambda e: e.tensor_tensor(out=b[:, :, 0:16], in0=src_ap[:, :, 16:32], in1=sa, op=ALU.mult), [src_b, cst.b], [RT2.b])
        P.dve(lambda e: e.tensor_tensor(out=b[:, :, 16:32], in0=src_ap[:, :, 0:16], in1=sb_, op=ALU.mult), [src_b, cst.b], [RT2.b])
        for oap, ob in outs:
            P.dve(lambda e, oap=oap: e.tensor_tensor(out=oap, in0=a, in1=b, op=ALU.add), [RT1.b, RT2.b], [ob])

    def kv_part(cst, ckvt_ap, krt_ap, ckvn_ap, kvbuf, track_rmax, pg=0):
        P.act(lambda e: e.activation(out=CKF.t[:], in_=pz[0].t[:, 0:128], func=AF.Copy), [pz[0].b], [CKF.b])
        sumsq(CKF.t[:], CKF.b, 128, SS8.t[:, 0:1], SS8.b)
        powm(RS8.t[:, 0:1], RS8.b, SS8.t[:, 0:1], SS8.b, 1.0 / 128, 1e-6, -0.5, RS8.t[:, 0:1])
        P.dve(lambda e: e.scalar_tensor_tensor(out=CKF.t[:], in0=CKF.t[:], scalar=RS8.t[:, 0:1], in1=G_kv.t[:],
                                               op0=ALU.mult, op1=ALU.mult), [CKF.b, RS8.b, G_kv.b, WB], [CKF.b])
        P.act(lambda e: e.activation(out=CKB.t[:], in_=CKF.t[:], func=AF.Copy), [CKF.b], [CKB.b])
        P.act(lambda e: e.activation(out=ckvn_ap, in_=CKF.t[:], func=AF.Copy), [CKF.b], [kvbuf])
        rope(pz[0].t[:, 128:160].rearrange("p (h r) -> p h r", r=32), pz[0].b, 1, cst,
             [(KRF.t[:].rearrange("p (h r) -> p h r", r=32), KRF.b)] +
             [(KRB.t[:, 32 * q:32 * q + 32].rearrange("p (h r) -> p h r", r=32), KRB.b) for q in range(3)])
        if track_rmax:
            sumsq(KRF.t[:], KRF.b, 32, NR2.t[:], NR2.b)
            P.dve(lambda e: e.tensor_tensor(out=RMAX2.t[:], in0=RMAX2.t[:], in1=NR2.t[:], op=ALU.max), [RMAX2.b, NR2.b], [RMAX2.b])
        P.pe(lambda e: e.transpose(out=ptp.t[:, 0:128], in_=CKB.t[:], identity=ident), [CKB.b, IDB.b], [ptp.b])
        P.pe(lambda e: e.transpose(out=ptp.t[0:96, 128:256], in_=KRB.t[:], identity=ident), [KRB.b, IDB.b], [ptp.b])
        P.act(lambda e: e.activation(out=ckvt_ap, in_=ptp.t[:, 0:128], func=AF.Copy), [ptp.b], [kvbuf])
        P.act(lambda e: e.activation(out=krt_ap, in_=ptp.t[32 * pg:32 * pg + 32, 128:256], func=AF.Copy), [ptp.b], [kvbuf])

    def gla_nat(triL_ap):
        linearT(ps[0].t[0:16, 0:128], ps[0].b, W_in, C_A, 16, XNT, NCH)
        P.act(lambda e: e.activation(out=AGT.t[:], in_=ps[0].t[0:16, 0:128], func=AF.Copy), [ps[0].b], [AGT.b])
        P.pe(lambda e: e.matmul(ps[1].t[:, 0:256], lhsT=AGT.t[:], rhs=W_a2.t[:], start=True, stop=True),
             [AGT.b, W_a2.b, WB], [ps[1].b])
        P.dve(lambda e: e.tensor_tensor(out=T1.t[:], in0=ps[1].t[:, 0:256], in1=B_a.t[:], op=ALU.add), [ps[1].b, B_a.b, WB], [T1.b])
        P.act(lambda e: e.activation(out=T1.t[:], in_=T1.t[:], func=AF.Exp, scale=-1.0), [T1.b], [T1.b])
        P.act(lambda e: e.activation(out=LSB.t[:], in_=T1.t[:], func=AF.Ln, bias=1.0), [T1.b], [LSB.b])
        P.pe(lambda e: e.matmul(pa[0].t[:, 0:256], lhsT=triL_ap, rhs=LSB.t[:], start=True, stop=True), [GC.b, LSB.b, WB], [pa[0].b])
        P.act(lambda e: e.activation(out=ERB.t[:], in_=pa[0].t[:, 0:256], func=AF.Exp), [pa[0].b], [ERB.b])
        P.dve(lambda e: e.tensor_tensor(out=KD.t[:], in0=pz[0].t[:, C_KG:C_KG + 256], in1=ERB.t[:], op=ALU.mult), [pz[0].b, ERB.b], [KD.b])
        P.act(lambda e: e.activation(out=VB.t[:], in_=pz[1].t[:], func=AF.Copy), [pz[1].b], [VB.b])

    def load_x(q, xt, src, sb=None):
        P.dma(q, lambda e: e.dma_start(out=xt.t[:], in_=src), [], [xt.b], xt.b)

    def store(src_t, src_ap, dst):
        P.dma("pool", lambda e: e.dma_start(out=dst, in_=src_ap), [src_t.b], [], src_t.b, is_out=True)

    TRIU_P, TRIL_P, MASK_P = GC.t[:, 0:128], GC.t[:, 128:256], GC.t[:, 256:384]
    TRIU_S, TRIL_S, MASK_S = GC.t[:, 384:512], GC.t[:, 512:640], GC.t[:, 640:768]

    def shared(blk):
        xt = XS[blk % 2]; cst = CSS[blk % 2]
        load_x("sp", xt, x_all[blk * 128:(blk + 1) * 128, :])
        load_x("sp", cst, cs_all[blk * 128:(blk + 1) * 128, :])
        front(xt)
        linear(pz[0], XNT, NCH, W_in, 0, 416)
        linear(pz[1], XNT, NCH, W_in, C_VG, 512)
        g_ = blk // 4; pg = g_ % 3; kc = (g_ // 3) * 512 + (blk % 4) * 128
        kv_part(cst, CKVT.t[:, blk * 128:(blk + 1) * 128], KRT.t[32 * pg:32 * pg + 32, kc:kc + 128], CKVN.t[:, blk, :], kvb[blk], True, pg)
        store(CKF, CKF.t[:], ckv_o[blk * 128:(blk + 1) * 128, :])
        store(KRF, KRF.t[:], kr_o[blk * 128:(blk + 1) * 128, :])
        gla_nat(TRIL_P)
        for h in range(4):
            P.pe(lambda e, h=h: e.matmul(pa[0].t[0:64, 256 + h:257 + h], lhsT=LSB.t[:, h * 64:(h + 1) * 64], rhs=TRIU_P[:, 127:128],
                                         start=True, stop=True), [LSB.b, GC.b, WB], [pa[0].b])
        P.act(lambda e: e.activation(out=EBL.t[0:64, 0:4], in_=pa[0].t[0:64, 256:260], func=AF.Exp), [pa[0].b], [EBL.b])
        sin, sout = ST[blk % 3], ST[(blk + 1) % 3]
        for h in range(4):
            P.pe(lambda e, h=h: e.matmul(pa[1].t[0:64, h * 128:(h + 1) * 128], lhsT=KD.t[:, h * 64:(h + 1) * 64],
                                         rhs=VB.t[:, h * 128:(h + 1) * 128], start=True, stop=True), [KD.b, VB.b], [pa[1].b])
        for h in range(4):
            P.dve(lambda e, h=h: e.scalar_tensor_tensor(out=sout.t[:, h, :], in0=sin.t[:, h, :], scalar=EBL.t[0:64, h:h + 1],
                                                        in1=pa[1].t[0:64, h * 128:(h + 1) * 128], op0=ALU.mult, op1=ALU.add),
                  [sin.b, EBL.b, pa[1].b], [sout.b])

    def silu_to(dst_ap, dst_b, gate_ps, o_ap, o_bufs, n):
        P.act(lambda e: e.activation(out=SIG.t[:, 0:n], in_=gate_ps.t[:, 0:n], func=AF.Exp, scale=-1.0), [gate_ps.b], [SIG.b])
        P.dve(lambda e: e.tensor_scalar(out=SIG.t[:, 0:n], in0=SIG.t[:, 0:n], scalar1=1.0, scalar2=None, op0=ALU.add), [SIG.b], [SIG.b])
        P.dve(lambda e: e.reciprocal(out=SIG.t[:, 0:n], in_=SIG.t[:, 0:n]), [SIG.b], [SIG.b])
        P.dve(lambda e: e.tensor_tensor(out=SIG.t[:, 0:n], in0=gate_ps.t[:, 0:n], in1=SIG.t[:, 0:n], op=ALU.mult), [gate_ps.b, SIG.b], [SIG.b])
        P.dve(lambda e: e.tensor_tensor(out=dst_ap, in0=o_ap, in1=SIG.t[:, 0:n], op=ALU.mult), o_bufs + [SIG.b], [dst_b])

    def own(i, smp):
        if smp:
            xsrc, psrc, cssrc, ydst = x_smp, p_smp, cs_smp, y_smp
        else:
            sl = slice(i * 128, (i + 1) * 128)
            xsrc, psrc, cssrc, ydst = x_own[sl, :], p_own[sl, :], cs_own[sl, :], y_own[sl, :]
        load_x("sp", XO, xsrc); load_x("sp", PIN, psrc); load_x("sp", CSO, cssrc)
        front(XO)
        triU, triL, maskT = (TRIU_S, TRIL_S, MASK_S) if smp else (TRIU_P, TRIL_P, MASK_P)
        linear(pz[0], XNT, NCH, W_in, 0, 416)
        linear(pz[1], XNT, NCH, W_in, C_VG, 512)
        if smp:
            kv_part(CSO, Z.CKVT_S.t[:], Z.KRT_S.t[:], Z.CKVN_S.t[:], Z.CKVN_S.b, False)
            store(CKF, CKF.t[:], ckvs_o); store(KRF, KRF.t[:], krs_o)
        gla_nat(triL)
        for h in range(4):
            P.pe(lambda e, h=h: e.matmul(pa[1].t[0:64, h * 128:(h + 1) * 128], lhsT=LSB.t[:, h * 64:(h + 1) * 64], rhs=triU,
                                         start=True, stop=True), [LSB.b, GC.b, WB], [pa[1].b])
        P.act(lambda e: e.activation(out=EBT.t[:], in_=pa[1].t[0:64, :], func=AF.Exp), [pa[1].b], [EBT.b])
        P.act(lambda e: e.activation(out=ENBT.t[:], in_=pa[1].t[0:64, :], func=AF.Exp, scale=-1.0), [pa[1].b], [ENBT.b])
        for h in range(4):
            linearT(ps[0].t[0:64, h * 128:(h + 1) * 128], ps[0].b, W_in, C_QG + h * 64, 64, XNT, NCH)
        for h in range(4):
            linearT(ps[1].t[0:64, h * 128:(h + 1) * 128], ps[1].b, W_in, C_KG + h * 64, 64, XNT, NCH)
        P.dve(lambda e: e.scalar_tensor_tensor(out=QET.t[:], in0=ps[0].t[0:64, :], scalar=0.125, in1=EBT.t[:], op0=ALU.mult,
                                               op1=ALU.mult), [ps[0].b, EBT.b], [QET.b])
        P.dve(lambda e: e.tensor_tensor(out=KET.t[:], in0=ps[1].t[0:64, :], in1=ENBT.t[:], op=ALU.mult), [ps[1].b, ENBT.b], [KET.b])
        for h in range(4):
            P.pe(lambda e, h=h: e.matmul(pa[0].t[:, h * 128:(h + 1) * 128], lhsT=KET.t[:, h * 128:(h + 1) * 128],
                                         rhs=QET.t[:, h * 128:(h + 1) * 128], start=True, stop=True), [KET.b, QET.b], [pa[0].b])
        P.dve(lambda e: e.tensor_tensor(out=ATM.t[:].rearrange("p (h t) -> p h t", h=4), in0=pa[0].t[:].rearrange("p (h t) -> p h t", h=4),
                                        in1=maskT.unsqueeze(1).to_broadcast([128, 4, 128]), op=ALU.mult), [pa[0].b, GC.b], [ATM.b])
        if not smp:
            sa, sb_ = ST[(2 * i) % 3], ST[(2 * i + 1) % 3]
            sown = OG.t[0:64, :].rearrange("k (h v) -> k h v", h=4)
            P.dve(lambda e: e.tensor_scalar(out=sown, in0=sa.t[:], scalar1=PAR.t[0:64, 0:1], scalar2=None, op0=ALU.mult),
                  [sa.b, PAR.b, WB], [OG.b])
            P.dve(lambda e: e.scalar_tensor_tensor(out=SOWNB.t[:], in0=sb_.t[:], scalar=PAR.t[0:64, 1:2], in1=sown,
                                                   op0=ALU.mult, op1=ALU.add), [sb_.b, PAR.b, OG.b], [SOWNB.b])
            for h in range(4):
                P.pe(lambda e, h=h: e.matmul(ps[0].t[:, h * 128:(h + 1) * 128], lhsT=ATM.t[:, h * 128:(h + 1) * 128],
                                             rhs=VB.t[:, h * 128:(h + 1) * 128], start=True, stop=False), [ATM.b, VB.b], [ps[0].b])
                P.pe(lambda e, h=h: e.matmul(ps[0].t[:, h * 128:(h + 1) * 128], lhsT=QET.t[:, h * 128:(h + 1) * 128],
                                             rhs=SOWNB.t[:, h, :], start=False, stop=True), [QET.b, SOWNB.b], [ps[0].b])
        else:
            parm = SX.t[:, 0:2]; msel = SX.t[:, 2:10]; kdm = SX.t[:, 10:26]
            for h in range(4):
                s0f = Z.S0F[h % 2]
                for il in range(2):
                    P.dma("sp", lambda e, h=h, il=il, s0f=s0f: e.dma_start(
                        out=s0f.t[il * 64:(il + 1) * 64, :, :],
                        in_=st_in[:, h, :, :].rearrange("(p il) k v -> il k p v", il=2)[il]), [], [s0f.b], s0f.b)
                P.act(lambda e, s0f=s0f: e.activation(out=Z.S0B.t[:].rearrange("p a v -> p (a v)"), in_=s0f.t[:].rearrange("p a v -> p (a v)"),
                                                      func=AF.Copy), [s0f.b], [Z.S0B.b])
                P.pe(lambda e, h=h: e.transpose(out=ppt.t[:, 0:64], in_=QET.t[:, h * 128:(h + 1) * 128], identity=IDB.t[0:64, 0:64]),
                     [QET.b, IDB.b], [ppt.b])
                P.dve(lambda e: e.tensor_copy(out=Z.QEN.t[:], in_=ppt.t[:, 0:64].unsqueeze(1).to_broadcast([128, 2, 64])), [ppt.b], [Z.QEN.b])
                P.pe(lambda e: e.transpose(out=ppt.t[:, 128:256], in_=Z.QEN.t[:].rearrange("p a k -> p (a k)"), identity=ident),
                     [Z.QEN.b, IDB.b], [ppt.b])
                P.dve(lambda e: e.tensor_tensor(out=Z.QEX.t[:], in0=ppt.t[:, 128:256].unsqueeze(1).to_broadcast([128, 8, 128]),
                                                in1=Z.M2.t[:].rearrange("p (a t) -> p a t", a=8), op=ALU.mult), [ppt.b, Z.M2.b], [Z.QEX.b])
                P.pe(lambda e, h=h: e.matmul(ps[0].t[:, h * 128:(h + 1) * 128], lhsT=ATM.t[:, h * 128:(h + 1) * 128],
                                             rhs=VB.t[:, h * 128:(h + 1) * 128], start=True, stop=False), [ATM.b, VB.b], [ps[0].b])
                for p in range(8):
                    P.pe(lambda e, h=h, p=p: e.matmul(ps[0].t[:, h * 128:(h + 1) * 128], lhsT=Z.QEX.t[:, p, :], rhs=Z.S0B.t[:, p, :],
                                                      start=False, stop=(p == 7)), [Z.QEX.b, Z.S0B.b], [ps[0].b])
                P.dve(lambda e, h=h: e.tensor_tensor(out=Z.L2.t[:], in0=LSB.t[:, h * 64:(h + 1) * 64].unsqueeze(1).to_broadcast([128, 2, 64]),
                                                     in1=parm.unsqueeze(2).to_broadcast([128, 2, 64]), op=ALU.mult), [LSB.b, SX.b], [Z.L2.b])
                P.pe(lambda e: e.matmul(pa[1].t[:, 0:8], lhsT=Z.L2.t[:].rearrange("p a k -> p (a k)"), rhs=msel, start=True, stop=True),
                     [Z.L2.b, SX.b], [pa[1].b])
                P.act(lambda e: e.activation(out=Z.EBS.t[:], in_=pa[1].t[:, 0:8], func=AF.Exp), [pa[1].b], [Z.EBS.b])
                P.dve(lambda e, h=h: e.tensor_tensor(out=Z.KDX.t[:], in0=KD.t[:, h * 64:(h + 1) * 64].unsqueeze(1).to_broadcast([128, 16, 64]),
                                                     in1=kdm.unsqueeze(2).to_broadcast([128, 16, 64]), op=ALU.mult), [KD.b, SX.b], [Z.KDX.b])
                for half in range(2):
                    for p4 in range(4):
                        p = half * 4 + p4
                        P.pe(lambda e, h=h, p=p, p4=p4: e.matmul(pa[0].t[:, p4 * 128:(p4 + 1) * 128],
                                                                 lhsT=Z.KDX.t[:, 2 * p:2 * p + 2, :].rearrange("t a k -> t (a k)"),
                                                                 rhs=VB.t[:, h * 128:(h + 1) * 128], start=True, stop=True),
                             [Z.KDX.b, VB.b], [pa[0].b])
                    for p4 in range(4):
                        p = half * 4 + p4
                        P.dve(lambda e, p=p, p4=p4, s0f=s0f: e.scalar_tensor_tensor(
                            out=Z.SNEW.t[:, p, :], in0=s0f.t[:, p, :], scalar=Z.EBS.t[:, p:p + 1], in1=pa[0].t[:, p4 * 128:(p4 + 1) * 128],
                            op0=ALU.mult, op1=ALU.add), [s0f.b, Z.EBS.b, pa[0].b], [Z.SNEW.b])
                for il in range(2):
                    P.dma("pool", lambda e, h=h, il=il: e.dma_start(
                        out=sts_o[:, h, :, :].rearrange("(p il) k v -> il k p v", il=2)[il],
                        in_=Z.SNEW.t[il * 64:(il + 1) * 64, :, :]), [Z.SNEW.b], [], Z.SNEW.b, is_out=True)
        P.act(lambda e: e.activation(out=OG.t[:], in_=ps[0].t[:], func=AF.Copy), [ps[0].b], [OG.b])
        for h in range(4):
            sumsq(OG.t[:, h * 128:(h + 1) * 128], OG.b, 128, SS8.t[:, h:h + 1], SS8.b)
        powm(RS8.t[:, 0:4], RS8.b, SS8.t[:, 0:4], SS8.b, 1.0 / 128, 1e-6, -0.5, RS8.t[:, 0:4])
        for h in range(4):
            P.dve(lambda e, h=h: e.scalar_tensor_tensor(out=OG.t[:, h * 128:(h + 1) * 128], in0=OG.t[:, h * 128:(h + 1) * 128],
                                                        scalar=RS8.t[:, h:h + 1], in1=G_on.t[:], op0=ALU.mult, op1=ALU.mult),
                  [OG.b, RS8.b, G_on.b, WB], [OG.b])
        linear(pz[0], XNT, NCH, W_in, C_GG, 512)
        silu_to(CAT.t[:, 512:1024], CAT.b, pz[0], OG.t[:], [OG.b], 512)

        linear(pz[1], XNT, NCH, W_in, C_QC, 256)
        P.act(lambda e: e.activation(out=RT1.t[:], in_=pz[1].t[:, 0:256], func=AF.Copy), [pz[1].b], [RT1.b])
        sumsq(RT1.t[:], RT1.b, 256, SS.t[:], SS.b)
        powm(RS.t[:], RS.b, SS.t[:], SS.b, 1.0 / 256, 1e-6, -0.5, RS.t[:])
        P.dve(lambda e: e.scalar_tensor_tensor(out=QCN.t[:], in0=RT1.t[:], scalar=RS.t[:, 0:1], in1=G_q.t[:],
                                               op0=ALU.mult, op1=ALU.mult), [RT1.b, RS.b, G_q.b, WB], [QCN.b])
        transposes(QCN, QCNT, 2)
        for h in range(8):
            linearT(ps[h // 4].t[0:64, (h % 4) * 128:(h % 4 + 1) * 128], ps[h // 4].b, W_qup, h * 64, 64, QCNT, 2)
        for k in range(2):
            P.act(lambda e, k=k: e.activation(out=QNT.t[:, k * 512:(k + 1) * 512], in_=ps[k].t[0:64, :], func=AF.Copy), [ps[k].b], [QNT.b])
        for h in range(8):
            P.pe(lambda e, h=h: e.matmul(pz[h // 4].t[:, (h % 4) * 128:(h % 4 + 1) * 128], lhsT=W_ukT.t[:, h * 128:(h + 1) * 128],
                                         rhs=QNT.t[:, h * 128:(h + 1) * 128], start=True, stop=True), [W_ukT.b, WB, QNT.b], [pz[h // 4].b])
        for k in range(2):
            if smp:
                o = QLT.t[:].rearrange("l (s h t) -> l h s t", s=16, h=8, t=8)[:, 4 * k:4 * k + 4, :, :]
                i_ = pz[k].t[:].rearrange("l (h s t) -> l h s t", h=4, s=16, t=8)
            else:
                o = QLT.t[:, k * 512:(k + 1) * 512]; i_ = pz[k].t[:]
            P.act(lambda e, o=o, i_=i_: e.activation(out=o, in_=i_, func=AF.Copy), [pz[k].b], [QLT.b])
        linear(ps[0], QCNT, 2, W_qup, 512, 256)
        rope(ps[0].t[:, 0:256].rearrange("p (h r) -> p h r", r=32), ps[0].b, 8, CSO,
             [(QRB.t[:].rearrange("p (h q r) -> p h q r", q=3, r=32)[:, :, q, :], QRB.b) for q in range(3)])
        for h in range(8):
            P.pe(lambda e, h=h: e.transpose(out=ptp.t[0:96, h * 128:(h + 1) * 128], in_=QRB.t[:, h * 96:(h + 1) * 96], identity=ident),
                 [QRB.b, IDB.b], [ptp.b])
        if smp:
            o = QRT.t[:].rearrange("l (s h t) -> l h s t", s=16, h=8, t=8)
            i_ = ptp.t[0:96, :].rearrange("l (h s t) -> l h s t", h=8, s=16, t=8)
        else:
            o = QRT.t[:]; i_ = ptp.t[0:96, :]
        P.act(lambda e: e.activation(out=o, in_=i_, func=AF.Copy), [ptp.b], [QRT.b])

        if not smp:
            P.dve(lambda e: e.tensor_tensor(out=QSQ.t[:], in0=QLT.t[:], in1=QLT.t[:], op=ALU.mult), [QLT.b], [QSQ.b])
            for h in range(8):
                P.pe(lambda e, h=h: e.matmul(pa[0].t[:, h:h + 1], lhsT=QSQ.t[:, h * 128:(h + 1) * 128], rhs=ONESB.t[:],
                                             start=True, stop=True), [QSQ.b, ONESB.b], [pa[0].b])
            qr0 = QRB.t[:].rearrange("p (h q r) -> p h q r", q=3, r=32)[:, :, 0, :]
            P.dve(lambda e: e.tensor_tensor(out=RT1.t[:].rearrange("p (h r) -> p h r", r=32), in0=qr0, in1=qr0, op=ALU.mult), [QRB.b], [RT1.b])
            P.dve(lambda e: e.tensor_reduce(out=NRQ.t[:], in_=RT1.t[:].rearrange("p (h r) -> p h r", r=32), axis=AX.X, op=ALU.add),
                  [RT1.b], [NRQ.b])
            P.pe(lambda e: e.transpose(out=pa[1].t[0:1, 0:128], in_=RMAX2.t[:], identity=IDF.t[:]), [RMAX2.b, IDF.b, WB], [pa[1].b])
            P.dve(lambda e: e.reduce_max(out=R1.t[:], in_=pa[1].t[0:1, 0:128], axis=AX.X), [pa[1].b], [R1.b])
            P.pe(lambda e: e.matmul(pa[1].t[:, 128:129], lhsT=ONESF.t[:], rhs=R1.t[:], start=True, stop=True), [ONESF.b, R1.b], [pa[1].b])
            powm(RM.t[:], RM.b, pa[1].t[:, 128:129], pa[1].b, 1.0, 1e-20, 0.5, RM.t[:])
            powm(NL2.t[:], NL2.b, pa[0].t[:, 0:8], pa[0].b, 1.0, 1e-20, 0.5, NL2.t[:])
            powm(NRQ.t[:], NRQ.b, NRQ.t[:], NRQ.b, 1.0, 1e-20, 0.5, NRQ.t[:])
            P.dve(lambda e: e.tensor_scalar(out=NL2.t[:], in0=NL2.t[:], scalar1=CMAX.t[:, 0:1], scalar2=None, op0=ALU.mult),
                  [NL2.b, CMAX.b], [NL2.b])
            P.dve(lambda e: e.scalar_tensor_tensor(out=NEGM.t[:], in0=NRQ.t[:], scalar=RM.t[:, 0:1], in1=NL2.t[:], op0=ALU.mult,
                                                   op1=ALU.add), [NRQ.b, RM.b, NL2.b], [NEGM.b])
            P.dve(lambda e: e.tensor_scalar(out=NEGM.t[:], in0=NEGM.t[:], scalar1=-SC, scalar2=None, op0=ALU.mult), [NEGM.b], [NEGM.b])
            nkb = 2 * i + 2
            ng = (nkb + 3) // 4
            steps = [(h, g) for h in range(8) for g in range(ng)]
            nst = len(steps)

            def geo(t):
                h, g = steps[t]
                nb = min(4, nkb - 4 * g)
                return h, g, nb, nb * 128

            def st_S(t):
                h, g, nb, Wd = geo(t)
                psx = ps[t % 2]; c0 = 4 * g * 128
                kbufs = [kvb[4 * g + j] for j in range(nb)]
                P.pe(lambda e: e.matmul(psx.t[:, 0:Wd], lhsT=QLT.t[:, h * 128:(h + 1) * 128], rhs=CKVT.t[:, c0:c0 + Wd],
                                        start=True, stop=False), [QLT.b] + kbufs, [psx.b])
                pg = g % 3; kc = (g // 3) * 512
                P.pe(lambda e: e.matmul(psx.t[:, 0:Wd], lhsT=QRT.t[32 * pg:32 * pg + 32, h * 128:(h + 1) * 128],
                                        rhs=KRT.t[32 * pg:32 * pg + 32, kc:kc + Wd], start=False, stop=True), [QRT.b] + kbufs, [psx.b])

            def st_X(t):
                h, g, nb, Wd = geo(t)
                psx = ps[t % 2]; pb = PB[t % 2]
                if g == ng - 1:
                    P.dve(lambda e: e.tensor_tensor(out=SM.t[:, 0:Wd], in0=psx.t[:, 0:Wd], in1=MK4.t[:, 512 - Wd:512], op=ALU.add),
                          [psx.b, MK4.b], [SM.b])
                    src_ap, src_b = SM.t[:, 0:Wd], SM.b
                else:
                    src_ap, src_b = psx.t[:, 0:Wd], psx.b
                P.act(lambda e: e.activation(out=pb.t[:, 0:Wd], in_=src_ap, func=AF.Exp, bias=NEGM.t[:, h:h + 1], scale=SC,
                                             accum_out=LSUM.t[:, h, g:g + 1]), [src_b, NEGM.b], [pb.b, LSUM.b])

            def st_T(t):
                h, g, nb, Wd = geo(t)
                pb = PB[t % 2]; pt = PT[t % 2]; pph = ppth[t % 2]
                for j in range(nb):
                    P.pe(lambda e, j=j: e.transpose(out=pph.t[:, j * 128:(j + 1) * 128], in_=pb.t[:, j * 128:(j + 1) * 128], identity=ident),
                         [pb.b, IDB.b], [pph.b])
                P.dve(lambda e: e.tensor_copy(out=pt.t[:, 0:Wd], in_=pph.t[:, 0:Wd]), [pph.b], [pt.b])

            def st_PV(t):
                h, g, nb, Wd = geo(t)
                pt = PT[t % 2]; bank = pa[h // 4]
                for j in range(nb):
                    first = (h % 4 == 0 and g == 0 and j == 0)
                    lastm = (h % 4 == 3 and g == ng - 1 and j == nb - 1)
                    P.pe(lambda e, j=j, first=first, lastm=lastm: e.matmul(
                        bank.t[:, (h % 4) * 128:(h % 4 + 1) * 128], lhsT=pt.t[:, j * 128:(j + 1) * 128], rhs=CKVN.t[:, 4 * g + j, :],
                        start=first, stop=lastm, skip_group_check=True), [pt.b, kvb[4 * g + j]], [bank.b])

            st_S(0)
            for t in range(nst):
                if t + 1 < nst:
                    st_S(t + 1)
                st_X(t)
                st_T(t)
                if t >= 1:
                    st_PV(t - 1)
            st_PV(nst - 1)
            P.dve(lambda e: e.tensor_reduce(out=LR.t[:], in_=LSUM.t[:, :, 0:ng], axis=AX.X, op=ALU.add), [LSUM.b], [LR.b])
            P.dve(lambda e: e.reciprocal(out=LR.t[:], in_=LR.t[:]), [LR.b], [LR.b])
            for h in range(8):
                P.dve(lambda e, h=h: e.tensor_scalar(out=OLAT.t[:, h, :], in0=pa[h // 4].t[:, (h % 4) * 128:(h % 4 + 1) * 128],
                                                     scalar1=LR.t[:, h:h + 1], scalar2=None, op0=ALU.mult), [pa[h // 4].b, LR.b], [OLAT.b])
            for h in range(8):
                P.pe(lambda e, h=h: e.transpose(out=ptp.t[:, h * 128:(h + 1) * 128], in_=OLAT.t[:, h, :], identity=ident),
                     [OLAT.b, IDB.b], [ptp.b])
            P.act(lambda e: e.activation(out=OLT.t[:], in_=ptp.t[:], func=AF.Copy), [ptp.b], [OLT.b])
        else:
            decode_attention()

        for h in range(8):
            P.pe(lambda e, h=h: e.matmul(pz[0].t[:, h * 64:(h + 1) * 64], lhsT=OLT.t[:, h * 128:(h + 1) * 128], rhs=W_uv.t[:, h * 64:(h + 1) * 64],
                                         start=True, stop=True), [OLT.b, W_uv.b, WB], [pz[0].b])
        P.act(lambda e: e.activation(out=OG.t[:], in_=pz[0].t[:], func=AF.Copy), [pz[0].b], [OG.b])
        linear(pz[1], XNT, NCH, W_in, C_GM, 512)
        silu_to(CAT.t[:, 0:512], CAT.b, pz[1], OG.t[:], [OG.b], 512)
        transposes(CAT, CATT, NCH)
        for k in range(2):
            linear(pz[k], CATT, NCH, W_out, k * 512, 512)
            P.dve(lambda e, k=k: e.tensor_tensor(out=H1.t[:, k * 512:(k + 1) * 512], in0=pz[k].t[:], in1=XO.t[:, k * 512:(k + 1) * 512],
                                                 op=ALU.add), [pz[k].b, XO.b], [H1.b])
        P.act(lambda e: e.activation(out=H1B.t[:], in_=H1.t[:], func=AF.Copy), [H1.b], [H1B.b])
        transposes(H1B, H1T, NCH)
        P.act(lambda e: e.activation(out=PBF.t[:], in_=PIN.t[:], func=AF.Copy), [PIN.b], [PBF.b])
        transposes(PBF, PTT, 2)
        for k in range(2):
            linear(pz[k], H1T, NCH, W_pg, k * 512, 512)
            sg = SIG.t[:, k * 512:(k + 1) * 512]
            P.act(lambda e, k=k, sg=sg: e.activation(out=sg, in_=pz[k].t[:], func=AF.Exp, scale=-1.0), [pz[k].b], [SIG.b])
            P.dve(lambda e, sg=sg: e.tensor_scalar(out=sg, in0=sg, scalar1=1.0, scalar2=None, op0=ALU.add), [SIG.b], [SIG.b])
            P.dve(lambda e, sg=sg: e.reciprocal(out=sg, in_=sg), [SIG.b], [SIG.b])
            linear(ps[k], PTT, 2, W_pp, k * 512, 512)
            P.dve(lambda e, k=k, sg=sg: e.tensor_tensor(out=sg, in0=ps[k].t[:], in1=sg, op=ALU.mult), [ps[k].b, SIG.b], [SIG.b])
            P.dve(lambda e, k=k, sg=sg: e.tensor_tensor(out=H2.t[:, k * 512:(k + 1) * 512], in0=H1.t[:, k * 512:(k + 1) * 512], in1=sg,
                                                        op=ALU.add), [H1.b, SIG.b], [H2.b])
        rms_prep(H2)
        P.dve(lambda e: e.scalar_tensor_tensor(out=YO.t[:], in0=H2.t[:], scalar=RS.t[:, 0:1], in1=G_fin.t[:], op0=ALU.mult,
                                               op1=ALU.mult), [H2.b, RS.b, G_fin.b, WB], [YO.b])
        store(YO, YO.t[:], ydst)

    def decode_attention():
        def block_group(s, cT_ap, cT_b, kT_ap, kT_b, vsrc, nb, mask_ap):
            Wd = nb * 128
            psx = ps[0]
            P.pe(lambda e: e.matmul(psx.t[0:64, 0:Wd], lhsT=QLT.t[:, s * 64:(s + 1) * 64], rhs=cT_ap, start=True, stop=False),
                 [QLT.b, cT_b], [psx.b])
            P.pe(lambda e: e.matmul(psx.t[0:64, 0:Wd], lhsT=QRT.t[0:32, s * 64:(s + 1) * 64], rhs=kT_ap, start=False, stop=True),
                 [QRT.b, kT_b], [psx.b])
            if mask_ap is not None:
                P.dve(lambda e: e.tensor_tensor(out=SM.t[0:64, 0:Wd], in0=psx.t[0:64, 0:Wd], in1=mask_ap, op=ALU.add), [psx.b, Z.DMASK.b, WB], [SM.b])
                src_ap, src_b = SM.t[0:64, 0:Wd], SM.b
            else:
                src_ap, src_b = psx.t[0:64, 0:Wd], psx.b
            P.dve(lambda e: e.reduce_max(out=Z.GMX.t[:], in_=src_ap, axis=AX.X), [src_b], [Z.GMX.b])
            P.dve(lambda e: e.tensor_tensor(out=Z.MNEW.t[:], in0=Z.MRUN.t[:], in1=Z.GMX.t[:], op=ALU.max), [Z.MRUN.b, Z.GMX.b], [Z.MNEW.b])
            P.dve(lambda e: e.tensor_tensor(out=Z.ALP.t[:], in0=Z.MRUN.t[:], in1=Z.MNEW.t[:], op=ALU.subtract), [Z.MRUN.b, Z.MNEW.b], [Z.ALP.b])
            P.act(lambda e: e.activation(out=Z.ALP.t[:], in_=Z.ALP.t[:], func=AF.Exp, scale=SC), [Z.ALP.b], [Z.ALP.b])
            P.dve(lambda e: e.tensor_scalar(out=Z.NMN.t[:], in0=Z.MNEW.t[:], scalar1=-SC, scalar2=None, op0=ALU.mult), [Z.MNEW.b], [Z.NMN.b])
            P.dve(lambda e: e.tensor_copy(out=Z.MRUN.t[:], in_=Z.MNEW.t[:]), [Z.MNEW.b, Z.ALP.b], [Z.MRUN.b])
            pb = PB[0]
            P.act(lambda e: e.activation(out=pb.t[0:64, 0:Wd], in_=src_ap, func=AF.Exp, bias=Z.NMN.t[:, 0:1], scale=SC, accum_out=Z.LG.t[:, 0:1]),
                  [src_b, Z.NMN.b], [pb.b, Z.LG.b])
            P.dve(lambda e: e.scalar_tensor_tensor(out=Z.LRUN.t[:], in0=Z.LRUN.t[:], scalar=Z.ALP.t[:, 0:1], in1=Z.LG.t[:], op0=ALU.mult,
                                                   op1=ALU.add), [Z.LRUN.b, Z.ALP.b, Z.LG.b], [Z.LRUN.b])
            for j in range(nb):
                P.pe(lambda e, j=j: e.transpose(out=ppt.t[:, j * 64:(j + 1) * 64], in_=pb.t[0:64, j * 128:(j + 1) * 128],
                                                identity=IDB.t[0:64, 0:64]), [pb.b, IDB.b], [ppt.b])
            pt = PT[0]
            P.dve(lambda e: e.tensor_copy(out=pt.t[:, 0:nb * 64], in_=ppt.t[:, 0:nb * 64]), [ppt.b], [pt.b])
            for j in range(nb):
                vap, vb = vsrc(j)
                P.pe(lambda e, j=j, vap=vap: e.matmul(pa[0].t[0:64, 0:128], lhsT=pt.t[:, j * 64:(j + 1) * 64], rhs=vap,
                                                      start=(j == 0), stop=(j == nb - 1)), [pt.b, vb], [pa[0].b])
            P.dve(lambda e: e.scalar_tensor_tensor(out=Z.ACC.t[:], in0=Z.ACC.t[:], scalar=Z.ALP.t[:, 0:1], in1=pa[0].t[0:64, 0:128],
                                                   op0=ALU.mult, op1=ALU.add), [Z.ACC.b, Z.ALP.b, pa[0].b], [Z.ACC.b])

        gcount = 0
        for s in range(16):
            P.dve(lambda e: e.memset(Z.MRUN.t[:], -1.0e30), [], [Z.MRUN.b])
            P.dve(lambda e: e.memset(Z.LRUN.t[:], 0.0), [], [Z.LRUN.b])
            P.dve(lambda e: e.memset(Z.ACC.t[:], 0.0), [], [Z.ACC.b])
            for g8 in range(NG8):
                k = gcount % 2; gcount += 1
                col = s * NG8 + g8
                P.dma("pool", lambda e, k=k, col=col: e.indirect_dma_start(
                    out=Z.PGC[k].t[:].rearrange("p a l -> p (a l)"), out_offset=None, in_=pool_c,
                    in_offset=bass.IndirectOffsetOnAxis(ap=Z.IDX.t[:, col:col + 1], axis=0)), [Z.IDX.b], [Z.PGC[k].b], Z.PGC[k].b)
                P.dma("pool", lambda e, k=k, col=col: e.indirect_dma_start(
                    out=Z.PGR[k].t[:].rearrange("p a l -> p (a l)"), out_offset=None, in_=pool_r,
                    in_offset=bass.IndirectOffsetOnAxis(ap=Z.IDX.t[:, col:col + 1], axis=0)), [Z.IDX.b], [Z.PGR[k].b], Z.PGR[k].b)
                P.act(lambda e, k=k: e.activation(out=Z.PGCB[k].t[:].rearrange("p a l -> p (a l)"), in_=Z.PGC[k].t[:].rearrange("p a l -> p (a l)"),
                                                  func=AF.Copy), [Z.PGC[k].b], [Z.PGCB[k].b])
                P.dve(lambda e, k=k: e.tensor_copy(out=Z.PGRB[k].t[:].rearrange("p a l -> p (a l)"), in_=Z.PGR[k].t[:].rearrange("p a l -> p (a l)")),
                      [Z.PGR[k].b], [Z.PGRB[k].b])
                for half in range(2):
                    for j in range(4):
                        a = half * 4 + j
                        P.pe(lambda e, k=k, a=a, j=j: e.transpose(out=ptp.t[:, j * 128:(j + 1) * 128], in_=Z.PGCB[k].t[:, a, :], identity=ident),
                             [Z.PGCB[k].b, IDB.b], [ptp.b])
                        P.pe(lambda e, k=k, a=a, j=j: e.transpose(out=ptp.t[0:32, 512 + j * 128:512 + (j + 1) * 128], in_=Z.PGRB[k].t[:, a, :],
                                                                  identity=ident), [Z.PGRB[k].b, IDB.b], [ptp.b])
                    P.act(lambda e: e.activation(out=Z.CTS[0].t[:, 0:512], in_=ptp.t[:, 0:512], func=AF.Copy), [ptp.b], [Z.CTS[0].b])
                    P.act(lambda e: e.activation(out=Z.KTS[0].t[:, 0:512], in_=ptp.t[0:32, 512:1024], func=AF.Copy), [ptp.b], [Z.KTS[0].b])
                    block_group(s, Z.CTS[0].t[:, 0:512], Z.CTS[0].b, Z.KTS[0].t[:, 0:512], Z.KTS[0].b,
                                lambda j, k=k, half=half: (Z.PGCB[k].t[:, half * 4 + j, :], Z.PGCB[k].b), 4, None)
            block_group(s, Z.CKVT_S.t[:], Z.CKVN_S.b, Z.KRT_S.t[:], Z.CKVN_S.b, lambda j: (Z.CKVN_S.t[:], Z.CKVN_S.b), 1,
                        Z.DMASK.t[:, s * 128:(s + 1) * 128])
            P.dve(lambda e: e.reciprocal(out=Z.LG.t[:], in_=Z.LRUN.t[:]), [Z.LRUN.b], [Z.LG.b])
            P.dve(lambda e: e.tensor_scalar(out=Z.ACCB.t[:], in0=Z.ACC.t[:], scalar1=Z.LG.t[:, 0:1], scalar2=None, op0=ALU.mult),
                  [Z.ACC.b, Z.LG.b], [Z.ACCB.b])
            P.pe(lambda e: e.transpose(out=ppt.t[:, 512:576], in_=Z.ACCB.t[:], identity=IDB.t[0:64, 0:64]), [Z.ACCB.b, IDB.b], [ppt.b])
            P.dve(lambda e, s=s: e.tensor_copy(out=OLT.t[:].rearrange("l (h s t) -> l h s t", h=8, s=16, t=8)[:, :, s, :],
                                               in_=ppt.t[:, 512:576].rearrange("l (h t) -> l h t", h=8)), [ppt.b], [OLT.b])

    P.barrier()
    for i in range(NS):
        shared(2 * i)
        shared(2 * i + 1)
        own(i, False)
    fs = ST[NKB % 3]
    P.dma("pool", lambda e: e.dma_start(out=stp_o.rearrange("h k v -> k h v"), in_=fs.t[:]), [fs.b], [], fs.b, is_out=True)
    P.barrier()
    es_p.close()
    alloc_sample()
    P.barrier()
    own(0, True)

    P.finalize()
    with nc.Block() as block:
        P.emit(block)
    es.close()
    return nc


def _consts():
    import ml_dtypes
    t = np.arange(128)
    c = {}
    su = (t[:, None] <= t[None, :])
    sq = (t[:, None] // 8 == t[None, :] // 8)
    g = np.zeros((128, 768), np.float32)
    g[:, 0:128] = np.where(su, -1.0 / 16, 0.0)
    g[:, 128:256] = np.where(~su, -1.0 / 16, 0.0)
    g[:, 256:384] = su
    g[:, 384:512] = np.where(su & sq, -1.0 / 16, 0.0)
    g[:, 512:640] = np.where((~su) & sq, -1.0 / 16, 0.0)
    g[:, 640:768] = su & sq
    c["gla_c"] = g
    seq = t // 8
    sx = np.zeros((128, 26), np.float32)
    sx[:, 0] = (seq % 2 == 0); sx[:, 1] = (seq % 2 == 1)
    for p in range(8):
        sx[:, 2 + p] = np.where(seq // 2 == p, -1.0 / 16, 0.0)
    for i in range(16):
        sx[:, 10 + i] = (seq == i)
    c["smp_x"] = sx
    m2 = np.zeros((128, 8, 128), np.float32)
    for il in range(2):
        for p in range(8):
            m2[il * 64:(il + 1) * 64, p, :] = (seq == 2 * p + il)[None, :]
    c["m2"] = m2.reshape(128, 1024)
    dm = np.full((64, 16, 128), NEG, np.float32)
    for s in range(16):
        for tq in range(8):
            for h in range(8):
                dm[h * 8 + tq, s, 8 * s:8 * s + tq + 1] = 0.0
    c["dmask"] = dm.reshape(64, 2048)
    c["ident_b"] = np.eye(128, dtype=np.float32)
    c["ident_f"] = np.eye(128, dtype=np.float32)
    c["pm16"] = (t % 16).astype(np.int32).reshape(128, 1)
    return c


def _cs_table(pos):
    inv = (1.0 / (np.float32(10000.0) ** (np.arange(0, 32, 2, dtype=np.float32) / np.float32(32)))).astype(np.float32)
    ang = (pos.astype(np.float32)[:, None] * inv[None, :]).astype(np.float32)
    co, si = np.cos(ang).astype(np.float32), np.sin(ang).astype(np.float32)
    return np.concatenate([co, co, -si, si], axis=1).astype(np.float32)


_NC_CACHE = {}


def kernel(x_prompt, x_sample, p_prompt, p_sample, cache_ckv, cache_krope, state_gla, page_table,
           g_mix_norm, w_in, g_qnorm, w_qup, g_kvnorm, w_uk, w_uv, w_gla_a2, b_gla_a, g_gla_onorm,
           w_out, w_ple_gate, w_ple_proj, g_final):
    f = lambda a: np.ascontiguousarray(np.asarray(a))
    x_prompt, x_sample, p_prompt, p_sample = f(x_prompt), f(x_sample), f(p_prompt), f(p_sample)
    cache_ckv, cache_krope, state_gla, page_table = f(cache_ckv), f(cache_krope), f(state_gla), f(page_table)
    B, S, _ = x_prompt.shape
    BD, TD, _ = x_sample.shape
    NPG = page_table.shape[1]
    NPOOL = cache_ckv.shape[1]
    n_cores = 2 * B
    assert BD == 16 * n_cores and TD == 8 and NPG % 8 == 0 and S % 256 == 0
    NS, NG8 = S // 256, NPG // 8
    past = NPG * cache_ckv.shape[2]
    key = (NS, NG8, NPOOL)
    if key not in _NC_CACHE:
        _NC_CACHE[key] = build(NS, NG8, NPOOL)
    nc = _NC_CACHE[key]

    w_in0 = f(w_in)[0]
    bnd = np.cumsum([0, 256, 160, 512, 256, 256, 512, 16, 512])
    qc, kv, gm, qg, kg, vg, ag, gg = [w_in0[:, bnd[j]:bnd[j + 1]] for j in range(8)]
    w_in_r = f(np.concatenate([kv, kg, vg, ag, qc, gm, qg, gg], axis=1))
    wq = f(w_qup)[0].reshape(256, 8, 96)
    w_qup_r = f(np.concatenate([wq[:, :, :64].reshape(256, 512), wq[:, :, 64:].reshape(256, 256)], axis=1))
    w_ukT = f(np.transpose(f(w_uk)[0], (2, 1, 0)).reshape(64, 1024))
    w_uv2 = f(f(w_uv)[0].reshape(128, 512))
    common = dict(w_in_r=w_in_r, w_qup_r=w_qup_r, w_ukT=w_ukT, w_uv=w_uv2, w_a2=f(w_gla_a2)[0], w_out=f(w_out)[0],
                  w_pg=f(w_ple_gate)[0], w_pp=f(w_ple_proj)[0], g_mix=f(g_mix_norm), g_q=f(g_qnorm), g_kv=f(g_kvnorm),
                  g_on=f(g_gla_onorm), b_a=f(b_gla_a), g_fin=f(g_final).reshape(1, -1),
                  pool_c=cache_ckv[0].reshape(NPOOL * 16, 1024), pool_r=cache_krope[0].reshape(NPOOL * 16, 256))
    common.update(_consts())
    cs_all = _cs_table(np.arange(S))
    cs_smp = _cs_table(past + (np.arange(128) % 8))
    tri = np.where(np.arange(128)[:, None] >= np.arange(128)[None, :], 0.0, NEG).astype(np.float32)
    in_maps = []
    for c in range(n_cores):
        b, r = c // 2, c % 2
        xb = x_prompt[b].reshape(NS, 2, 128, D)
        m = dict(common)
        m["x_all"] = x_prompt[b]
        m["x_own"] = f(xb[:, r].reshape(NS * 128, D))
        m["p_own"] = f(p_prompt[0, b].reshape(NS, 2, 128, 256)[:, r].reshape(NS * 128, 256))
        m["cs_all"] = cs_all
        m["cs_own"] = f(cs_all.reshape(NS, 2, 128, 64)[:, r].reshape(NS * 128, 64))
        m["x_smp"] = f(x_sample[16 * c:16 * c + 16].reshape(128, D))
        m["p_smp"] = f(p_sample[0, 16 * c:16 * c + 16].reshape(128, 256))
        m["cs_smp"] = cs_smp
        mk4 = np.zeros((128, 512), np.float32)
        if r == 0:
            mk4[:, 256:384] = tri; mk4[:, 384:512] = NEG
        else:
            mk4[:, 384:512] = tri
        m["mk4"] = mk4
        par = np.zeros((128, 2), np.float32); par[:, 0] = 1 - r; par[:, 1] = r
        m["par"] = par
        pt = page_table[16 * c:16 * c + 16].reshape(16, NG8, 8)
        ptl = np.repeat(np.transpose(pt, (2, 0, 1)).reshape(8, 16 * NG8), 16, axis=0)
        m["ptl"] = f(ptl.astype(np.int32))
        m["st_in"] = f(state_gla[0, 16 * c:16 * c + 16])
        in_maps.append(m)
    res = run_bass_kernel_spmd(nc, in_maps, core_ids=list(range(n_cores)))
    R = res.results
    y_p = np.zeros((B, S, D), np.float32)
    ckv_p = np.zeros((1, B, S, 128), np.float32); kr_p = np.zeros((1, B, S, 32), np.float32)
    st_p = np.zeros((1, B, 4, 64, 128), np.float32)
    y_s = np.zeros((BD, TD, D), np.float32); ckv_s = np.zeros((1, BD, TD, 128), np.float32)
    kr_s = np.zeros((1, BD, TD, 32), np.float32); st_s = np.zeros((1, BD, 4, 64, 128), np.float32)
    for c in range(n_cores):
        b, r = c // 2, c % 2
        y_p[b].reshape(NS, 2, 128, D)[:, r] = R[c]["y_own"].reshape(NS, 128, D)
        if r == 0:
            ckv_p[0, b] = R[c]["ckv_o"]; kr_p[0, b] = R[c]["kr_o"]; st_p[0, b] = R[c]["stp_o"]
        y_s[16 * c:16 * c + 16] = R[c]["y_smp"].reshape(16, 8, D)
        ckv_s[0, 16 * c:16 * c + 16] = R[c]["ckvs_o"].reshape(16, 8, 128)
        kr_s[0, 16 * c:16 * c + 16] = R[c]["krs_o"].reshape(16, 8, 32)
        st_s[0, 16 * c:16 * c + 16] = R[c]["sts_o"]
    return (y_p, y_s, ckv_p, kr_p, st_p, ckv_s, kr_s, st_s)
```

```python
import contextlib
import numpy as np
import concourse.bass as bass
import concourse.mybir as mybir
from concourse.bass_utils import run_bass_kernel_spmd

F32 = mybir.dt.float32
BF16 = mybir.dt.bfloat16
I32 = mybir.dt.int32
AF = mybir.ActivationFunctionType
ALU = mybir.AluOpType
AX = mybir.AxisListType

D = 1024
NCH = 8
SC = 96.0 ** -0.5
NEG = -30000.0
C_KV, C_KG, C_VG, C_A, C_QC, C_GM, C_QG, C_GG = 0, 160, 416, 928, 944, 1200, 1712, 1968


class Buf:
    __slots__ = ("name", "last_w", "readers", "dsem", "dcount")

    def __init__(self, name):
        self.name = name
        self.last_w = None
        self.readers = []
        self.dsem = None
        self.dcount = 0


class Op:
    __slots__ = ("eng", "fn", "deps", "is_dma", "done", "needed", "waits", "idx")

    def __init__(self, eng, fn, is_dma):
        self.eng = eng
        self.fn = fn
        self.deps = []
        self.is_dma = is_dma
        self.done = None
        self.needed = False
        self.waits = []


class Tl:
    __slots__ = ("t", "b")

    def __init__(self, t, b):
        self.t = t
        self.b = b


class Prog:
    ENGS = ("pe", "act", "dve", "pool", "sp")
    EPOCH = 12000

    def __init__(self, nc, es):
        self.nc = nc
        self.es = es
        self.ops = {e: [] for e in self.ENGS}
        self.order = []
        self.out_dmas = []

    def sem(self, name):
        return self.es.enter_context(self.nc.semaphore(name))

    def tile(self, name, shape, dtype, psum=False):
        if psum:
            t = self.es.enter_context(self.nc.psum_tensor(name, shape, dtype))
        else:
            t = self.es.enter_context(self.nc.sbuf_tensor(name, shape, dtype))
        return Tl(t, Buf(name))

    def _add(self, eng, fn, reads, writes, is_dma, dsem_buf=None):
        op = Op(eng, fn, is_dma)
        deps = []
        for b in reads:
            if b.last_w is not None:
                deps.append((b.last_w, "raw"))
        for b in writes:
            if b.last_w is not None:
                deps.append((b.last_w, "waw"))
            for r in b.readers:
                deps.append((r, "war"))
        for d, kind in deps:
            if d is op:
                continue
            same = (d.eng == eng) and (not d.is_dma) and (not is_dma)
            if same and (kind != "raw" or eng == "pe"):
                continue
            op.deps.append(d)
            d.needed = True
        for b in reads:
            b.readers.append(op)
        for b in writes:
            b.last_w = op
            b.readers = []
        if is_dma:
            if dsem_buf.dsem is None:
                dsem_buf.dsem = self.sem("d_" + dsem_buf.name)
            dsem_buf.dcount += 16
            op.done = (dsem_buf.dsem, dsem_buf.dcount)
        self.ops[eng].append(op)
        self.order.append(op)
        return op

    def pe(self, fn, reads, writes):
        return self._add("pe", fn, reads, writes, False)

    def act(self, fn, reads, writes):
        return self._add("act", fn, reads, writes, False)

    def dve(self, fn, reads, writes):
        return self._add("dve", fn, reads, writes, False)

    def pool(self, fn, reads, writes):
        return self._add("pool", fn, reads, writes, False)

    def dma(self, q, fn, reads, writes, sb, is_out=False):
        op = self._add(q, fn, reads, writes, True, sb)
        if is_out:
            self.out_dmas.append(op)
        return op

    def barrier(self):
        lasts = [self.ops[e][-1] for e in self.ENGS if self.ops[e]]
        dmas = [o for o in self.order if o.is_dma]
        for e in self.ENGS:
            op = Op(e, lambda eng: eng.nop(), False)
            for d in lasts + dmas:
                if d.is_dma or d.eng != e:
                    op.deps.append(d)
                    d.needed = True
            self.ops[e].append(op)
            self.order.append(op)

    def finalize(self):
        esems = {}
        for e in self.ENGS:
            cnt = 0
            ep = 0
            cur = None
            for op in self.ops[e]:
                if op.is_dma or not op.needed:
                    continue
                if cur is None or cnt >= self.EPOCH:
                    cur = self.sem(f"e_{e}_{ep}")
                    ep += 1
                    cnt = 0
                cnt += 1
                op.done = (cur, cnt)
        fin = Op("sp", None, False)
        last = {}
        for o in self.out_dmas:
            s, v = o.done
            k = id(s)
            if k not in last or last[k][1] < v:
                last[k] = (s, v)
        fin.waits = list(last.values())
        waited = {e: {} for e in self.ENGS}
        for op in self.order:
            w = {}
            for d in op.deps:
                s, v = d.done
                k = id(s)
                if k not in w or w[k][1] < v:
                    w[k] = (s, v)
            wm = waited[op.eng]
            for k, (s, v) in w.items():
                if wm.get(k, 0) >= v:
                    continue
                wm[k] = v
                op.waits.append((s, v))
        self.fin = fin

    def emit(self, block):
        nc = self.nc

        def run(e, eng):
            for op in self.ops[e]:
                for s, v in op.waits:
                    eng.wait_ge(s, v)
                ins = op.fn(eng)
                if op.is_dma:
                    ins.then_inc(op.done[0], 16)
                elif op.needed:
                    ins.then_inc(op.done[0], 1)
            if e == "sp":
                for s, v in self.fin.waits:
                    eng.wait_ge(s, v)

        @block.tensor
        def _(eng):
            run("pe", eng)

        @block.scalar
        def _(eng):
            run("act", eng)

        @block.vector
        def _(eng):
            run("dve", eng)

        @block.gpsimd
        def _(eng):
            run("pool", eng)

        @block.sync
        def _(eng):
            run("sp", eng)


def build(NS, NG8, NPOOL):
    SEQL = 256 * NS
    NKB = 2 * NS
    NOWN = NS * 128
    nc = bass.Bass("TRN2", target_bir_lowering=False)
    es = contextlib.ExitStack()
    P = Prog(nc, es)

    def din(name, shape, dt=F32):
        return nc.dram_tensor(name, list(shape), dt, kind="ExternalInput").ap()

    def dout(name, shape, dt=F32):
        return nc.dram_tensor(name, list(shape), dt, kind="ExternalOutput").ap()

    x_all = din("x_all", [SEQL, D]); x_own = din("x_own", [NOWN, D]); p_own = din("p_own", [NOWN, 256])
    x_smp = din("x_smp", [128, D]); p_smp = din("p_smp", [128, 256])
    cs_all = din("cs_all", [SEQL, 64]); cs_own = din("cs_own", [NOWN, 64]); cs_smp = din("cs_smp", [128, 64])
    w_in_d = din("w_in_r", [D, 2480]); w_qup_d = din("w_qup_r", [256, 768]); w_ukT_d = din("w_ukT", [64, 1024])
    w_uv_d = din("w_uv", [128, 512]); w_a2_d = din("w_a2", [16, 256]); w_out_d = din("w_out", [D, D])
    w_pg_d = din("w_pg", [D, D]); w_pp_d = din("w_pp", [256, D])
    g_mix_d = din("g_mix", [1, D]); g_q_d = din("g_q", [1, 256]); g_kv_d = din("g_kv", [1, 128])
    g_on_d = din("g_on", [1, 128]); b_a_d = din("b_a", [1, 256]); g_fin_d = din("g_fin", [1, D])
    mk4_d = din("mk4", [128, 512]); par_d = din("par", [128, 2])
    gc_d = din("gla_c", [128, 6 * 128])
    sx_d = din("smp_x", [128, 2 + 8 + 16])
    m2_d = din("m2", [128, 1024])
    dmask_d = din("dmask", [64, 16 * 128])
    idb_d = din("ident_b", [128, 128]); idf_d = din("ident_f", [128, 128])
    ptl_d = din("ptl", [128, 16 * NG8], I32); pm16_d = din("pm16", [128, 1], I32)
    pool_c = din("pool_c", [NPOOL * 16, 1024]); pool_r = din("pool_r", [NPOOL * 16, 256])
    st_in = din("st_in", [16, 4, 64, 128])

    y_own = dout("y_own", [NOWN, D]); y_smp = dout("y_smp", [128, D])
    ckv_o = dout("ckv_o", [SEQL, 128]); kr_o = dout("kr_o", [SEQL, 32]); stp_o = dout("stp_o", [4, 64, 128])
    ckvs_o = dout("ckvs_o", [128, 128]); krs_o = dout("krs_o", [128, 32]); sts_o = dout("sts_o", [16, 4, 64, 128])

    T = P.tile
    W_in = T("W_in", [128, NCH, 2480], BF16); W_qup = T("W_qup", [128, 2, 768], BF16)
    W_ukT = T("W_ukT", [64, 1024], BF16); W_uv = T("W_uv", [128, 512], BF16); W_a2 = T("W_a2", [16, 256], BF16)
    W_out = T("W_out", [128, NCH, D], BF16); W_pg = T("W_pg", [128, NCH, D], BF16); W_pp = T("W_pp", [128, 2, D], BF16)
    G_mix = T("G_mix", [128, D], F32); G_fin = T("G_fin", [128, D], F32); G_q = T("G_q", [128, 256], F32)
    G_kv = T("G_kv", [128, 128], F32); G_on = T("G_on", [128, 128], F32); B_a = T("B_a", [128, 256], F32)
    PAR = T("PAR", [128, 2], F32); GC = T("GC", [128, 768], F32)
    SX = T("SX", [128, 26], F32)
    IDB = T("IDB", [128, 128], BF16); IDF = T("IDF", [128, 128], F32)
    ONESB = T("ONESB", [128, 1], BF16); ONESF = T("ONESF", [1, 128], F32)
    CMAX = T("CMAX", [128, 1], F32)
    WB = Buf("weights")

    WBQ = {"pool": Buf("weights_pool"), "sp": Buf("weights_sp")}

    def ld(q, dst, src, dap=None):
        P.dma(q, lambda e: e.dma_start(out=(dst.t[:] if dap is None else dap), in_=src), [], [], WBQ[q])

    for c in range(NCH):
        ld("pool", W_in, w_in_d[c * 128:(c + 1) * 128, :], W_in.t[:, c, :])
        ld("pool", W_out, w_out_d[c * 128:(c + 1) * 128, :], W_out.t[:, c, :])
        ld("pool", W_pg, w_pg_d[c * 128:(c + 1) * 128, :], W_pg.t[:, c, :])
    for c in range(2):
        ld("pool", W_qup, w_qup_d[c * 128:(c + 1) * 128, :], W_qup.t[:, c, :])
        ld("pool", W_pp, w_pp_d[c * 128:(c + 1) * 128, :], W_pp.t[:, c, :])
    ld("pool", W_ukT, w_ukT_d); ld("pool", W_uv, w_uv_d); ld("pool", W_a2, w_a2_d)
    ld("pool", IDB, idb_d)
    for dst, src in ((G_mix, g_mix_d), (G_fin, g_fin_d), (G_q, g_q_d), (G_kv, g_kv_d), (G_on, g_on_d), (B_a, b_a_d)):
        ld("sp", dst, src.partition_broadcast(128))
    for dst, src in ((PAR, par_d), (GC, gc_d), (SX, sx_d), (IDF, idf_d)):
        ld("sp", dst, src)
    P.pool(lambda e: e.memset(ONESB.t[:], 1.0), [], [ONESB.b])
    P.pool(lambda e: e.memset(ONESF.t[:], 1.0), [], [ONESF.b])
    CT = T("CT", [128, 128], F32)

    def init_consts():
        P.dve(lambda e: e.tensor_tensor(out=CT.t[:], in0=G_kv.t[:], in1=G_kv.t[:], op=ALU.mult), [G_kv.b, WB], [CT.b])
        P.dve(lambda e: e.reduce_max(out=CMAX.t[:], in_=CT.t[:], axis=AX.X), [CT.b], [CMAX.b])
        P.act(lambda e: e.activation(out=CMAX.t[:], in_=CMAX.t[:], func=AF.Ln, scale=128.0), [CMAX.b], [CMAX.b])
        P.act(lambda e: e.activation(out=CMAX.t[:], in_=CMAX.t[:], func=AF.Exp, scale=0.5), [CMAX.b], [CMAX.b])


    RMAX2 = T("RMAX2", [128, 1], F32)
    P.dve(lambda e: e.memset(RMAX2.t[:], 0.0), [], [RMAX2.b])

    PZ2 = T("pz", [128, 1024], F32, True); PS2 = T("ps", [128, 1024], F32, True)
    pz = [Tl(PZ2.t[:, 0:512], Buf("pz0")), Tl(PZ2.t[:, 512:1024], Buf("pz1"))]
    ps = [Tl(PS2.t[:, 0:512], Buf("ps0")), Tl(PS2.t[:, 512:1024], Buf("ps1"))]
    pa = [T("pa0", [128, 512], F32, True), T("pa1", [128, 512], F32, True)]
    ptp = T("ptp", [128, 1024], BF16, True)
    ppt = T("ppt", [128, 1024], BF16, True)
    ppth = [Tl(ppt.t[:, 0:512], Buf("ppt_a")), Tl(ppt.t[:, 512:1024], Buf("ppt_b"))]

    XO = T("XO", [128, D], F32); PIN = T("PIN", [128, 256], F32)
    CSO = T("CSO", [128, 64], F32)
    JUNK = T("JUNK", [128, D], BF16)
    XN = T("XN", [128, D], BF16); XNT = T("XNT", [128, NCH, 128], BF16)
    SS = T("SS", [128, 1], F32); RS = T("RS", [128, 1], F32)
    SS8 = T("SS8", [128, 8], F32); RS8 = T("RS8", [128, 8], F32)
    CKF = T("CKF", [128, 128], F32); CKB = T("CKB", [128, 128], BF16)
    KRF = T("KRF", [128, 32], F32); KRB = T("KRB", [128, 96], BF16); RT1 = T("RT1", [128, 256], F32); RT2 = T("RT2", [128, 256], F32)
    NR2 = T("NR2", [128, 1], F32)
    AGT = T("AGT", [16, 128], BF16)
    T1 = T("T1", [128, 256], F32); LSB = T("LSB", [128, 256], F32); ERB = T("ERB", [128, 256], F32)
    KD = T("KD", [128, 256], BF16); VB = T("VB", [128, 512], BF16); EBL = T("EBL", [128, 8], F32)
    EBT = T("EBT", [64, 512], F32); ENBT = T("ENBT", [64, 512], F32)
    QET = T("QET", [64, 512], BF16); KET = T("KET", [64, 512], BF16); ATM = T("ATM", [128, 512], BF16)
    SOWNB = T("SOWNB", [64, 4, 128], BF16)
    OG = T("OG", [128, 512], F32)
    SIG = T("SIG", [128, D], F32)
    CAT = T("CAT", [128, D], BF16); CATT = T("CATT", [128, NCH, 128], BF16); H1T = CATT
    QCN = T("QCN", [128, 256], BF16); QCNT = T("QCNT", [128, 2, 128], BF16)
    QNT = T("QNT", [64, 1024], BF16); QLT = T("QLT", [128, 1024], BF16)
    QRB = T("QRB", [128, 768], BF16); QRT = T("QRT", [96, 1024], BF16)
    NL2 = T("NL2", [128, 8], F32); NRQ = T("NRQ", [128, 8], F32); NEGM = T("NEGM", [128, 8], F32)
    R1 = T("R1", [1, 1], F32); RM = T("RM", [128, 1], F32)
    SM = T("SM", [128, 512], F32)
    PB = [T(f"PB{k}", [128, 512], BF16) for k in range(2)]
    PT = [T(f"PT{k}", [128, 512], BF16) for k in range(2)]
    LR = T("LR", [128, 8], F32)
    OLAT = T("OLAT", [128, 8, 128], BF16); OLT = T("OLT", [128, 1024], BF16); QSQ = OLT
    H1 = XO; H1B = JUNK; H2 = XO; YO = SIG
    PBF = T("PBF", [128, 256], BF16); PTT = T("PTT", [128, 2, 128], BF16)
    es_main = P.es
    es_p = contextlib.ExitStack(); P.es = es_p
    NGM = max(1, (NKB + 3) // 4); NCC = (NGM + 2) // 3
    CKVT = T("CKVT", [128, SEQL], BF16); KRT = T("KRT", [96, NCC * 512], BF16); CKVN = T("CKVN", [128, NKB, 128], BF16)
    kvb = [Buf(f"kv{j}") for j in range(NKB)]
    ST = [T(f"ST{k}", [64, 4, 128], F32) for k in range(3)]
    MK4 = T("MK4", [128, 512], BF16)
    XS1 = T("XS", [128, D], F32); XS = [XS1, XS1]
    CSS1 = T("CSS", [128, 64], F32); CSS = [CSS1, CSS1]
    LSUM = T("LSUM", [128, 8, NGM], F32)
    P.es = es_main
    ld("pool", MK4, mk4_d)
    P.dve(lambda e: e.memset(ST[0].t[:], 0.0), [], [ST[0].b])
    Z = type("Z", (), {})()

    def alloc_sample():
        Z.PGC = [T(f"PGC{k}", [128, 8, 128], F32) for k in range(2)]; Z.PGR = [T(f"PGR{k}", [128, 8, 32], F32) for k in range(2)]
        Z.PGCB = [T(f"PGCB{k}", [128, 8, 128], BF16) for k in range(2)]; Z.PGRB = [T(f"PGRB{k}", [128, 8, 32], BF16) for k in range(2)]
        Z.CTS = [T(f"CTS{k}", [128, 512], BF16) for k in range(1)]; Z.KTS = [T(f"KTS{k}", [32, 512], BF16) for k in range(1)]
        Z.CKVT_S = T("CKVT_S", [128, 128], BF16); Z.KRT_S = T("KRT_S", [32, 128], BF16); Z.CKVN_S = T("CKVN_S", [128, 128], BF16)
        Z.MRUN = T("MRUN", [64, 1], F32); Z.MNEW = T("MNEW", [64, 1], F32); Z.NMN = T("NMN", [64, 1], F32)
        Z.ALP = T("ALP", [64, 1], F32); Z.LRUN = T("LRUN", [64, 1], F32); Z.LG = T("LG", [64, 1], F32); Z.GMX = T("GMX", [64, 1], F32)
        Z.ACC = T("ACC", [64, 128], F32); Z.ACCB = T("ACCB", [64, 128], BF16)
        s0 = T("S0F", [128, 8, 128], F32); Z.S0F = [s0, s0]; Z.S0B = T("S0B", [128, 8, 128], BF16)
        Z.L2 = T("L2", [128, 2, 64], F32); Z.EBS = T("EBS", [128, 8], F32)
        Z.QEX = T("QEX", [128, 8, 128], BF16); Z.KDX = T("KDX", [128, 16, 64], BF16)
        Z.QEN = T("QEN", [128, 2, 64], BF16)
        Z.SNEW = T("SNEW", [128, 8, 128], F32)
        Z.M2 = T("M2", [128, 1024], BF16); Z.DMASK = T("DMASK", [64, 2048], BF16)
        Z.PTL = T("PTL", [128, 16 * NG8], I32); Z.PM16 = T("PM16", [128, 1], I32); Z.IDX = T("IDX", [128, 16 * NG8], I32)
        ld("pool", Z.M2, m2_d); ld("pool", Z.DMASK, dmask_d); ld("sp", Z.PTL, ptl_d); ld("sp", Z.PM16, pm16_d)

    ident = IDB.t[:]

    def transposes(src, dst, nch, cp="act"):
        for c in range(nch):
            P.pe(lambda e, c=c: e.transpose(out=ptp.t[:, c * 128:(c + 1) * 128], in_=src.t[:, c * 128:(c + 1) * 128],
                                            identity=ident), [src.b, IDB.b], [ptp.b])
        f = (lambda e: e.activation(out=dst.t[:].rearrange("p c t -> p (c t)"), in_=ptp.t[:, 0:nch * 128], func=AF.Copy)) \
            if cp == "act" else (lambda e: e.tensor_copy(out=dst.t[:].rearrange("p c t -> p (c t)"), in_=ptp.t[:, 0:nch * 128]))
        (P.act if cp == "act" else P.dve)(f, [ptp.b], [dst.b])

    def linear(out_ps, xT, nch, W, c0, n):
        for c in range(nch):
            P.pe(lambda e, c=c: e.matmul(out_ps.t[:, 0:n], lhsT=xT.t[:, c, :], rhs=W.t[:, c, c0:c0 + n],
                                         start=(c == 0), stop=(c == nch - 1)), [xT.b, W.b, WB], [out_ps.b])

    def linearT(out_ap, out_b, W, c0, m, xT, nch):
        for c in range(nch):
            P.pe(lambda e, c=c: e.matmul(out_ap, lhsT=W.t[:, c, c0:c0 + m], rhs=xT.t[:, c, :],
                                         start=(c == 0), stop=(c == nch - 1)), [xT.b, W.b, WB], [out_b])

    def sumsq(src_ap, src_b, n, out_ap, out_b):
        P.dve(lambda e: e.scalar_tensor_tensor(out=JUNK.t[:, 0:n], in0=src_ap, scalar=1.0, in1=src_ap, op0=ALU.mult,
                                               op1=ALU.mult, accum_out=out_ap), [src_b], [JUNK.b, out_b])

    def powm(out_ap, out_b, in_ap, in_b, scale, bias, power, tmp_ap):
        P.act(lambda e: e.activation(out=tmp_ap, in_=in_ap, func=AF.Ln, bias=bias, scale=scale), [in_b], [out_b])
        P.act(lambda e: e.activation(out=out_ap, in_=tmp_ap, func=AF.Exp, scale=power), [out_b], [out_b])

    def rms_prep(xt):
        sumsq(xt.t[:], xt.b, D, SS.t[:], SS.b)
        powm(RS.t[:], RS.b, SS.t[:], SS.b, 1.0 / D, 1e-6, -0.5, RS.t[:])

    def front(xt):
        rms_prep(xt)
        P.dve(lambda e: e.scalar_tensor_tensor(out=XN.t[:], in0=xt.t[:], scalar=RS.t[:, 0:1], in1=G_mix.t[:],
                                               op0=ALU.mult, op1=ALU.mult), [xt.b, RS.b, G_mix.b, WB], [XN.b])
        transposes(XN, XNT, NCH)

    def rope(src_ap, src_b, nh, cst, outs):
        n = nh * 32
        cs = cst.t[:, 0:32].unsqueeze(1).to_broadcast([128, nh, 32])
        sa = cst.t[:, 32:48].unsqueeze(1).to_broadcast([128, nh, 16])
        sb_ = cst.t[:, 48:64].unsqueeze(1).to_broadcast([128, nh, 16])
        a = RT1.t[:, 0:n].rearrange("p (h r) -> p h r", r=32)
        b = RT2.t[:, 0:n].rearrange("p (h r) -> p h r", r=32)
        P.dve(lambda e: e.tensor_tensor(out=a, in0=src_ap, in1=cs, op=ALU.mult), [src_b, cst.b], [RT1.b])
        P.dve(lambda e: e.tensor_tensor(out=b[:, :, 0:16], in0=src_ap[:, :, 16:32], in1=sa, op=ALU.mult), [src_b, cst.b], [RT2.b])
        P.dve(lambda e: e.tensor_tensor(out=b[:, :, 16:32], in0=src_ap[:, :, 0:16], in1=sb_, op=ALU.mult), [src_b, cst.b], [RT2.b])
        for oap, ob in outs:
            P.dve(lambda e, oap=oap: e.tensor_tensor(out=oap, in0=a, in1=b, op=ALU.add), [RT1.b, RT2.b], [ob])

    def kv_part(cst, ckvt_ap, krt_ap, ckvn_ap, kvbuf, track_rmax, pg=0):
        P.act(lambda e: e.activation(out=CKF.t[:], in_=pz[0].t[:, 0:128], func=AF.Copy), [pz[0].b], [CKF.b])
        sumsq(CKF.t[:], CKF.b, 128, SS8.t[:, 0:1], SS8.b)
        powm(RS8.t[:, 0:1], RS8.b, SS8.t[:, 0:1], SS8.b, 1.0 / 128, 1e-6, -0.5, RS8.t[:, 0:1])
        P.dve(lambda e: e.scalar_tensor_tensor(out=CKF.t[:], in0=CKF.t[:], scalar=RS8.t[:, 0:1], in1=G_kv.t[:],
                                               op0=ALU.mult, op1=ALU.mult), [CKF.b, RS8.b, G_kv.b, WB], [CKF.b])
        P.act(lambda e: e.activation(out=CKB.t[:], in_=CKF.t[:], func=AF.Copy), [CKF.b], [CKB.b])
        P.act(lambda e: e.activation(out=ckvn_ap, in_=CKF.t[:], func=AF.Copy), [CKF.b], [kvbuf])
        rope(pz[0].t[:, 128:160].rearrange("p (h r) -> p h r", r=32), pz[0].b, 1, cst,
             [(KRF.t[:].rearrange("p (h r) -> p h r", r=32), KRF.b)] +
             [(KRB.t[:, 32 * q:32 * q + 32].rearrange("p (h r) -> p h r", r=32), KRB.b) for q in range(3)])
        if track_rmax:
            sumsq(KRF.t[:], KRF.b, 32, NR2.t[:], NR2.b)
            P.dve(lambda e: e.tensor_tensor(out=RMAX2.t[:], in0=RMAX2.t[:], in1=NR2.t[:], op=ALU.max), [RMAX2.b, NR2.b], [RMAX2.b])
        P.pe(lambda e: e.transpose(out=ptp.t[:, 0:128], in_=CKB.t[:], identity=ident), [CKB.b, IDB.b], [ptp.b])
        P.pe(lambda e: e.transpose(out=ptp.t[0:96, 128:256], in_=KRB.t[:], identity=ident), [KRB.b, IDB.b], [ptp.b])
        P.act(lambda e: e.activation(out=ckvt_ap, in_=ptp.t[:, 0:128], func=AF.Copy), [ptp.b], [kvbuf])
        P.act(lambda e: e.activation(out=krt_ap, in_=ptp.t[32 * pg:32 * pg + 32, 128:256], func=AF.Copy), [ptp.b], [kvbuf])

    def gla_nat(triL_ap):
        linearT(ps[0].t[0:16, 0:128], ps[0].b, W_in, C_A, 16, XNT, NCH)
        P.act(lambda e: e.activation(out=AGT.t[:], in_=ps[0].t[0:16, 0:128], func=AF.Copy), [ps[0].b], [AGT.b])
        P.pe(lambda e: e.matmul(ps[1].t[:, 0:256], lhsT=AGT.t[:], rhs=W_a2.t[:], start=True, stop=True),
             [AGT.b, W_a2.b, WB], [ps[1].b])
        P.dve(lambda e: e.tensor_tensor(out=T1.t[:], in0=ps[1].t[:, 0:256], in1=B_a.t[:], op=ALU.add), [ps[1].b, B_a.b, WB], [T1.b])
        P.act(lambda e: e.activation(out=T1.t[:], in_=T1.t[:], func=AF.Exp, scale=-1.0), [T1.b], [T1.b])
        P.act(lambda e: e.activation(out=LSB.t[:], in_=T1.t[:], func=AF.Ln, bias=1.0), [T1.b], [LSB.b])
        P.pe(lambda e: e.matmul(pa[0].t[:, 0:256], lhsT=triL_ap, rhs=LSB.t[:], start=True, stop=True), [GC.b, LSB.b, WB], [pa[0].b])
        P.act(lambda e: e.activation(out=ERB.t[:], in_=pa[0].t[:, 0:256], func=AF.Exp), [pa[0].b], [ERB.b])
        P.dve(lambda e: e.tensor_tensor(out=KD.t[:], in0=pz[0].t[:, C_KG:C_KG + 256], in1=ERB.t[:], op=ALU.mult), [pz[0].b, ERB.b], [KD.b])
        P.act(lambda e: e.activation(out=VB.t[:], in_=pz[1].t[:], func=AF.Copy), [pz[1].b], [VB.b])

    def load_x(q, xt, src, sb=None):
        P.dma(q, lambda e: e.dma_start(out=xt.t[:], in_=src), [], [xt.b], xt.b)

    def store(src_t, src_ap, dst):
        P.dma("pool", lambda e: e.dma_start(out=dst, in_=src_ap), [src_t.b], [], src_t.b, is_out=True)

    TRIU_P, TRIL_P, MASK_P = GC.t[:, 0:128], GC.t[:, 128:256], GC.t[:, 256:384]
    TRIU_S, TRIL_S, MASK_S = GC.t[:, 384:512], GC.t[:, 512:640], GC.t[:, 640:768]

    def shared(blk):
        xt = XS[blk % 2]; cst = CSS[blk % 2]
        load_x("sp", xt, x_all[blk * 128:(blk + 1) * 128, :])
        load_x("sp", cst, cs_all[blk * 128:(blk + 1) * 128, :])
        front(xt)
        linear(pz[0], XNT, NCH, W_in, 0, 416)
        linear(pz[1], XNT, NCH, W_in, C_VG, 512)
        g_ = blk // 4; pg = g_ % 3; kc = (g_ // 3) * 512 + (blk % 4) * 128
        kv_part(cst, CKVT.t[:, blk * 128:(blk + 1) * 128], KRT.t[32 * pg:32 * pg + 32, kc:kc + 128], CKVN.t[:, blk, :], kvb[blk], True, pg)
        store(CKF, CKF.t[:], ckv_o[blk * 128:(blk + 1) * 128, :])
        store(KRF, KRF.t[:], kr_o[blk * 128:(blk + 1) * 128, :])
        gla_nat(TRIL_P)
        for h in range(4):
            P.pe(lambda e, h=h: e.matmul(pa[0].t[0:64, 256 + h:257 + h], lhsT=LSB.t[:, h * 64:(h + 1) * 64], rhs=TRIU_P[:, 127:128],
                                         start=True, stop=True), [LSB.b, GC.b, WB], [pa[0].b])
        P.act(lambda e: e.activation(out=EBL.t[0:64, 0:4], in_=pa[0].t[0:64, 256:260], func=AF.Exp), [pa[0].b], [EBL.b])
        sin, sout = ST[blk % 3], ST[(blk + 1) % 3]
        for h in range(4):
            P.pe(lambda e, h=h: e.matmul(pa[1].t[0:64, h * 128:(h + 1) * 128], lhsT=KD.t[:, h * 64:(h + 1) * 64],
                                         rhs=VB.t[:, h * 128:(h + 1) * 128], start=True, stop=True), [KD.b, VB.b], [pa[1].b])
        for h in range(4):
            P.dve(lambda e, h=h: e.scalar_tensor_tensor(out=sout.t[:, h, :], in0=sin.t[:, h, :], scalar=EBL.t[0:64, h:h + 1],
                                                        in1=pa[1].t[0:64, h * 128:(h + 1) * 128], op0=ALU.mult, op1=ALU.add),
                  [sin.b, EBL.b, pa[1].b], [sout.b])

    def silu_to(dst_ap, dst_b, gate_ps, o_ap, o_bufs, n):
        P.act(lambda e: e.activation(out=SIG.t[:, 0:n], in_=gate_ps.t[:, 0:n], func=AF.Exp, scale=-1.0), [gate_ps.b], [SIG.b])
        P.dve(lambda e: e.tensor_scalar(out=SIG.t[:, 0:n], in0=SIG.t[:, 0:n], scalar1=1.0, scalar2=None, op0=ALU.add), [SIG.b], [SIG.b])
        P.dve(lambda e: e.reciprocal(out=SIG.t[:, 0:n], in_=SIG.t[:, 0:n]), [SIG.b], [SIG.b])
        P.dve(lambda e: e.tensor_tensor(out=SIG.t[:, 0:n], in0=gate_ps.t[:, 0:n], in1=SIG.t[:, 0:n], op=ALU.mult), [gate_ps.b, SIG.b], [SIG.b])
        P.dve(lambda e: e.tensor_tensor(out=dst_ap, in0=o_ap, in1=SIG.t[:, 0:n], op=ALU.mult), o_bufs + [SIG.b], [dst_b])

    def own(i, smp):
        if smp:
            xsrc, psrc, cssrc, ydst = x_smp, p_smp, cs_smp, y_smp
        else:
            sl = slice(i * 128, (i + 1) * 128)
            xsrc, psrc, cssrc, ydst = x_own[sl, :], p_own[sl, :], cs_own[sl, :], y_own[sl, :]
        load_x("sp", XO, xsrc); load_x("sp", PIN, psrc); load_x("sp", CSO, cssrc)
        front(XO)
        triU, triL, maskT = (TRIU_S, TRIL_S, MASK_S) if smp else (TRIU_P, TRIL_P, MASK_P)
        linear(pz[0], XNT, NCH, W_in, 0, 416)
        linear(pz[1], XNT, NCH, W_in, C_VG, 512)
        if smp:
            kv_part(CSO, Z.CKVT_S.t[:], Z.KRT_S.t[:], Z.CKVN_S.t[:], Z.CKVN_S.b, False)
            store(CKF, CKF.t[:], ckvs_o); store(KRF, KRF.t[:], krs_o)
        gla_nat(triL)
        for h in range(4):
            P.pe(lambda e, h=h: e.matmul(pa[1].t[0:64, h * 128:(h + 1) * 128], lhsT=LSB.t[:, h * 64:(h + 1) * 64], rhs=triU,
                                         start=True, stop=True), [LSB.b, GC.b, WB], [pa[1].b])
        P.act(lambda e: e.activation(out=EBT.t[:], in_=pa[1].t[0:64, :], func=AF.Exp), [pa[1].b], [EBT.b])
        P.act(lambda e: e.activation(out=ENBT.t[:], in_=pa[1].t[0:64, :], func=AF.Exp, scale=-1.0), [pa[1].b], [ENBT.b])
        for h in range(4):
            linearT(ps[0].t[0:64, h * 128:(h + 1) * 128], ps[0].b, W_in, C_QG + h * 64, 64, XNT, NCH)
        for h in range(4):
            linearT(ps[1].t[0:64, h * 128:(h + 1) * 128], ps[1].b, W_in, C_KG + h * 64, 64, XNT, NCH)
        P.dve(lambda e: e.scalar_tensor_tensor(out=QET.t[:], in0=ps[0].t[0:64, :], scalar=0.125, in1=EBT.t[:], op0=ALU.mult,
                                               op1=ALU.mult), [ps[0].b, EBT.b], [QET.b])
        P.dve(lambda e: e.tensor_tensor(out=KET.t[:], in0=ps[1].t[0:64, :], in1=ENBT.t[:], op=ALU.mult), [ps[1].b, ENBT.b], [KET.b])
        for h in range(4):
            P.pe(lambda e, h=h: e.matmul(pa[0].t[:, h * 128:(h + 1) * 128], lhsT=KET.t[:, h * 128:(h + 1) * 128],
                                         rhs=QET.t[:, h * 128:(h + 1) * 128], start=True, stop=True), [KET.b, QET.b], [pa[0].b])
        P.dve(lambda e: e.tensor_tensor(out=ATM.t[:].rearrange("p (h t) -> p h t", h=4), in0=pa[0].t[:].rearrange("p (h t) -> p h t", h=4),
                                        in1=maskT.unsqueeze(1).to_broadcast([128, 4, 128]), op=ALU.mult), [pa[0].b, GC.b], [ATM.b])
        if not smp:
            sa, sb_ = ST[(2 * i) % 3], ST[(2 * i + 1) % 3]
            sown = OG.t[0:64, :].rearrange("k (h v) -> k h v", h=4)
            P.dve(lambda e: e.tensor_scalar(out=sown, in0=sa.t[:], scalar1=PAR.t[0:64, 0:1], scalar2=None, op0=ALU.mult),
                  [sa.b, PAR.b, WB], [OG.b])
            P.dve(lambda e: e.scalar_tensor_tensor(out=SOWNB.t[:], in0=sb_.t[:], scalar=PAR.t[0:64, 1:2], in1=sown,
                                                   op0=ALU.mult, op1=ALU.add), [sb_.b, PAR.b, OG.b], [SOWNB.b])
            for h in range(4):
                P.pe(lambda e, h=h: e.matmul(ps[0].t[:, h * 128:(h + 1) * 128], lhsT=ATM.t[:, h * 128:(h + 1) * 128],
                                             rhs=VB.t[:, h * 128:(h + 1) * 128], start=True, stop=False), [ATM.b, VB.b], [ps[0].b])
                P.pe(lambda e, h=h: e.matmul(ps[0].t[:, h * 128:(h + 1) * 128], lhsT=QET.t[:, h * 128:(h + 1) * 128],
                                             rhs=SOWNB.t[:, h, :], start=False, stop=True), [QET.b, SOWNB.b], [ps[0].b])
        else:
            parm = SX.t[:, 0:2]; msel = SX.t[:, 2:10]; kdm = SX.t[:, 10:26]
            for h in range(4):
                s0f = Z.S0F[h % 2]
                for il in range(2):
                    P.dma("sp", lambda e, h=h, il=il, s0f=s0f: e.dma_start(
                        out=s0f.t[il * 64:(il + 1) * 64, :, :],
                        in_=st_in[:, h, :, :].rearrange("(p il) k v -> il k p v", il=2)[il]), [], [s0f.b], s0f.b)
                P.act(lambda e, s0f=s0f: e.activation(out=Z.S0B.t[:].rearrange("p a v -> p (a v)"), in_=s0f.t[:].rearrange("p a v -> p (a v)"),
                                                      func=AF.Copy), [s0f.b], [Z.S0B.b])
                P.pe(lambda e, h=h: e.transpose(out=ppt.t[:, 0:64], in_=QET.t[:, h * 128:(h + 1) * 128], identity=IDB.t[0:64, 0:64]),
                     [QET.b, IDB.b], [ppt.b])
                P.dve(lambda e: e.tensor_copy(out=Z.QEN.t[:], in_=ppt.t[:, 0:64].unsqueeze(1).to_broadcast([128, 2, 64])), [ppt.b], [Z.QEN.b])
                P.pe(lambda e: e.transpose(out=ppt.t[:, 128:256], in_=Z.QEN.t[:].rearrange("p a k -> p (a k)"), identity=ident),
                     [Z.QEN.b, IDB.b], [ppt.b])
                P.dve(lambda e: e.tensor_tensor(out=Z.QEX.t[:], in0=ppt.t[:, 128:256].unsqueeze(1).to_broadcast([128, 8, 128]),
                                                in1=Z.M2.t[:].rearrange("p (a t) -> p a t", a=8), op=ALU.mult), [ppt.b, Z.M2.b], [Z.QEX.b])
                P.pe(lambda e, h=h: e.matmul(ps[0].t[:, h * 128:(h + 1) * 128], lhsT=ATM.t[:, h * 128:(h + 1) * 128],
                                             rhs=VB.t[:, h * 128:(h + 1) * 128], start=True, stop=False), [ATM.b, VB.b], [ps[0].b])
                for p in range(8):
                    P.pe(lambda e, h=h, p=p: e.matmul(ps[0].t[:, h * 128:(h + 1) * 128], lhsT=Z.QEX.t[:, p, :], rhs=Z.S0B.t[:, p, :],
                                                      start=False, stop=(p == 7)), [Z.QEX.b, Z.S0B.b], [ps[0].b])
                P.dve(lambda e, h=h: e.tensor_tensor(out=Z.L2.t[:], in0=LSB.t[:, h * 64:(h + 1) * 64].unsqueeze(1).to_broadcast([128, 2, 64]),
                                                     in1=parm.unsqueeze(2).to_broadcast([128, 2, 64]), op=ALU.mult), [LSB.b, SX.b], [Z.L2.b])
                P.pe(lambda e: e.matmul(pa[1].t[:, 0:8], lhsT=Z.L2.t[:].rearrange("p a k -> p (a k)"), rhs=msel, start=True, stop=True),
                     [Z.L2.b, SX.b], [pa[1].b])
                P.act(lambda e: e.activation(out=Z.EBS.t[:], in_=pa[1].t[:, 0:8], func=AF.Exp), [pa[1].b], [Z.EBS.b])
                P.dve(lambda e, h=h: e.tensor_tensor(out=Z.KDX.t[:], in0=KD.t[:, h * 64:(h + 1) * 64].unsqueeze(1).to_broadcast([128, 16, 64]),
                                                     in1=kdm.unsqueeze(2).to_broadcast([128, 16, 64]), op=ALU.mult), [KD.b, SX.b], [Z.KDX.b])
                for half in range(2):
                    for p4 in range(4):
                        p = half * 4 + p4
                        P.pe(lambda e, h=h, p=p, p4=p4: e.matmul(pa[0].t[:, p4 * 128:(p4 + 1) * 128],
                                                                 lhsT=Z.KDX.t[:, 2 * p:2 * p + 2, :].rearrange("t a k -> t (a k)"),
                                                                 rhs=VB.t[:, h * 128:(h + 1) * 128], start=True, stop=True),
                             [Z.KDX.b, VB.b], [pa[0].b])
                    for p4 in range(4):
                        p = half * 4 + p4
                        P.dve(lambda e, p=p, p4=p4, s0f=s0f: e.scalar_tensor_tensor(
                            out=Z.SNEW.t[:, p, :], in0=s0f.t[:, p, :], scalar=Z.EBS.t[:, p:p + 1], in1=pa[0].t[:, p4 * 128:(p4 + 1) * 128],
                            op0=ALU.mult, op1=ALU.add), [s0f.b, Z.EBS.b, pa[0].b], [Z.SNEW.b])
                for il in range(2):
                    P.dma("pool", lambda e, h=h, il=il: e.dma_start(
                        out=sts_o[:, h, :, :].rearrange("(p il) k v -> il k p v", il=2)[il],
                        in_=Z.SNEW.t[il * 64:(il + 1) * 64, :, :]), [Z.SNEW.b], [], Z.SNEW.b, is_out=True)
        P.act(lambda e: e.activation(out=OG.t[:], in_=ps[0].t[:], func=AF.Copy), [ps[0].b], [OG.b])
        for h in range(4):
            sumsq(OG.t[:, h * 128:(h + 1) * 128], OG.b, 128, SS8.t[:, h:h + 1], SS8.b)
        powm(RS8.t[:, 0:4], RS8.b, SS8.t[:, 0:4], SS8.b, 1.0 / 128, 1e-6, -0.5, RS8.t[:, 0:4])
        for h in range(4):
            P.dve(lambda e, h=h: e.scalar_tensor_tensor(out=OG.t[:, h * 128:(h + 1) * 128], in0=OG.t[:, h * 128:(h + 1) * 128],
                                                        scalar=RS8.t[:, h:h + 1], in1=G_on.t[:], op0=ALU.mult, op1=ALU.mult),
                  [OG.b, RS8.b, G_on.b, WB], [OG.b])
        linear(pz[0], XNT, NCH, W_in, C_GG, 512)
        silu_to(CAT.t[:, 512:1024], CAT.b, pz[0], OG.t[:], [OG.b], 512)

        linear(pz[1], XNT, NCH, W_in, C_QC, 256)
        P.act(lambda e: e.activation(out=RT1.t[:], in_=pz[1].t[:, 0:256], func=AF.Copy), [pz[1].b], [RT1.b])
        sumsq(RT1.t[:], RT1.b, 256, SS.t[:], SS.b)
        powm(RS.t[:], RS.b, SS.t[:], SS.b, 1.0 / 256, 1e-6, -0.5, RS.t[:])
        P.dve(lambda e: e.scalar_tensor_tensor(out=QCN.t[:], in0=RT1.t[:], scalar=RS.t[:, 0:1], in1=G_q.t[:],
                                               op0=ALU.mult, op1=ALU.mult), [RT1.b, RS.b, G_q.b, WB], [QCN.b])
        transposes(QCN, QCNT, 2)
        for h in range(8):
            linearT(ps[h // 4].t[0:64, (h % 4) * 128:(h % 4 + 1) * 128], ps[h // 4].b, W_qup, h * 64, 64, QCNT, 2)
        for k in range(2):
            P.act(lambda e, k=k: e.activation(out=QNT.t[:, k * 512:(k + 1) * 512], in_=ps[k].t[0:64, :], func=AF.Copy), [ps[k].b], [QNT.b])
        for h in range(8):
            P.pe(lambda e, h=h: e.matmul(pz[h // 4].t[:, (h % 4) * 128:(h % 4 + 1) * 128], lhsT=W_ukT.t[:, h * 128:(h + 1) * 128],
                                         rhs=QNT.t[:, h * 128:(h + 1) * 128], start=True, stop=True), [W_ukT.b, WB, QNT.b], [pz[h // 4].b])
        for k in range(2):
            if smp:
                o = QLT.t[:].rearrange("l (s h t) -> l h s t", s=16, h=8, t=8)[:, 4 * k:4 * k + 4, :, :]
                i_ = pz[k].t[:].rearrange("l (h s t) -> l h s t", h=4, s=16, t=8)
            else:
                o = QLT.t[:, k * 512:(k + 1) * 512]; i_ = pz[k].t[:]
            P.act(lambda e, o=o, i_=i_: e.activation(out=o, in_=i_, func=AF.Copy), [pz[k].b], [QLT.b])
        linear(ps[0], QCNT, 2, W_qup, 512, 256)
        rope(ps[0].t[:, 0:256].rearrange("p (h r) -> p h r", r=32), ps[0].b, 8, CSO,
             [(QRB.t[:].rearrange("p (h q r) -> p h q r", q=3, r=32)[:, :, q, :], QRB.b) for q in range(3)])
        for h in range(8):
            P.pe(lambda e, h=h: e.transpose(out=ptp.t[0:96, h * 128:(h + 1) * 128], in_=QRB.t[:, h * 96:(h + 1) * 96], identity=ident),
                 [QRB.b, IDB.b], [ptp.b])
        if smp:
            o = QRT.t[:].rearrange("l (s h t) -> l h s t", s=16, h=8, t=8)
            i_ = ptp.t[0:96, :].rearrange("l (h s t) -> l h s t", h=8, s=16, t=8)
        else:
            o = QRT.t[:]; i_ = ptp.t[0:96, :]
        P.act(lambda e: e.activation(out=o, in_=i_, func=AF.Copy), [ptp.b], [QRT.b])

        if not smp:
            P.dve(lambda e: e.tensor_tensor(out=QSQ.t[:], in0=QLT.t[:], in1=QLT.t[:], op=ALU.mult), [QLT.b], [QSQ.b])
            for h in range(8):
                P.pe(lambda e, h=h: e.matmul(pa[0].t[:, h:h + 1], lhsT=QSQ.t[:, h * 128:(h + 1) * 128], rhs=ONESB.t[:],
                                             start=True, stop=True), [QSQ.b, ONESB.b], [pa[0].b])
            qr0 = QRB.t[:].rearrange("p (h q r) -> p h q r", q=3, r=32)[:, :, 0, :]
            P.dve(lambda e: e.tensor_tensor(out=RT1.t[:].rearrange("p (h r) -> p h r", r=32), in0=qr0, in1=qr0, op=ALU.mult), [QRB.b], [RT1.b])
            P.dve(lambda e: e.tensor_reduce(out=NRQ.t[:], in_=RT1.t[:].rearrange("p (h r) -> p h r", r=32), axis=AX.X, op=ALU.add),
                  [RT1.b], [NRQ.b])
            P.pe(lambda e: e.transpose(out=pa[1].t[0:1, 0:128], in_=RMAX2.t[:], identity=IDF.t[:]), [RMAX2.b, IDF.b, WB], [pa[1].b])
            P.dve(lambda e: e.reduce_max(out=R1.t[:], in_=pa[1].t[0:1, 0:128], axis=AX.X), [pa[1].b], [R1.b])
            P.pe(lambda e: e.matmul(pa[1].t[:, 128:129], lhsT=ONESF.t[:], rhs=R1.t[:], start=True, stop=True), [ONESF.b, R1.b], [pa[1].b])
            powm(RM.t[:], RM.b, pa[1].t[:, 128:129], pa[1].b, 1.0, 1e-20, 0.5, RM.t[:])
            powm(NL2.t[:], NL2.b, pa[0].t[:, 0:8], pa[0].b, 1.0, 1e-20, 0.5, NL2.t[:])
            powm(NRQ.t[:], NRQ.b, NRQ.t[:], NRQ.b, 1.0, 1e-20, 0.5, NRQ.t[:])
            P.dve(lambda e: e.tensor_scalar(out=NL2.t[:], in0=NL2.t[:], scalar1=CMAX.t[:, 0:1], scalar2=None, op0=ALU.mult),
                  [NL2.b, CMAX.b], [NL2.b])
            P.dve(lambda e: e.scalar_tensor_tensor(out=NEGM.t[:], in0=NRQ.t[:], scalar=RM.t[:, 0:1], in1=NL2.t[:], op0=ALU.mult,
                                                   op1=ALU.add), [NRQ.b, RM.b, NL2.b], [NEGM.b])
            P.dve(lambda e: e.tensor_scalar(out=NEGM.t[:], in0=NEGM.t[:], scalar1=-SC, scalar2=None, op0=ALU.mult), [NEGM.b], [NEGM.b])
            nkb = 2 * i + 2
            ng = (nkb + 3) // 4
            steps = [(h, g) for h in range(8) for g in range(ng)]
            nst = len(steps)

            def geo(t):
                h, g = steps[t]
                nb = min(4, nkb - 4 * g)
                return h, g, nb, nb * 128

            def st_S(t):
                h, g, nb, Wd = geo(t)
                psx = ps[t % 2]; c0 = 4 * g * 128
                kbufs = [kvb[4 * g + j] for j in range(nb)]
                P.pe(lambda e: e.matmul(psx.t[:, 0:Wd], lhsT=QLT.t[:, h * 128:(h + 1) * 128], rhs=CKVT.t[:, c0:c0 + Wd],
                                        start=True, stop=False), [QLT.b] + kbufs, [psx.b])
                pg = g % 3; kc = (g // 3) * 512
                P.pe(lambda e: e.matmul(psx.t[:, 0:Wd], lhsT=QRT.t[32 * pg:32 * pg + 32, h * 128:(h + 1) * 128],
                                        rhs=KRT.t[32 * pg:32 * pg + 32, kc:kc + Wd], start=False, stop=True), [QRT.b] + kbufs, [psx.b])

            def st_X(t):
                h, g, nb, Wd = geo(t)
                psx = ps[t % 2]; pb = PB[t % 2]
                if g == ng - 1:
                    P.dve(lambda e: e.tensor_tensor(out=SM.t[:, 0:Wd], in0=psx.t[:, 0:Wd], in1=MK4.t[:, 512 - Wd:512], op=ALU.add),
                          [psx.b, MK4.b], [SM.b])
                    src_ap, src_b = SM.t[:, 0:Wd], SM.b
                else:
                    src_ap, src_b = psx.t[:, 0:Wd], psx.b
                P.act(lambda e: e.activation(out=pb.t[:, 0:Wd], in_=src_ap, func=AF.Exp, bias=NEGM.t[:, h:h + 1], scale=SC,
                                             accum_out=LSUM.t[:, h, g:g + 1]), [src_b, NEGM.b], [pb.b, LSUM.b])

            def st_T(t):
                h, g, nb, Wd = geo(t)
                pb = PB[t % 2]; pt = PT[t % 2]; pph = ppth[t % 2]
                for j in range(nb):
                    P.pe(lambda e, j=j: e.transpose(out=pph.t[:, j * 128:(j + 1) * 128], in_=pb.t[:, j * 128:(j + 1) * 128], identity=ident),
                         [pb.b, IDB.b], [pph.b])
                P.dve(lambda e: e.tensor_copy(out=pt.t[:, 0:Wd], in_=pph.t[:, 0:Wd]), [pph.b], [pt.b])

            def st_PV(t):
                h, g, nb, Wd = geo(t)
                pt = PT[t % 2]; bank = pa[h // 4]
                for j in range(nb):
                    first = (h % 4 == 0 and g == 0 and j == 0)
                    lastm = (h % 4 == 3 and g == ng - 1 and j == nb - 1)
                    P.pe(lambda e, j=j, first=first, lastm=lastm: e.matmul(
                        bank.t[:, (h % 4) * 128:(h % 4 + 1) * 128], lhsT=pt.t[:, j * 128:(j + 1) * 128], rhs=CKVN.t[:, 4 * g + j, :],
                        start=first, stop=lastm, skip_group_check=True), [pt.b, kvb[4 * g + j]], [bank.b])

            st_S(0)
            for t in range(nst):
                if t + 1 < nst:
                    st_S(t + 1)
                st_X(t)
                st_T(t)
                if t >= 1:
                    st_PV(t - 1)
            st_PV(nst - 1)
            P.dve(lambda e: e.tensor_reduce(out=LR.t[:], in_=LSUM.t[:, :, 0:ng], axis=AX.X, op=ALU.add), [LSUM.b], [LR.b])
            P.dve(lambda e: e.reciprocal(out=LR.t[:], in_=LR.t[:]), [LR.b], [LR.b])
            for h in range(8):
                P.dve(lambda e, h=h: e.tensor_scalar(out=OLAT.t[:, h, :], in0=pa[h // 4].t[:, (h % 4) * 128:(h % 4 + 1) * 128],
                                                     scalar1=LR.t[:, h:h + 1], scalar2=None, op0=ALU.mult), [pa[h // 4].b, LR.b], [OLAT.b])
            for h in range(8):
                P.pe(lambda e, h=h: e.transpose(out=ptp.t[:, h * 128:(h + 1) * 128], in_=OLAT.t[:, h, :], identity=ident),
                     [OLAT.b, IDB.b], [ptp.b])
            P.act(lambda e: e.activation(out=OLT.t[:], in_=ptp.t[:], func=AF.Copy), [ptp.b], [OLT.b])
        else:
            decode_attention()

        for h in range(8):
            P.pe(lambda e, h=h: e.matmul(pz[0].t[:, h * 64:(h + 1) * 64], lhsT=OLT.t[:, h * 128:(h + 1) * 128], rhs=W_uv.t[:, h * 64:(h + 1) * 64],
                                         start=True, stop=True), [OLT.b, W_uv.b, WB], [pz[0].b])
        P.act(lambda e: e.activation(out=OG.t[:], in_=pz[0].t[:], func=AF.Copy), [pz[0].b], [OG.b])
        linear(pz[1], XNT, NCH, W_in, C_GM, 512)
        silu_to(CAT.t[:, 0:512], CAT.b, pz[1], OG.t[:], [OG.b], 512)
        transposes(CAT, CATT, NCH)
        for k in range(2):
            linear(pz[k], CATT, NCH, W_out, k * 512, 512)
            P.dve(lambda e, k=k: e.tensor_tensor(out=H1.t[:, k * 512:(k + 1) * 512], in0=pz[k].t[:], in1=XO.t[:, k * 512:(k + 1) * 512],
                                                 op=ALU.add), [pz[k].b, XO.b], [H1.b])
        P.act(lambda e: e.activation(out=H1B.t[:], in_=H1.t[:], func=AF.Copy), [H1.b], [H1B.b])
        transposes(H1B, H1T, NCH)
        P.act(lambda e: e.activation(out=PBF.t[:], in_=PIN.t[:], func=AF.Copy), [PIN.b], [PBF.b])
        transposes(PBF, PTT, 2)
        for k in range(2):
            linear(pz[k], H1T, NCH, W_pg, k * 512, 512)
            sg = SIG.t[:, k * 512:(k + 1) * 512]
            P.act(lambda e, k=k, sg=sg: e.activation(out=sg, in_=pz[k].t[:], func=AF.Exp, scale=-1.0), [pz[k].b], [SIG.b])
            P.dve(lambda e, sg=sg: e.tensor_scalar(out=sg, in0=sg, scalar1=1.0, scalar2=None, op0=ALU.add), [SIG.b], [SIG.b])
            P.dve(lambda e, sg=sg: e.reciprocal(out=sg, in_=sg), [SIG.b], [SIG.b])
            linear(ps[k], PTT, 2, W_pp, k * 512, 512)
            P.dve(lambda e, k=k, sg=sg: e.tensor_tensor(out=sg, in0=ps[k].t[:], in1=sg, op=ALU.mult), [ps[k].b, SIG.b], [SIG.b])
            P.dve(lambda e, k=k, sg=sg: e.tensor_tensor(out=H2.t[:, k * 512:(k + 1) * 512], in0=H1.t[:, k * 512:(k + 1) * 512], in1=sg,
                                                        op=ALU.add), [H1.b, SIG.b], [H2.b])
        rms_prep(H2)
        P.dve(lambda e: e.scalar_tensor_tensor(out=YO.t[:], in0=H2.t[:], scalar=RS.t[:, 0:1], in1=G_fin.t[:], op0=ALU.mult,
                                               op1=ALU.mult), [H2.b, RS.b, G_fin.b, WB], [YO.b])
        store(YO, YO.t[:], ydst)

    def decode_attention():
        def block_group(s, cT_ap, cT_b, kT_ap, kT_b, vsrc, nb, mask_ap):
            Wd = nb * 128
            psx = ps[0]
            P.pe(lambda e: e.matmul(psx.t[0:64, 0:Wd], lhsT=QLT.t[:, s * 64:(s + 1) * 64], rhs=cT_ap, start=True, stop=False),
                 [QLT.b, cT_b], [psx.b])
            P.pe(lambda e: e.matmul(psx.t[0:64, 0:Wd], lhsT=QRT.t[0:32, s * 64:(s + 1) * 64], rhs=kT_ap, start=False, stop=True),
                 [QRT.b, kT_b], [psx.b])
            if mask_ap is not None:
                P.dve(lambda e: e.tensor_tensor(out=SM.t[0:64, 0:Wd], in0=psx.t[0:64, 0:Wd], in1=mask_ap, op=ALU.add), [psx.b, Z.DMASK.b, WB], [SM.b])
                src_ap, src_b = SM.t[0:64, 0:Wd], SM.b
            else:
                src_ap, src_b = psx.t[0:64, 0:Wd], psx.b
            P.dve(lambda e: e.reduce_max(out=Z.GMX.t[:], in_=src_ap, axis=AX.X), [src_b], [Z.GMX.b])
            P.dve(lambda e: e.tensor_tensor(out=Z.MNEW.t[:], in0=Z.MRUN.t[:], in1=Z.GMX.t[:], op=ALU.max), [Z.MRUN.b, Z.GMX.b], [Z.MNEW.b])
            P.dve(lambda e: e.tensor_tensor(out=Z.ALP.t[:], in0=Z.MRUN.t[:], in1=Z.MNEW.t[:], op=ALU.subtract), [Z.MRUN.b, Z.MNEW.b], [Z.ALP.b])
            P.act(lambda e: e.activation(out=Z.ALP.t[:], in_=Z.ALP.t[:], func=AF.Exp, scale=SC), [Z.ALP.b], [Z.ALP.b])
            P.dve(lambda e: e.tensor_scalar(out=Z.NMN.t[:], in0=Z.MNEW.t[:], scalar1=-SC, scalar2=None, op0=ALU.mult), [Z.MNEW.b], [Z.NMN.b])
            P.dve(lambda e: e.tensor_copy(out=Z.MRUN.t[:], in_=Z.MNEW.t[:]), [Z.MNEW.b, Z.ALP.b], [Z.MRUN.b])
            pb = PB[0]
            P.act(lambda e: e.activation(out=pb.t[0:64, 0:Wd], in_=src_ap, func=AF.Exp, bias=Z.NMN.t[:, 0:1], scale=SC, accum_out=Z.LG.t[:, 0:1]),
                  [src_b, Z.NMN.b], [pb.b, Z.LG.b])
            P.dve(lambda e: e.scalar_tensor_tensor(out=Z.LRUN.t[:], in0=Z.LRUN.t[:], scalar=Z.ALP.t[:, 0:1], in1=Z.LG.t[:], op0=ALU.mult,
                                                   op1=ALU.add), [Z.LRUN.b, Z.ALP.b, Z.LG.b], [Z.LRUN.b])
            for j in range(nb):
                P.pe(lambda e, j=j: e.transpose(out=ppt.t[:, j * 64:(j + 1) * 64], in_=pb.t[0:64, j * 128:(j + 1) * 128],
                                                identity=IDB.t[0:64, 0:64]), [pb.b, IDB.b], [ppt.b])
            pt = PT[0]
            P.dve(lambda e: e.tensor_copy(out=pt.t[:, 0:nb * 64], in_=ppt.t[:, 0:nb * 64]), [ppt.b], [pt.b])
            for j in range(nb):
                vap, vb = vsrc(j)
                P.pe(lambda e, j=j, vap=vap: e.matmul(pa[0].t[0:64, 0:128], lhsT=pt.t[:, j * 64:(j + 1) * 64], rhs=vap,
                                                      start=(j == 0), stop=(j == nb - 1)), [pt.b, vb], [pa[0].b])
            P.dve(lambda e: e.scalar_tensor_tensor(out=Z.ACC.t[:], in0=Z.ACC.t[:], scalar=Z.ALP.t[:, 0:1], in1=pa[0].t[0:64, 0:128],
                                                   op0=ALU.mult, op1=ALU.add), [Z.ACC.b, Z.ALP.b, pa[0].b], [Z.ACC.b])

        gcount = 0
        for s in range(16):
            P.dve(lambda e: e.memset(Z.MRUN.t[:], -1.0e30), [], [Z.MRUN.b])
            P.dve(lambda e: e.memset(Z.LRUN.t[:], 0.0), [], [Z.LRUN.b])
            P.dve(lambda e: e.memset(Z.ACC.t[:], 0.0), [], [Z.ACC.b])
            for g8 in range(NG8):
                k = gcount % 2; gcount += 1
                col = s * NG8 + g8
                P.dma("pool", lambda e, k=k, col=col: e.indirect_dma_start(
                    out=Z.PGC[k].t[:].rearrange("p a l -> p (a l)"), out_offset=None, in_=pool_c,
                    in_offset=bass.IndirectOffsetOnAxis(ap=Z.IDX.t[:, col:col + 1], axis=0)), [Z.IDX.b], [Z.PGC[k].b], Z.PGC[k].b)
                P.dma("pool", lambda e, k=k, col=col: e.indirect_dma_start(
                    out=Z.PGR[k].t[:].rearrange("p a l -> p (a l)"), out_offset=None, in_=pool_r,
                    in_offset=bass.IndirectOffsetOnAxis(ap=Z.IDX.t[:, col:col + 1], axis=0)), [Z.IDX.b], [Z.PGR[k].b], Z.PGR[k].b)
                P.act(lambda e, k=k: e.activation(out=Z.PGCB[k].t[:].rearrange("p a l -> p (a l)"), in_=Z.PGC[k].t[:].rearrange("p a l -> p (a l)"),
                                                  func=AF.Copy), [Z.PGC[k].b], [Z.PGCB[k].b])
                P.dve(lambda e, k=k: e.tensor_copy(out=Z.PGRB[k].t[:].rearrange("p a l -> p (a l)"), in_=Z.PGR[k].t[:].rearrange("p a l -> p (a l)")),
                      [Z.PGR[k].b], [Z.PGRB[k].b])
                for half in range(2):
                    for j in range(4):
                        a = half * 4 + j
                        P.pe(lambda e, k=k, a=a, j=j: e.transpose(out=ptp.t[:, j * 128:(j + 1) * 128], in_=Z.PGCB[k].t[:, a, :], identity=ident),
                             [Z.PGCB[k].b, IDB.b], [ptp.b])
                        P.pe(lambda e, k=k, a=a, j=j: e.transpose(out=ptp.t[0:32, 512 + j * 128:512 + (j + 1) * 128], in_=Z.PGRB[k].t[:, a, :],
                                                                  identity=ident), [Z.PGRB[k].b, IDB.b], [ptp.b])
                    P.act(lambda e: e.activation(out=Z.CTS[0].t[:, 0:512], in_=ptp.t[:, 0:512], func=AF.Copy), [ptp.b], [Z.CTS[0].b])
                    P.act(lambda e: e.activation(out=Z.KTS[0].t[:, 0:512], in_=ptp.t[0:32, 512:1024], func=AF.Copy), [ptp.b], [Z.KTS[0].b])
                    block_group(s, Z.CTS[0].t[:, 0:512], Z.CTS[0].b, Z.KTS[0].t[:, 0:512], Z.KTS[0].b,
                                lambda j, k=k, half=half: (Z.PGCB[k].t[:, half * 4 + j, :], Z.PGCB[k].b), 4, None)
            block_group(s, Z.CKVT_S.t[:], Z.CKVN_S.b, Z.KRT_S.t[:], Z.CKVN_S.b, lambda j: (Z.CKVN_S.t[:], Z.CKVN_S.b), 1,
                        Z.DMASK.t[:, s * 128:(s + 1) * 128])
            P.dve(lambda e: e.reciprocal(out=Z.LG.t[:], in_=Z.LRUN.t[:]), [Z.LRUN.b], [Z.LG.b])
            P.dve(lambda e: e.tensor_scalar(out=Z.ACCB.t[:], in0=Z.ACC.t[:], scalar1=Z.LG.t[:, 0:1], scalar2=None, op0=ALU.mult),
                  [Z.ACC.b, Z.LG.b], [Z.ACCB.b])
            P.pe(lambda e: e.transpose(out=ppt.t[:, 512:576], in_=Z.ACCB.t[:], identity=IDB.t[0:64, 0:64]), [Z.ACCB.b, IDB.b], [ppt.b])
            P.dve(lambda e, s=s: e.tensor_copy(out=OLT.t[:].rearrange("l (h s t) -> l h s t", h=8, s=16, t=8)[:, :, s, :],
                                               in_=ppt.t[:, 512:576].rearrange("l (h t) -> l h t", h=8)), [ppt.b], [OLT.b])

    P.barrier()
    init_consts()
    for i in range(NS):
        shared(2 * i)
        shared(2 * i + 1)
        own(i, False)
    fs = ST[NKB % 3]
    P.dma("pool", lambda e: e.dma_start(out=stp_o.rearrange("h k v -> k h v"), in_=fs.t[:]), [fs.b], [], fs.b, is_out=True)
    P.barrier()
    es_p.close()
    alloc_sample()
    P.barrier()
    P.dve(lambda e: e.tensor_scalar(out=Z.IDX.t[:], in0=Z.PTL.t[:], scalar1=16.0, scalar2=Z.PM16.t[:, 0:1],
                                    op0=ALU.mult, op1=ALU.add), [Z.PTL.b, Z.PM16.b], [Z.IDX.b])
    own(0, True)

    P.finalize()
    with nc.Block() as block:
        P.emit(block)
    es.close()
    return nc


def _consts():
    import ml_dtypes
    t = np.arange(128)
    c = {}
    su = (t[:, None] <= t[None, :])
    sq = (t[:, None] // 8 == t[None, :] // 8)
    g = np.zeros((128, 768), np.float32)
    g[:, 0:128] = np.where(su, -1.0 / 16, 0.0)
    g[:, 128:256] = np.where(~su, -1.0 / 16, 0.0)
    g[:, 256:384] = su
    g[:, 384:512] = np.where(su & sq, -1.0 / 16, 0.0)
    g[:, 512:640] = np.where((~su) & sq, -1.0 / 16, 0.0)
    g[:, 640:768] = su & sq
    c["gla_c"] = g
    seq = t // 8
    sx = np.zeros((128, 26), np.float32)
    sx[:, 0] = (seq % 2 == 0); sx[:, 1] = (seq % 2 == 1)
    for p in range(8):
        sx[:, 2 + p] = np.where(seq // 2 == p, -1.0 / 16, 0.0)
    for i in range(16):
        sx[:, 10 + i] = (seq == i)
    c["smp_x"] = sx
    m2 = np.zeros((128, 8, 128), np.float32)
    for il in range(2):
        for p in range(8):
            m2[il * 64:(il + 1) * 64, p, :] = (seq == 2 * p + il)[None, :]
    c["m2"] = m2.reshape(128, 1024)
    dm = np.full((64, 16, 128), NEG, np.float32)
    for s in range(16):
        for tq in range(8):
            for h in range(8):
                dm[h * 8 + tq, s, 8 * s:8 * s + tq + 1] = 0.0
    c["dmask"] = dm.reshape(64, 2048)
    c["ident_b"] = np.eye(128, dtype=np.float32)
    c["ident_f"] = np.eye(128, dtype=np.float32)
    c["pm16"] = (t % 16).astype(np.int32).reshape(128, 1)
    return c


def _cs_table(pos):
    inv = (1.0 / (np.float32(10000.0) ** (np.arange(0, 32, 2, dtype=np.float32) / np.float32(32)))).astype(np.float32)
    ang = (pos.astype(np.float32)[:, None] * inv[None, :]).astype(np.float32)
    co, si = np.cos(ang).astype(np.float32), np.sin(ang).astype(np.float32)
    return np.concatenate([co, co, -si, si], axis=1).astype(np.float32)


_NC_CACHE = {}


def kernel(x_prompt, x_sample, p_prompt, p_sample, cache_ckv, cache_krope, state_gla, page_table,
           g_mix_norm, w_in, g_qnorm, w_qup, g_kvnorm, w_uk, w_uv, w_gla_a2, b_gla_a, g_gla_onorm,
           w_out, w_ple_gate, w_ple_proj, g_final):
    f = lambda a: np.ascontiguousarray(np.asarray(a))
    x_prompt, x_sample, p_prompt, p_sample = f(x_prompt), f(x_sample), f(p_prompt), f(p_sample)
    cache_ckv, cache_krope, state_gla, page_table = f(cache_ckv), f(cache_krope), f(state_gla), f(page_table)
    B, S, _ = x_prompt.shape
    BD, TD, _ = x_sample.shape
    NPG = page_table.shape[1]
    NPOOL = cache_ckv.shape[1]
    n_cores = 2 * B
    assert BD == 16 * n_cores and TD == 8 and NPG % 8 == 0 and S % 256 == 0
    NS, NG8 = S // 256, NPG // 8
    past = NPG * cache_ckv.shape[2]
    key = (NS, NG8, NPOOL)
    if key not in _NC_CACHE:
        _NC_CACHE[key] = build(NS, NG8, NPOOL)
    nc = _NC_CACHE[key]

    w_in0 = f(w_in)[0]
    bnd = np.cumsum([0, 256, 160, 512, 256, 256, 512, 16, 512])
    qc, kv, gm, qg, kg, vg, ag, gg = [w_in0[:, bnd[j]:bnd[j + 1]] for j in range(8)]
    w_in_r = f(np.concatenate([kv, kg, vg, ag, qc, gm, qg, gg], axis=1))
    wq = f(w_qup)[0].reshape(256, 8, 96)
    w_qup_r = f(np.concatenate([wq[:, :, :64].reshape(256, 512), wq[:, :, 64:].reshape(256, 256)], axis=1))
    w_ukT = f(np.transpose(f(w_uk)[0], (2, 1, 0)).reshape(64, 1024))
    w_uv2 = f(f(w_uv)[0].reshape(128, 512))
    common = dict(w_in_r=w_in_r, w_qup_r=w_qup_r, w_ukT=w_ukT, w_uv=w_uv2, w_a2=f(w_gla_a2)[0], w_out=f(w_out)[0],
                  w_pg=f(w_ple_gate)[0], w_pp=f(w_ple_proj)[0], g_mix=f(g_mix_norm), g_q=f(g_qnorm), g_kv=f(g_kvnorm),
                  g_on=f(g_gla_onorm), b_a=f(b_gla_a), g_fin=f(g_final).reshape(1, -1),
                  pool_c=cache_ckv[0].reshape(NPOOL * 16, 1024), pool_r=cache_krope[0].reshape(NPOOL * 16, 256))
    common.update(_consts())
    cs_all = _cs_table(np.arange(S))
    cs_smp = _cs_table(past + (np.arange(128) % 8))
    tri = np.where(np.arange(128)[:, None] >= np.arange(128)[None, :], 0.0, NEG).astype(np.float32)
    in_maps = []
    for c in range(n_cores):
        b, r = c // 2, c % 2
        xb = x_prompt[b].reshape(NS, 2, 128, D)
        m = dict(common)
        m["x_all"] = x_prompt[b]
        m["x_own"] = f(xb[:, r].reshape(NS * 128, D))
        m["p_own"] = f(p_prompt[0, b].reshape(NS, 2, 128, 256)[:, r].reshape(NS * 128, 256))
        m["cs_all"] = cs_all
        m["cs_own"] = f(cs_all.reshape(NS, 2, 128, 64)[:, r].reshape(NS * 128, 64))
        m["x_smp"] = f(x_sample[16 * c:16 * c + 16].reshape(128, D))
        m["p_smp"] = f(p_sample[0, 16 * c:16 * c + 16].reshape(128, 256))
        m["cs_smp"] = cs_smp
        mk4 = np.zeros((128, 512), np.float32)
        if r == 0:
            mk4[:, 256:384] = tri; mk4[:, 384:512] = NEG
        else:
            mk4[:, 384:512] = tri
        m["mk4"] = mk4
        par = np.zeros((128, 2), np.float32); par[:, 0] = 1 - r; par[:, 1] = r
        m["par"] = par
        pt = page_table[16 * c:16 * c + 16].reshape(16, NG8, 8)
        ptl = np.repeat(np.transpose(pt, (2, 0, 1)).reshape(8, 16 * NG8), 16, axis=0)
        m["ptl"] = f(ptl.astype(np.int32))
        m["st_in"] = f(state_gla[0, 16 * c:16 * c + 16])
        in_maps.append(m)
    res = run_bass_kernel_spmd(nc, in_maps, core_ids=list(range(n_cores)))
    R = res.results
    y_p = np.zeros((B, S, D), np.float32)
    ckv_p = np.zeros((1, B, S, 128), np.float32); kr_p = np.zeros((1, B, S, 32), np.float32)
    st_p = np.zeros((1, B, 4, 64, 128), np.float32)
    y_s = np.zeros((BD, TD, D), np.float32); ckv_s = np.zeros((1, BD, TD, 128), np.float32)
    kr_s = np.zeros((1, BD, TD, 32), np.float32); st_s = np.zeros((1, BD, 4, 64, 128), np.float32)
    for c in range(n_cores):
        b, r = c // 2, c % 2
        y_p[b].reshape(NS, 2, 128, D)[:, r] = R[c]["y_own"].reshape(NS, 128, D)
        if r == 0:
            ckv_p[0, b] = R[c]["ckv_o"]; kr_p[0, b] = R[c]["kr_o"]; st_p[0, b] = R[c]["stp_o"]
        y_s[16 * c:16 * c + 16] = R[c]["y_smp"].reshape(16, 8, D)
        ckv_s[0, 16 * c:16 * c + 16] = R[c]["ckvs_o"].reshape(16, 8, 128)
        kr_s[0, 16 * c:16 * c + 16] = R[c]["krs_o"].reshape(16, 8, 32)
        st_s[0, 16 * c:16 * c + 16] = R[c]["sts_o"]
    return (y_p, y_s, ckv_p, kr_p, st_p, ckv_s, kr_s, st_s)
```

```python
import contextlib
import numpy as np
import concourse.bass as bass
import concourse.mybir as mybir
from concourse.bass_utils import run_bass_kernel_spmd

F32 = mybir.dt.float32
BF16 = mybir.dt.bfloat16
I32 = mybir.dt.int32
AF = mybir.ActivationFunctionType
ALU = mybir.AluOpType
AX = mybir.AxisListType

D = 1024
NCH = 8
SC = 96.0 ** -0.5
NEG = -30000.0
C_KV, C_KG, C_VG, C_A, C_QC, C_GM, C_QG, C_GG = 0, 160, 416, 928, 944, 1200, 1712, 1968


class Buf:
    __slots__ = ("name", "last_w", "readers", "dsem", "dcount")

    def __init__(self, name):
        self.name = name
        self.last_w = None
        self.readers = []
        self.dsem = None
        self.dcount = 0


class Op:
    __slots__ = ("eng", "fn", "deps", "is_dma", "done", "needed", "waits", "idx")

    def __init__(self, eng, fn, is_dma):
        self.eng = eng
        self.fn = fn
        self.deps = []
        self.is_dma = is_dma
        self.done = None
        self.needed = False
        self.waits = []


class Tl:
    __slots__ = ("t", "b")

    def __init__(self, t, b):
        self.t = t
        self.b = b


class Prog:
    ENGS = ("pe", "act", "dve", "pool", "sp")
    EPOCH = 12000

    def __init__(self, nc, es):
        self.nc = nc
        self.es = es
        self.ops = {e: [] for e in self.ENGS}
        self.order = []
        self.out_dmas = []

    def sem(self, name):
        return self.es.enter_context(self.nc.semaphore(name))

    def tile(self, name, shape, dtype, psum=False):
        if psum:
            t = self.es.enter_context(self.nc.psum_tensor(name, shape, dtype))
        else:
            t = self.es.enter_context(self.nc.sbuf_tensor(name, shape, dtype))
        return Tl(t, Buf(name))

    def _add(self, eng, fn, reads, writes, is_dma, dsem_buf=None):
        op = Op(eng, fn, is_dma)
        deps = []
        for b in reads:
            if b.last_w is not None:
                deps.append((b.last_w, "raw"))
        for b in writes:
            if b.last_w is not None:
                deps.append((b.last_w, "waw"))
            for r in b.readers:
                deps.append((r, "war"))
        for d, kind in deps:
            if d is op:
                continue
            same = (d.eng == eng) and (not d.is_dma) and (not is_dma)
            if same and (kind != "raw" or eng == "pe"):
                continue
            op.deps.append(d)
            d.needed = True
        for b in reads:
            b.readers.append(op)
        for b in writes:
            b.last_w = op
            b.readers = []
        if is_dma:
            if dsem_buf.dsem is None:
                dsem_buf.dsem = self.sem("d_" + dsem_buf.name)
            dsem_buf.dcount += 16
            op.done = (dsem_buf.dsem, dsem_buf.dcount)
        self.ops[eng].append(op)
        self.order.append(op)
        return op

    def pe(self, fn, reads, writes):
        return self._add("pe", fn, reads, writes, False)

    def act(self, fn, reads, writes):
        return self._add("act", fn, reads, writes, False)

    def dve(self, fn, reads, writes):
        return self._add("dve", fn, reads, writes, False)

    def pool(self, fn, reads, writes):
        return self._add("pool", fn, reads, writes, False)

    def dma(self, q, fn, reads, writes, sb, is_out=False):
        op = self._add(q, fn, reads, writes, True, sb)
        if is_out:
            self.out_dmas.append(op)
        return op

    def barrier(self):
        lasts = [self.ops[e][-1] for e in self.ENGS if self.ops[e]]
        dmas = [o for o in self.order if o.is_dma]
        for e in self.ENGS:
            op = Op(e, lambda eng: eng.nop(), False)
            for d in lasts + dmas:
                if d.is_dma or d.eng != e:
                    op.deps.append(d)
                    d.needed = True
            self.ops[e].append(op)
            self.order.append(op)

    def finalize(self):
        esems = {}
        for e in self.ENGS:
            cnt = 0
            ep = 0
            cur = None
            for op in self.ops[e]:
                if op.is_dma or not op.needed:
                    continue
                if cur is None or cnt >= self.EPOCH:
                    cur = self.sem(f"e_{e}_{ep}")
                    ep += 1
                    cnt = 0
                cnt += 1
                op.done = (cur, cnt)
        fin = Op("sp", None, False)
        last = {}
        for o in self.out_dmas:
            s, v = o.done
            k = id(s)
            if k not in last or last[k][1] < v:
                last[k] = (s, v)
        fin.waits = list(last.values())
        waited = {e: {} for e in self.ENGS}
        for op in self.order:
            w = {}
            for d in op.deps:
                s, v = d.done
                k = id(s)
                if k not in w or w[k][1] < v:
                    w[k] = (s, v)
            wm = waited[op.eng]
            for k, (s, v) in w.items():
                if wm.get(k, 0) >= v:
                    continue
                wm[k] = v
                op.waits.append((s, v))
        self.fin = fin

    def emit(self, block):
        nc = self.nc

        def run(e, eng):
            for op in self.ops[e]:
                for s, v in op.waits:
                    eng.wait_ge(s, v)
                ins = op.fn(eng)
                if op.is_dma:
                    ins.then_inc(op.done[0], 16)
                elif op.needed:
                    ins.then_inc(op.done[0], 1)
            if e == "sp":
                for s, v in self.fin.waits:
                    eng.wait_ge(s, v)

        @block.tensor
        def _(eng):
            run("pe", eng)

        @block.scalar
        def _(eng):
            run("act", eng)

        @block.vector
        def _(eng):
            run("dve", eng)

        @block.gpsimd
        def _(eng):
            run("pool", eng)

        @block.sync
        def _(eng):
            run("sp", eng)


def build(NS, NG8, NPOOL):
    SEQL = 256 * NS
    NKB = 2 * NS
    NOWN = NS * 128
    nc = bass.Bass("TRN2", target_bir_lowering=False)
    es = contextlib.ExitStack()
    P = Prog(nc, es)

    def din(name, shape, dt=F32):
        return nc.dram_tensor(name, list(shape), dt, kind="ExternalInput").ap()

    def dout(name, shape, dt=F32):
        return nc.dram_tensor(name, list(shape), dt, kind="ExternalOutput").ap()

    x_all = din("x_all", [SEQL, D]); x_own = din("x_own", [NOWN, D]); p_own = din("p_own", [NOWN, 256])
    x_smp = din("x_smp", [128, D]); p_smp = din("p_smp", [128, 256])
    cs_all = din("cs_all", [SEQL, 64]); cs_own = din("cs_own", [NOWN, 64]); cs_smp = din("cs_smp", [128, 64])
    w_in_d = din("w_in_r", [D, 2480]); w_qup_d = din("w_qup_r", [256, 768]); w_ukT_d = din("w_ukT", [64, 1024])
    w_uv_d = din("w_uv", [128, 512]); w_a2_d = din("w_a2", [16, 256]); w_out_d = din("w_out", [D, D])
    w_pg_d = din("w_pg", [D, D]); w_pp_d = din("w_pp", [256, D])
    g_mix_d = din("g_mix", [1, D]); g_q_d = din("g_q", [1, 256]); g_kv_d = din("g_kv", [1, 128])
    g_on_d = din("g_on", [1, 128]); b_a_d = din("b_a", [1, 256]); g_fin_d = din("g_fin", [1, D])
    mk4_d = din("mk4", [128, 512]); par_d = din("par", [128, 2])
    gc_d = din("gla_c", [128, 6 * 128])
    sx_d = din("smp_x", [128, 2 + 8 + 16])
    m2_d = din("m2", [128, 1024])
    dmask_d = din("dmask", [64, 16 * 128])
    idb_d = din("ident_b", [128, 128]); idf_d = din("ident_f", [128, 128])
    ptl_d = din("ptl", [128, 16 * NG8], I32); pm16_d = din("pm16", [128, 1], I32)
    pool_c = din("pool_c", [NPOOL * 16, 1024]); pool_r = din("pool_r", [NPOOL * 16, 256])
    st_in = din("st_in", [16, 4, 64, 128])

    y_own = dout("y_own", [NOWN, D]); y_smp = dout("y_smp", [128, D])
    ckv_o = dout("ckv_o", [SEQL, 128]); kr_o = dout("kr_o", [SEQL, 32]); stp_o = dout("stp_o", [4, 64, 128])
    ckvs_o = dout("ckvs_o", [128, 128]); krs_o = dout("krs_o", [128, 32]); sts_o = dout("sts_o", [16, 4, 64, 128])

    T = P.tile
    W_in = T("W_in", [128, NCH, 2480], BF16); W_qup = T("W_qup", [128, 2, 768], BF16)
    W_ukT = T("W_ukT", [64, 1024], BF16); W_uv = T("W_uv", [128, 512], BF16); W_a2 = T("W_a2", [16, 256], BF16)
    W_out = T("W_out", [128, NCH, D], BF16); W_pg = T("W_pg", [128, NCH, D], BF16); W_pp = T("W_pp", [128, 2, D], BF16)
    G_mix = T("G_mix", [128, D], F32); G_fin = T("G_fin", [128, D], F32); G_q = T("G_q", [128, 256], F32)
    G_kv = T("G_kv", [128, 128], F32); G_on = T("G_on", [128, 128], F32); B_a = T("B_a", [128, 256], F32)
    PAR = T("PAR", [128, 2], F32); GC = T("GC", [128, 768], F32)
    SX = T("SX", [128, 26], F32)
    IDB = T("IDB", [128, 128], BF16); IDF = T("IDF", [128, 128], F32)
    ONESB = T("ONESB", [128, 1], BF16); ONESF = T("ONESF", [1, 128], F32)
    CMAX = T("CMAX", [128, 1], F32)
    WB = Buf("weights")

    WBQ = {"pool": Buf("weights_pool"), "sp": Buf("weights_sp")}

    def ld(q, dst, src, dap=None):
        P.dma(q, lambda e: e.dma_start(out=(dst.t[:] if dap is None else dap), in_=src), [], [], WBQ[q])

    for c in range(NCH):
        ld("pool", W_in, w_in_d[c * 128:(c + 1) * 128, :], W_in.t[:, c, :])
        ld("pool", W_out, w_out_d[c * 128:(c + 1) * 128, :], W_out.t[:, c, :])
        ld("pool", W_pg, w_pg_d[c * 128:(c + 1) * 128, :], W_pg.t[:, c, :])
    for c in range(2):
        ld("pool", W_qup, w_qup_d[c * 128:(c + 1) * 128, :], W_qup.t[:, c, :])
        ld("pool", W_pp, w_pp_d[c * 128:(c + 1) * 128, :], W_pp.t[:, c, :])
    ld("pool", W_ukT, w_ukT_d); ld("pool", W_uv, w_uv_d); ld("pool", W_a2, w_a2_d)
    ld("pool", IDB, idb_d)
    for dst, src in ((G_mix, g_mix_d), (G_fin, g_fin_d), (G_q, g_q_d), (G_kv, g_kv_d), (G_on, g_on_d), (B_a, b_a_d)):
        ld("sp", dst, src.partition_broadcast(128))
    for dst, src in ((PAR, par_d), (GC, gc_d), (SX, sx_d), (IDF, idf_d)):
        ld("sp", dst, src)
    P.pool(lambda e: e.memset(ONESB.t[:], 1.0), [], [ONESB.b])
    P.pool(lambda e: e.memset(ONESF.t[:], 1.0), [], [ONESF.b])
    CT = T("CT", [128, 128], F32)

    def init_consts():
        P.dve(lambda e: e.tensor_tensor(out=CT.t[:], in0=G_kv.t[:], in1=G_kv.t[:], op=ALU.mult), [G_kv.b, WB], [CT.b])
        P.dve(lambda e: e.reduce_max(out=CMAX.t[:], in_=CT.t[:], axis=AX.X), [CT.b], [CMAX.b])
        P.act(lambda e: e.activation(out=CMAX.t[:], in_=CMAX.t[:], func=AF.Ln, scale=128.0), [CMAX.b], [CMAX.b])
        P.act(lambda e: e.activation(out=CMAX.t[:], in_=CMAX.t[:], func=AF.Exp, scale=0.5), [CMAX.b], [CMAX.b])


    RMAX2 = T("RMAX2", [128, 1], F32)
    P.dve(lambda e: e.memset(RMAX2.t[:], 0.0), [], [RMAX2.b])

    PZ2 = T("pz", [128, 1024], F32, True); PS2 = T("ps", [128, 1024], F32, True)
    pz = [Tl(PZ2.t[:, 0:512], Buf("pz0")), Tl(PZ2.t[:, 512:1024], Buf("pz1"))]
    ps = [Tl(PS2.t[:, 0:512], Buf("ps0")), Tl(PS2.t[:, 512:1024], Buf("ps1"))]
    pa = [T("pa0", [128, 512], F32, True), T("pa1", [128, 512], F32, True)]
    ptp = T("ptp", [128, 1024], BF16, True)
    ppt = T("ppt", [128, 1024], BF16, True)
    ppth = [Tl(ppt.t[:, 0:512], Buf("ppt_a")), Tl(ppt.t[:, 512:1024], Buf("ppt_b"))]

    XO = T("XO", [128, D], F32); PIN = T("PIN", [128, 256], F32)
    CSO = T("CSO", [128, 64], F32)
    JUNK = T("JUNK", [128, D], BF16)
    XN = T("XN", [128, D], BF16); XNT = T("XNT", [128, NCH, 128], BF16)
    SS = T("SS", [128, 1], F32); RS = T("RS", [128, 1], F32)
    SS8 = T("SS8", [128, 8], F32); RS8 = T("RS8", [128, 8], F32)
    CKF = T("CKF", [128, 128], F32); CKB = T("CKB", [128, 128], BF16)
    KRF = T("KRF", [128, 32], F32); KRB = T("KRB", [128, 96], BF16); RT1 = T("RT1", [128, 256], F32); RT2 = T("RT2", [128, 256], F32)
    NR2 = T("NR2", [128, 1], F32)
    AGT = T("AGT", [16, 128], BF16)
    T1 = T("T1", [128, 256], F32); LSB = T("LSB", [128, 256], F32); ERB = T("ERB", [128, 256], F32)
    KD = T("KD", [128, 256], BF16); VB = T("VB", [128, 512], BF16); EBL = T("EBL", [128, 8], F32)
    EBT = T("EBT", [64, 512], F32); ENBT = T("ENBT", [64, 512], F32)
    QET = T("QET", [64, 512], BF16); KET = T("KET", [64, 512], BF16); ATM = T("ATM", [128, 512], BF16)
    SOWNB = T("SOWNB", [64, 4, 128], BF16)
    OG = T("OG", [128, 512], F32)
    SIG = T("SIG", [128, D], F32)
    CAT = T("CAT", [128, D], BF16); CATT = T("CATT", [128, NCH, 128], BF16); H1T = CATT
    QCN = T("QCN", [128, 256], BF16); QCNT = T("QCNT", [128, 2, 128], BF16)
    QNT = T("QNT", [64, 1024], BF16); QLT = T("QLT", [128, 1024], BF16)
    QRB = T("QRB", [128, 768], BF16); QRT = T("QRT", [96, 1024], BF16)
    NL2 = T("NL2", [128, 8], F32); NRQ = T("NRQ", [128, 8], F32); NEGM = T("NEGM", [128, 8], F32)
    R1 = T("R1", [1, 1], F32); RM = T("RM", [128, 1], F32)
    SM = T("SM", [128, 512], F32)
    PB = [T(f"PB{k}", [128, 512], BF16) for k in range(2)]
    PT = [T(f"PT{k}", [128, 512], BF16) for k in range(2)]
    LR = T("LR", [128, 8], F32)
    OLAT = T("OLAT", [128, 8, 128], BF16); OLT = T("OLT", [128, 1024], BF16); QSQ = OLT
    H1 = XO; H1B = JUNK; H2 = XO; YO = SIG
    PBF = T("PBF", [128, 256], BF16); PTT = T("PTT", [128, 2, 128], BF16)
    es_main = P.es
    es_p = contextlib.ExitStack(); P.es = es_p
    NGM = max(1, (NKB + 3) // 4); NCC = (NGM + 2) // 3
    CKVT = T("CKVT", [128, SEQL], BF16); KRT = T("KRT", [96, NCC * 512], BF16); CKVN = T("CKVN", [128, NKB, 128], BF16)
    kvb = [Buf(f"kv{j}") for j in range(NKB)]
    ST = [T(f"ST{k}", [64, 4, 128], F32) for k in range(3)]
    MK4 = T("MK4", [128, 512], BF16)
    XS1 = T("XS", [128, D], F32); XS = [XS1, XS1]
    CSS1 = T("CSS", [128, 64], F32); CSS = [CSS1, CSS1]
    LSUM = T("LSUM", [128, 8, NGM], F32)
    P.es = es_main
    ld("pool", MK4, mk4_d)
    P.dve(lambda e: e.memset(ST[0].t[:], 0.0), [], [ST[0].b])
    Z = type("Z", (), {})()

    def alloc_sample():
        Z.PGC = [T(f"PGC{k}", [128, 8, 128], F32) for k in range(2)]; Z.PGR = [T(f"PGR{k}", [128, 8, 32], F32) for k in range(2)]
        Z.PGCB = [T(f"PGCB{k}", [128, 8, 128], BF16) for k in range(2)]; Z.PGRB = [T(f"PGRB{k}", [128, 8, 32], BF16) for k in range(2)]
        Z.CTS = [T(f"CTS{k}", [128, 512], BF16) for k in range(2)]; Z.KTS = [T(f"KTS{k}", [32, 512], BF16) for k in range(2)]
        Z.SMALL = [[T(f"{nm}_{q}", [64, 1], F32) for nm in ("MRUN", "MNEW", "NMN", "ALP", "LRUN", "LG", "GMX")] for q in range(2)]
        Z.ACC2 = [T(f"ACC_{q}", [64, 128], F32) for q in range(2)]; Z.ACCB2 = [T(f"ACCB_{q}", [64, 128], BF16) for q in range(2)]
        Z.CKVT_S = T("CKVT_S", [128, 128], BF16); Z.KRT_S = T("KRT_S", [32, 128], BF16); Z.CKVN_S = T("CKVN_S", [128, 128], BF16)
        Z.MRUN = T("MRUN", [64, 1], F32); Z.MNEW = T("MNEW", [64, 1], F32); Z.NMN = T("NMN", [64, 1], F32)
        Z.ALP = T("ALP", [64, 1], F32); Z.LRUN = T("LRUN", [64, 1], F32); Z.LG = T("LG", [64, 1], F32); Z.GMX = T("GMX", [64, 1], F32)
        Z.ACC = T("ACC", [64, 128], F32); Z.ACCB = T("ACCB", [64, 128], BF16)
        s0 = T("S0F", [128, 8, 128], F32); Z.S0F = [s0, s0]; Z.S0B = T("S0B", [128, 8, 128], BF16)
        Z.L2 = T("L2", [128, 2, 64], F32); Z.EBS = T("EBS", [128, 8], F32)
        Z.QEX = T("QEX", [128, 8, 128], BF16); Z.KDX = T("KDX", [128, 16, 64], BF16)
        Z.QEN = T("QEN", [128, 2, 64], BF16)
        Z.SNEW = T("SNEW", [128, 8, 128], F32)
        Z.M2 = T("M2", [128, 1024], BF16); Z.DMASK = T("DMASK", [64, 2048], BF16)
        Z.PTL = T("PTL", [128, 16 * NG8], I32); Z.PM16 = T("PM16", [128, 1], I32); Z.IDX = T("IDX", [128, 16 * NG8], I32)
        ld("pool", Z.M2, m2_d); ld("pool", Z.DMASK, dmask_d); ld("sp", Z.PTL, ptl_d); ld("sp", Z.PM16, pm16_d)

    ident = IDB.t[:]

    def transposes(src, dst, nch, cp="act"):
        for c in range(nch):
            P.pe(lambda e, c=c: e.transpose(out=ptp.t[:, c * 128:(c + 1) * 128], in_=src.t[:, c * 128:(c + 1) * 128],
                                            identity=ident), [src.b, IDB.b], [ptp.b])
        f = (lambda e: e.activation(out=dst.t[:].rearrange("p c t -> p (c t)"), in_=ptp.t[:, 0:nch * 128], func=AF.Copy)) \
            if cp == "act" else (lambda e: e.tensor_copy(out=dst.t[:].rearrange("p c t -> p (c t)"), in_=ptp.t[:, 0:nch * 128]))
        (P.act if cp == "act" else P.dve)(f, [ptp.b], [dst.b])

    def linear(out_ps, xT, nch, W, c0, n):
        for c in range(nch):
            P.pe(lambda e, c=c: e.matmul(out_ps.t[:, 0:n], lhsT=xT.t[:, c, :], rhs=W.t[:, c, c0:c0 + n],
                                         start=(c == 0), stop=(c == nch - 1)), [xT.b, W.b, WB], [out_ps.b])

    def linearT(out_ap, out_b, W, c0, m, xT, nch):
        for c in range(nch):
            P.pe(lambda e, c=c: e.matmul(out_ap, lhsT=W.t[:, c, c0:c0 + m], rhs=xT.t[:, c, :],
                                         start=(c == 0), stop=(c == nch - 1)), [xT.b, W.b, WB], [out_b])

    def sumsq(src_ap, src_b, n, out_ap, out_b):
        P.dve(lambda e: e.scalar_tensor_tensor(out=JUNK.t[:, 0:n], in0=src_ap, scalar=1.0, in1=src_ap, op0=ALU.mult,
                                               op1=ALU.mult, accum_out=out_ap), [src_b], [JUNK.b, out_b])

    def powm(out_ap, out_b, in_ap, in_b, scale, bias, power, tmp_ap):
        P.act(lambda e: e.activation(out=tmp_ap, in_=in_ap, func=AF.Ln, bias=bias, scale=scale), [in_b], [out_b])
        P.act(lambda e: e.activation(out=out_ap, in_=tmp_ap, func=AF.Exp, scale=power), [out_b], [out_b])

    def rms_prep(xt):
        sumsq(xt.t[:], xt.b, D, SS.t[:], SS.b)
        powm(RS.t[:], RS.b, SS.t[:], SS.b, 1.0 / D, 1e-6, -0.5, RS.t[:])

    def front(xt):
        rms_prep(xt)
        P.dve(lambda e: e.scalar_tensor_tensor(out=XN.t[:], in0=xt.t[:], scalar=RS.t[:, 0:1], in1=G_mix.t[:],
                                               op0=ALU.mult, op1=ALU.mult), [xt.b, RS.b, G_mix.b, WB], [XN.b])
        transposes(XN, XNT, NCH)

    def rope(src_ap, src_b, nh, cst, outs):
        n = nh * 32
        cs = cst.t[:, 0:32].unsqueeze(1).to_broadcast([128, nh, 32])
        sa = cst.t[:, 32:48].unsqueeze(1).to_broadcast([128, nh, 16])
        sb_ = cst.t[:, 48:64].unsqueeze(1).to_broadcast([128, nh, 16])
        a = RT1.t[:, 0:n].rearrange("p (h r) -> p h r", r=32)
        b = RT2.t[:, 0:n].rearrange("p (h r) -> p h r", r=32)
        P.dve(lambda e: e.tensor_tensor(out=a, in0=src_ap, in1=cs, op=ALU.mult), [src_b, cst.b], [RT1.b])
        P.dve(lambda e: e.tensor_tensor(out=b[:, :, 0:16], in0=src_ap[:, :, 16:32], in1=sa, op=ALU.mult), [src_b, cst.b], [RT2.b])
        P.dve(lambda e: e.tensor_tensor(out=b[:, :, 16:32], in0=src_ap[:, :, 0:16], in1=sb_, op=ALU.mult), [src_b, cst.b], [RT2.b])
        for oap, ob in outs:
            P.dve(lambda e, oap=oap: e.tensor_tensor(out=oap, in0=a, in1=b, op=ALU.add), [RT1.b, RT2.b], [ob])

    def kv_part(cst, ckvt_ap, krt_ap, ckvn_ap, kvbuf, track_rmax, pg=0):
        P.act(lambda e: e.activation(out=CKF.t[:], in_=pz[0].t[:, 0:128], func=AF.Copy), [pz[0].b], [CKF.b])
        sumsq(CKF.t[:], CKF.b, 128, SS8.t[:, 0:1], SS8.b)
        powm(RS8.t[:, 0:1], RS8.b, SS8.t[:, 0:1], SS8.b, 1.0 / 128, 1e-6, -0.5, RS8.t[:, 0:1])
        P.dve(lambda e: e.scalar_tensor_tensor(out=CKF.t[:], in0=CKF.t[:], scalar=RS8.t[:, 0:1], in1=G_kv.t[:],
                                               op0=ALU.mult, op1=ALU.mult), [CKF.b, RS8.b, G_kv.b, WB], [CKF.b])
        P.act(lambda e: e.activation(out=CKB.t[:], in_=CKF.t[:], func=AF.Copy), [CKF.b], [CKB.b])
        P.act(lambda e: e.activation(out=ckvn_ap, in_=CKF.t[:], func=AF.Copy), [CKF.b], [kvbuf])
        rope(pz[0].t[:, 128:160].rearrange("p (h r) -> p h r", r=32), pz[0].b, 1, cst,
             [(KRF.t[:].rearrange("p (h r) -> p h r", r=32), KRF.b)] +
             [(KRB.t[:, 32 * q:32 * q + 32].rearrange("p (h r) -> p h r", r=32), KRB.b) for q in range(3)])
        if track_rmax:
            sumsq(KRF.t[:], KRF.b, 32, NR2.t[:], NR2.b)
            P.dve(lambda e: e.tensor_tensor(out=RMAX2.t[:], in0=RMAX2.t[:], in1=NR2.t[:], op=ALU.max), [RMAX2.b, NR2.b], [RMAX2.b])
        P.pe(lambda e: e.transpose(out=ptp.t[:, 0:128], in_=CKB.t[:], identity=ident), [CKB.b, IDB.b], [ptp.b])
        P.pe(lambda e: e.transpose(out=ptp.t[0:96, 128:256], in_=KRB.t[:], identity=ident), [KRB.b, IDB.b], [ptp.b])
        P.act(lambda e: e.activation(out=ckvt_ap, in_=ptp.t[:, 0:128], func=AF.Copy), [ptp.b], [kvbuf])
        P.act(lambda e: e.activation(out=krt_ap, in_=ptp.t[32 * pg:32 * pg + 32, 128:256], func=AF.Copy), [ptp.b], [kvbuf])

    def gla_nat(triL_ap):
        linearT(ps[0].t[0:16, 0:128], ps[0].b, W_in, C_A, 16, XNT, NCH)
        P.act(lambda e: e.activation(out=AGT.t[:], in_=ps[0].t[0:16, 0:128], func=AF.Copy), [ps[0].b], [AGT.b])
        P.pe(lambda e: e.matmul(ps[1].t[:, 0:256], lhsT=AGT.t[:], rhs=W_a2.t[:], start=True, stop=True),
             [AGT.b, W_a2.b, WB], [ps[1].b])
        P.dve(lambda e: e.tensor_tensor(out=T1.t[:], in0=ps[1].t[:, 0:256], in1=B_a.t[:], op=ALU.add), [ps[1].b, B_a.b, WB], [T1.b])
        P.act(lambda e: e.activation(out=T1.t[:], in_=T1.t[:], func=AF.Exp, scale=-1.0), [T1.b], [T1.b])
        P.act(lambda e: e.activation(out=LSB.t[:], in_=T1.t[:], func=AF.Ln, bias=1.0), [T1.b], [LSB.b])
        P.pe(lambda e: e.matmul(pa[0].t[:, 0:256], lhsT=triL_ap, rhs=LSB.t[:], start=True, stop=True), [GC.b, LSB.b, WB], [pa[0].b])
        P.act(lambda e: e.activation(out=ERB.t[:], in_=pa[0].t[:, 0:256], func=AF.Exp), [pa[0].b], [ERB.b])
        P.dve(lambda e: e.tensor_tensor(out=KD.t[:], in0=pz[0].t[:, C_KG:C_KG + 256], in1=ERB.t[:], op=ALU.mult), [pz[0].b, ERB.b], [KD.b])
        P.act(lambda e: e.activation(out=VB.t[:], in_=pz[1].t[:], func=AF.Copy), [pz[1].b], [VB.b])

    def load_x(q, xt, src, sb=None):
        P.dma(q, lambda e: e.dma_start(out=xt.t[:], in_=src), [], [xt.b], xt.b)

    def store(src_t, src_ap, dst):
        P.dma("pool", lambda e: e.dma_start(out=dst, in_=src_ap), [src_t.b], [], src_t.b, is_out=True)

    TRIU_P, TRIL_P, MASK_P = GC.t[:, 0:128], GC.t[:, 128:256], GC.t[:, 256:384]
    TRIU_S, TRIL_S, MASK_S = GC.t[:, 384:512], GC.t[:, 512:640], GC.t[:, 640:768]

    def shared(blk):
        xt = XS[blk % 2]; cst = CSS[blk % 2]
        load_x("sp", xt, x_all[blk * 128:(blk + 1) * 128, :])
        load_x("sp", cst, cs_all[blk * 128:(blk + 1) * 128, :])
        front(xt)
        linear(pz[0], XNT, NCH, W_in, 0, 416)
        linear(pz[1], XNT, NCH, W_in, C_VG, 512)
        g_ = blk // 4; pg = g_ % 3; kc = (g_ // 3) * 512 + (blk % 4) * 128
        kv_part(cst, CKVT.t[:, blk * 128:(blk + 1) * 128], KRT.t[32 * pg:32 * pg + 32, kc:kc + 128], CKVN.t[:, blk, :], kvb[blk], True, pg)
        store(CKF, CKF.t[:], ckv_o[blk * 128:(blk + 1) * 128, :])
        store(KRF, KRF.t[:], kr_o[blk * 128:(blk + 1) * 128, :])
        gla_nat(TRIL_P)
        for h in range(4):
            P.pe(lambda e, h=h: e.matmul(pa[0].t[0:64, 256 + h:257 + h], lhsT=LSB.t[:, h * 64:(h + 1) * 64], rhs=TRIU_P[:, 127:128],
                                         start=True, stop=True), [LSB.b, GC.b, WB], [pa[0].b])
        P.act(lambda e: e.activation(out=EBL.t[0:64, 0:4], in_=pa[0].t[0:64, 256:260], func=AF.Exp), [pa[0].b], [EBL.b])
        sin, sout = ST[blk % 3], ST[(blk + 1) % 3]
        for h in range(4):
            P.pe(lambda e, h=h: e.matmul(pa[1].t[0:64, h * 128:(h + 1) * 128], lhsT=KD.t[:, h * 64:(h + 1) * 64],
                                         rhs=VB.t[:, h * 128:(h + 1) * 128], start=True, stop=True), [KD.b, VB.b], [pa[1].b])
        for h in range(4):
            P.dve(lambda e, h=h: e.scalar_tensor_tensor(out=sout.t[:, h, :], in0=sin.t[:, h, :], scalar=EBL.t[0:64, h:h + 1],
                                                        in1=pa[1].t[0:64, h * 128:(h + 1) * 128], op0=ALU.mult, op1=ALU.add),
                  [sin.b, EBL.b, pa[1].b], [sout.b])

    def silu_to(dst_ap, dst_b, gate_ps, o_ap, o_bufs, n):
        P.act(lambda e: e.activation(out=SIG.t[:, 0:n], in_=gate_ps.t[:, 0:n], func=AF.Exp, scale=-1.0), [gate_ps.b], [SIG.b])
        P.dve(lambda e: e.tensor_scalar(out=SIG.t[:, 0:n], in0=SIG.t[:, 0:n], scalar1=1.0, scalar2=None, op0=ALU.add), [SIG.b], [SIG.b])
        P.dve(lambda e: e.reciprocal(out=SIG.t[:, 0:n], in_=SIG.t[:, 0:n]), [SIG.b], [SIG.b])
        P.dve(lambda e: e.tensor_tensor(out=SIG.t[:, 0:n], in0=gate_ps.t[:, 0:n], in1=SIG.t[:, 0:n], op=ALU.mult), [gate_ps.b, SIG.b], [SIG.b])
        P.dve(lambda e: e.tensor_tensor(out=dst_ap, in0=o_ap, in1=SIG.t[:, 0:n], op=ALU.mult), o_bufs + [SIG.b], [dst_b])

    def own(i, smp):
        if smp:
            xsrc, psrc, cssrc, ydst = x_smp, p_smp, cs_smp, y_smp
        else:
            sl = slice(i * 128, (i + 1) * 128)
            xsrc, psrc, cssrc, ydst = x_own[sl, :], p_own[sl, :], cs_own[sl, :], y_own[sl, :]
        load_x("sp", XO, xsrc); load_x("sp", PIN, psrc); load_x("sp", CSO, cssrc)
        front(XO)
        triU, triL, maskT = (TRIU_S, TRIL_S, MASK_S) if smp else (TRIU_P, TRIL_P, MASK_P)
        linear(pz[0], XNT, NCH, W_in, 0, 416)
        linear(pz[1], XNT, NCH, W_in, C_VG, 512)
        if smp:
            kv_part(CSO, Z.CKVT_S.t[:], Z.KRT_S.t[:], Z.CKVN_S.t[:], Z.CKVN_S.b, False)
            store(CKF, CKF.t[:], ckvs_o); store(KRF, KRF.t[:], krs_o)
        gla_nat(triL)
        for h in range(4):
            P.pe(lambda e, h=h: e.matmul(pa[1].t[0:64, h * 128:(h + 1) * 128], lhsT=LSB.t[:, h * 64:(h + 1) * 64], rhs=triU,
                                         start=True, stop=True), [LSB.b, GC.b, WB], [pa[1].b])
        P.act(lambda e: e.activation(out=EBT.t[:], in_=pa[1].t[0:64, :], func=AF.Exp), [pa[1].b], [EBT.b])
        P.act(lambda e: e.activation(out=ENBT.t[:], in_=pa[1].t[0:64, :], func=AF.Exp, scale=-1.0), [pa[1].b], [ENBT.b])
        for h in range(4):
            linearT(ps[0].t[0:64, h * 128:(h + 1) * 128], ps[0].b, W_in, C_QG + h * 64, 64, XNT, NCH)
        for h in range(4):
            linearT(ps[1].t[0:64, h * 128:(h + 1) * 128], ps[1].b, W_in, C_KG + h * 64, 64, XNT, NCH)
        P.dve(lambda e: e.scalar_tensor_tensor(out=QET.t[:], in0=ps[0].t[0:64, :], scalar=0.125, in1=EBT.t[:], op0=ALU.mult,
                                               op1=ALU.mult), [ps[0].b, EBT.b], [QET.b])
        P.dve(lambda e: e.tensor_tensor(out=KET.t[:], in0=ps[1].t[0:64, :], in1=ENBT.t[:], op=ALU.mult), [ps[1].b, ENBT.b], [KET.b])
        for h in range(4):
            P.pe(lambda e, h=h: e.matmul(pa[0].t[:, h * 128:(h + 1) * 128], lhsT=KET.t[:, h * 128:(h + 1) * 128],
                                         rhs=QET.t[:, h * 128:(h + 1) * 128], start=True, stop=True), [KET.b, QET.b], [pa[0].b])
        P.dve(lambda e: e.tensor_tensor(out=ATM.t[:].rearrange("p (h t) -> p h t", h=4), in0=pa[0].t[:].rearrange("p (h t) -> p h t", h=4),
                                        in1=maskT.unsqueeze(1).to_broadcast([128, 4, 128]), op=ALU.mult), [pa[0].b, GC.b], [ATM.b])
        if not smp:
            sa, sb_ = ST[(2 * i) % 3], ST[(2 * i + 1) % 3]
            sown = OG.t[0:64, :].rearrange("k (h v) -> k h v", h=4)
            P.dve(lambda e: e.tensor_scalar(out=sown, in0=sa.t[:], scalar1=PAR.t[0:64, 0:1], scalar2=None, op0=ALU.mult),
                  [sa.b, PAR.b, WB], [OG.b])
            P.dve(lambda e: e.scalar_tensor_tensor(out=SOWNB.t[:], in0=sb_.t[:], scalar=PAR.t[0:64, 1:2], in1=sown,
                                                   op0=ALU.mult, op1=ALU.add), [sb_.b, PAR.b, OG.b], [SOWNB.b])
            for h in range(4):
                P.pe(lambda e, h=h: e.matmul(ps[0].t[:, h * 128:(h + 1) * 128], lhsT=ATM.t[:, h * 128:(h + 1) * 128],
                                             rhs=VB.t[:, h * 128:(h + 1) * 128], start=True, stop=False), [ATM.b, VB.b], [ps[0].b])
                P.pe(lambda e, h=h: e.matmul(ps[0].t[:, h * 128:(h + 1) * 128], lhsT=QET.t[:, h * 128:(h + 1) * 128],
                                             rhs=SOWNB.t[:, h, :], start=False, stop=True), [QET.b, SOWNB.b], [ps[0].b])
        else:
            parm = SX.t[:, 0:2]; msel = SX.t[:, 2:10]; kdm = SX.t[:, 10:26]
            for h in range(4):
                s0f = Z.S0F[h % 2]
                for il in range(2):
                    P.dma("sp", lambda e, h=h, il=il, s0f=s0f: e.dma_start(
                        out=s0f.t[il * 64:(il + 1) * 64, :, :],
                        in_=st_in[:, h, :, :].rearrange("(p il) k v -> il k p v", il=2)[il]), [], [s0f.b], s0f.b)
                P.act(lambda e, s0f=s0f: e.activation(out=Z.S0B.t[:].rearrange("p a v -> p (a v)"), in_=s0f.t[:].rearrange("p a v -> p (a v)"),
                                                      func=AF.Copy), [s0f.b], [Z.S0B.b])
                P.pe(lambda e, h=h: e.transpose(out=ppt.t[:, 0:64], in_=QET.t[:, h * 128:(h + 1) * 128], identity=IDB.t[0:64, 0:64]),
                     [QET.b, IDB.b], [ppt.b])
                P.dve(lambda e: e.tensor_copy(out=Z.QEN.t[:], in_=ppt.t[:, 0:64].unsqueeze(1).to_broadcast([128, 2, 64])), [ppt.b], [Z.QEN.b])
                P.pe(lambda e: e.transpose(out=ppt.t[:, 128:256], in_=Z.QEN.t[:].rearrange("p a k -> p (a k)"), identity=ident),
                     [Z.QEN.b, IDB.b], [ppt.b])
                P.dve(lambda e: e.tensor_tensor(out=Z.QEX.t[:], in0=ppt.t[:, 128:256].unsqueeze(1).to_broadcast([128, 8, 128]),
                                                in1=Z.M2.t[:].rearrange("p (a t) -> p a t", a=8), op=ALU.mult), [ppt.b, Z.M2.b], [Z.QEX.b])
                P.pe(lambda e, h=h: e.matmul(ps[0].t[:, h * 128:(h + 1) * 128], lhsT=ATM.t[:, h * 128:(h + 1) * 128],
                                             rhs=VB.t[:, h * 128:(h + 1) * 128], start=True, stop=False), [ATM.b, VB.b], [ps[0].b])
                for p in range(8):
                    P.pe(lambda e, h=h, p=p: e.matmul(ps[0].t[:, h * 128:(h + 1) * 128], lhsT=Z.QEX.t[:, p, :], rhs=Z.S0B.t[:, p, :],
                                                      start=False, stop=(p == 7)), [Z.QEX.b, Z.S0B.b], [ps[0].b])
                P.dve(lambda e, h=h: e.tensor_tensor(out=Z.L2.t[:], in0=LSB.t[:, h * 64:(h + 1) * 64].unsqueeze(1).to_broadcast([128, 2, 64]),
                                                     in1=parm.unsqueeze(2).to_broadcast([128, 2, 64]), op=ALU.mult), [LSB.b, SX.b], [Z.L2.b])
                P.pe(lambda e: e.matmul(pa[1].t[:, 0:8], lhsT=Z.L2.t[:].rearrange("p a k -> p (a k)"), rhs=msel, start=True, stop=True),
                     [Z.L2.b, SX.b], [pa[1].b])
                P.act(lambda e: e.activation(out=Z.EBS.t[:], in_=pa[1].t[:, 0:8], func=AF.Exp), [pa[1].b], [Z.EBS.b])
                P.dve(lambda e, h=h: e.tensor_tensor(out=Z.KDX.t[:], in0=KD.t[:, h * 64:(h + 1) * 64].unsqueeze(1).to_broadcast([128, 16, 64]),
                                                     in1=kdm.unsqueeze(2).to_broadcast([128, 16, 64]), op=ALU.mult), [KD.b, SX.b], [Z.KDX.b])
                for half in range(2):
                    for p4 in range(4):
                        p = half * 4 + p4
                        P.pe(lambda e, h=h, p=p, p4=p4: e.matmul(pa[0].t[:, p4 * 128:(p4 + 1) * 128],
                                                                 lhsT=Z.KDX.t[:, 2 * p:2 * p + 2, :].rearrange("t a k -> t (a k)"),
                                                                 rhs=VB.t[:, h * 128:(h + 1) * 128], start=True, stop=True),
                             [Z.KDX.b, VB.b], [pa[0].b])
                    for p4 in range(4):
                        p = half * 4 + p4
                        P.dve(lambda e, p=p, p4=p4, s0f=s0f: e.scalar_tensor_tensor(
                            out=Z.SNEW.t[:, p, :], in0=s0f.t[:, p, :], scalar=Z.EBS.t[:, p:p + 1], in1=pa[0].t[:, p4 * 128:(p4 + 1) * 128],
                            op0=ALU.mult, op1=ALU.add), [s0f.b, Z.EBS.b, pa[0].b], [Z.SNEW.b])
                for il in range(2):
                    P.dma("pool", lambda e, h=h, il=il: e.dma_start(
                        out=sts_o[:, h, :, :].rearrange("(p il) k v -> il k p v", il=2)[il],
                        in_=Z.SNEW.t[il * 64:(il + 1) * 64, :, :]), [Z.SNEW.b], [], Z.SNEW.b, is_out=True)
        P.act(lambda e: e.activation(out=OG.t[:], in_=ps[0].t[:], func=AF.Copy), [ps[0].b], [OG.b])
        for h in range(4):
            sumsq(OG.t[:, h * 128:(h + 1) * 128], OG.b, 128, SS8.t[:, h:h + 1], SS8.b)
        powm(RS8.t[:, 0:4], RS8.b, SS8.t[:, 0:4], SS8.b, 1.0 / 128, 1e-6, -0.5, RS8.t[:, 0:4])
        for h in range(4):
            P.dve(lambda e, h=h: e.scalar_tensor_tensor(out=OG.t[:, h * 128:(h + 1) * 128], in0=OG.t[:, h * 128:(h + 1) * 128],
                                                        scalar=RS8.t[:, h:h + 1], in1=G_on.t[:], op0=ALU.mult, op1=ALU.mult),
                  [OG.b, RS8.b, G_on.b, WB], [OG.b])
        linear(pz[0], XNT, NCH, W_in, C_GG, 512)
        silu_to(CAT.t[:, 512:1024], CAT.b, pz[0], OG.t[:], [OG.b], 512)

        linear(pz[1], XNT, NCH, W_in, C_QC, 256)
        P.act(lambda e: e.activation(out=RT1.t[:], in_=pz[1].t[:, 0:256], func=AF.Copy), [pz[1].b], [RT1.b])
        sumsq(RT1.t[:], RT1.b, 256, SS.t[:], SS.b)
        powm(RS.t[:], RS.b, SS.t[:], SS.b, 1.0 / 256, 1e-6, -0.5, RS.t[:])
        P.dve(lambda e: e.scalar_tensor_tensor(out=QCN.t[:], in0=RT1.t[:], scalar=RS.t[:, 0:1], in1=G_q.t[:],
                                               op0=ALU.mult, op1=ALU.mult), [RT1.b, RS.b, G_q.b, WB], [QCN.b])
        transposes(QCN, QCNT, 2)
        for h in range(8):
            linearT(ps[h // 4].t[0:64, (h % 4) * 128:(h % 4 + 1) * 128], ps[h // 4].b, W_qup, h * 64, 64, QCNT, 2)
        for k in range(2):
            P.act(lambda e, k=k: e.activation(out=QNT.t[:, k * 512:(k + 1) * 512], in_=ps[k].t[0:64, :], func=AF.Copy), [ps[k].b], [QNT.b])
        for h in range(8):
            P.pe(lambda e, h=h: e.matmul(pz[h // 4].t[:, (h % 4) * 128:(h % 4 + 1) * 128], lhsT=W_ukT.t[:, h * 128:(h + 1) * 128],
                                         rhs=QNT.t[:, h * 128:(h + 1) * 128], start=True, stop=True), [W_ukT.b, WB, QNT.b], [pz[h // 4].b])
        for k in range(2):
            if smp:
                o = QLT.t[:].rearrange("l (s h t) -> l h s t", s=16, h=8, t=8)[:, 4 * k:4 * k + 4, :, :]
                i_ = pz[k].t[:].rearrange("l (h s t) -> l h s t", h=4, s=16, t=8)
            else:
                o = QLT.t[:, k * 512:(k + 1) * 512]; i_ = pz[k].t[:]
            P.act(lambda e, o=o, i_=i_: e.activation(out=o, in_=i_, func=AF.Copy), [pz[k].b], [QLT.b])
        linear(ps[0], QCNT, 2, W_qup, 512, 256)
        rope(ps[0].t[:, 0:256].rearrange("p (h r) -> p h r", r=32), ps[0].b, 8, CSO,
             [(QRB.t[:].rearrange("p (h q r) -> p h q r", q=3, r=32)[:, :, q, :], QRB.b) for q in range(3)])
        for h in range(8):
            P.pe(lambda e, h=h: e.transpose(out=ptp.t[0:96, h * 128:(h + 1) * 128], in_=QRB.t[:, h * 96:(h + 1) * 96], identity=ident),
                 [QRB.b, IDB.b], [ptp.b])
        if smp:
            o = QRT.t[:].rearrange("l (s h t) -> l h s t", s=16, h=8, t=8)
            i_ = ptp.t[0:96, :].rearrange("l (h s t) -> l h s t", h=8, s=16, t=8)
        else:
            o = QRT.t[:]; i_ = ptp.t[0:96, :]
        P.act(lambda e: e.activation(out=o, in_=i_, func=AF.Copy), [ptp.b], [QRT.b])

        if not smp:
            P.dve(lambda e: e.tensor_tensor(out=QSQ.t[:], in0=QLT.t[:], in1=QLT.t[:], op=ALU.mult), [QLT.b], [QSQ.b])
            for h in range(8):
                P.pe(lambda e, h=h: e.matmul(pa[0].t[:, h:h + 1], lhsT=QSQ.t[:, h * 128:(h + 1) * 128], rhs=ONESB.t[:],
                                             start=True, stop=True), [QSQ.b, ONESB.b], [pa[0].b])
            qr0 = QRB.t[:].rearrange("p (h q r) -> p h q r", q=3, r=32)[:, :, 0, :]
            P.dve(lambda e: e.tensor_tensor(out=RT1.t[:].rearrange("p (h r) -> p h r", r=32), in0=qr0, in1=qr0, op=ALU.mult), [QRB.b], [RT1.b])
            P.dve(lambda e: e.tensor_reduce(out=NRQ.t[:], in_=RT1.t[:].rearrange("p (h r) -> p h r", r=32), axis=AX.X, op=ALU.add),
                  [RT1.b], [NRQ.b])
            P.pe(lambda e: e.transpose(out=pa[1].t[0:1, 0:128], in_=RMAX2.t[:], identity=IDF.t[:]), [RMAX2.b, IDF.b, WB], [pa[1].b])
            P.dve(lambda e: e.reduce_max(out=R1.t[:], in_=pa[1].t[0:1, 0:128], axis=AX.X), [pa[1].b], [R1.b])
            P.pe(lambda e: e.matmul(pa[1].t[:, 128:129], lhsT=ONESF.t[:], rhs=R1.t[:], start=True, stop=True), [ONESF.b, R1.b], [pa[1].b])
            powm(RM.t[:], RM.b, pa[1].t[:, 128:129], pa[1].b, 1.0, 1e-20, 0.5, RM.t[:])
            powm(NL2.t[:], NL2.b, pa[0].t[:, 0:8], pa[0].b, 1.0, 1e-20, 0.5, NL2.t[:])
            powm(NRQ.t[:], NRQ.b, NRQ.t[:], NRQ.b, 1.0, 1e-20, 0.5, NRQ.t[:])
            P.dve(lambda e: e.tensor_scalar(out=NL2.t[:], in0=NL2.t[:], scalar1=CMAX.t[:, 0:1], scalar2=None, op0=ALU.mult),
                  [NL2.b, CMAX.b], [NL2.b])
            P.dve(lambda e: e.scalar_tensor_tensor(out=NEGM.t[:], in0=NRQ.t[:], scalar=RM.t[:, 0:1], in1=NL2.t[:], op0=ALU.mult,
                                                   op1=ALU.add), [NRQ.b, RM.b, NL2.b], [NEGM.b])
            P.dve(lambda e: e.tensor_scalar(out=NEGM.t[:], in0=NEGM.t[:], scalar1=-SC, scalar2=None, op0=ALU.mult), [NEGM.b], [NEGM.b])
            nkb = 2 * i + 2
            ng = (nkb + 3) // 4
            steps = [(h, g) for h in range(8) for g in range(ng)]
            nst = len(steps)

            def geo(t):
                h, g = steps[t]
                nb = min(4, nkb - 4 * g)
                return h, g, nb, nb * 128

            def st_S(t):
                h, g, nb, Wd = geo(t)
                psx = ps[t % 2]; c0 = 4 * g * 128
                kbufs = [kvb[4 * g + j] for j in range(nb)]
                P.pe(lambda e: e.matmul(psx.t[:, 0:Wd], lhsT=QLT.t[:, h * 128:(h + 1) * 128], rhs=CKVT.t[:, c0:c0 + Wd],
                                        start=True, stop=False), [QLT.b] + kbufs, [psx.b])
                pg = g % 3; kc = (g // 3) * 512
                P.pe(lambda e: e.matmul(psx.t[:, 0:Wd], lhsT=QRT.t[32 * pg:32 * pg + 32, h * 128:(h + 1) * 128],
                                        rhs=KRT.t[32 * pg:32 * pg + 32, kc:kc + Wd], start=False, stop=True), [QRT.b] + kbufs, [psx.b])

            def st_X(t):
                h, g, nb, Wd = geo(t)
                psx = ps[t % 2]; pb = PB[t % 2]
                if g == ng - 1:
                    P.dve(lambda e: e.tensor_tensor(out=SM.t[:, 0:Wd], in0=psx.t[:, 0:Wd], in1=MK4.t[:, 512 - Wd:512], op=ALU.add),
                          [psx.b, MK4.b], [SM.b])
                    src_ap, src_b = SM.t[:, 0:Wd], SM.b
                else:
                    src_ap, src_b = psx.t[:, 0:Wd], psx.b
                P.act(lambda e: e.activation(out=pb.t[:, 0:Wd], in_=src_ap, func=AF.Exp, bias=NEGM.t[:, h:h + 1], scale=SC,
                                             accum_out=LSUM.t[:, h, g:g + 1]), [src_b, NEGM.b], [pb.b, LSUM.b])

            def st_T(t):
                h, g, nb, Wd = geo(t)
                pb = PB[t % 2]; pt = PT[t % 2]; pph = ppth[t % 2]
                for j in range(nb):
                    P.pe(lambda e, j=j: e.transpose(out=pph.t[:, j * 128:(j + 1) * 128], in_=pb.t[:, j * 128:(j + 1) * 128], identity=ident),
                         [pb.b, IDB.b], [pph.b])
                P.dve(lambda e: e.tensor_copy(out=pt.t[:, 0:Wd], in_=pph.t[:, 0:Wd]), [pph.b], [pt.b])

            def st_PV(t):
                h, g, nb, Wd = geo(t)
                pt = PT[t % 2]; bank = pa[h // 4]
                for j in range(nb):
                    first = (h % 4 == 0 and g == 0 and j == 0)
                    lastm = (h % 4 == 3 and g == ng - 1 and j == nb - 1)
                    P.pe(lambda e, j=j, first=first, lastm=lastm: e.matmul(
                        bank.t[:, (h % 4) * 128:(h % 4 + 1) * 128], lhsT=pt.t[:, j * 128:(j + 1) * 128], rhs=CKVN.t[:, 4 * g + j, :],
                        start=first, stop=lastm, skip_group_check=True), [pt.b, kvb[4 * g + j]], [bank.b])

            st_S(0)
            for t in range(nst):
                if t + 1 < nst:
                    st_S(t + 1)
                st_X(t)
                st_T(t)
                if t >= 1:
                    st_PV(t - 1)
            st_PV(nst - 1)
            P.dve(lambda e: e.tensor_reduce(out=LR.t[:], in_=LSUM.t[:, :, 0:ng], axis=AX.X, op=ALU.add), [LSUM.b], [LR.b])
            P.dve(lambda e: e.reciprocal(out=LR.t[:], in_=LR.t[:]), [LR.b], [LR.b])
            for h in range(8):
                P.dve(lambda e, h=h: e.tensor_scalar(out=OLAT.t[:, h, :], in0=pa[h // 4].t[:, (h % 4) * 128:(h % 4 + 1) * 128],
                                                     scalar1=LR.t[:, h:h + 1], scalar2=None, op0=ALU.mult), [pa[h // 4].b, LR.b], [OLAT.b])
            for h in range(8):
                P.pe(lambda e, h=h: e.transpose(out=ptp.t[:, h * 128:(h + 1) * 128], in_=OLAT.t[:, h, :], identity=ident),
                     [OLAT.b, IDB.b], [ptp.b])
            P.act(lambda e: e.activation(out=OLT.t[:], in_=ptp.t[:], func=AF.Copy), [ptp.b], [OLT.b])
        else:
            decode_attention()

        for h in range(8):
            P.pe(lambda e, h=h: e.matmul(pz[0].t[:, h * 64:(h + 1) * 64], lhsT=OLT.t[:, h * 128:(h + 1) * 128], rhs=W_uv.t[:, h * 64:(h + 1) * 64],
                                         start=True, stop=True), [OLT.b, W_uv.b, WB], [pz[0].b])
        P.act(lambda e: e.activation(out=OG.t[:], in_=pz[0].t[:], func=AF.Copy), [pz[0].b], [OG.b])
        linear(pz[1], XNT, NCH, W_in, C_GM, 512)
        silu_to(CAT.t[:, 0:512], CAT.b, pz[1], OG.t[:], [OG.b], 512)
        transposes(CAT, CATT, NCH)
        for k in range(2):
            linear(pz[k], CATT, NCH, W_out, k * 512, 512)
            P.dve(lambda e, k=k: e.tensor_tensor(out=H1.t[:, k * 512:(k + 1) * 512], in0=pz[k].t[:], in1=XO.t[:, k * 512:(k + 1) * 512],
                                                 op=ALU.add), [pz[k].b, XO.b], [H1.b])
        P.act(lambda e: e.activation(out=H1B.t[:], in_=H1.t[:], func=AF.Copy), [H1.b], [H1B.b])
        transposes(H1B, H1T, NCH)
        P.act(lambda e: e.activation(out=PBF.t[:], in_=PIN.t[:], func=AF.Copy), [PIN.b], [PBF.b])
        transposes(PBF, PTT, 2)
        for k in range(2):
            linear(pz[k], H1T, NCH, W_pg, k * 512, 512)
            sg = SIG.t[:, k * 512:(k + 1) * 512]
            P.act(lambda e, k=k, sg=sg: e.activation(out=sg, in_=pz[k].t[:], func=AF.Exp, scale=-1.0), [pz[k].b], [SIG.b])
            P.dve(lambda e, sg=sg: e.tensor_scalar(out=sg, in0=sg, scalar1=1.0, scalar2=None, op0=ALU.add), [SIG.b], [SIG.b])
            P.dve(lambda e, sg=sg: e.reciprocal(out=sg, in_=sg), [SIG.b], [SIG.b])
            linear(ps[k], PTT, 2, W_pp, k * 512, 512)
            P.dve(lambda e, k=k, sg=sg: e.tensor_tensor(out=sg, in0=ps[k].t[:], in1=sg, op=ALU.mult), [ps[k].b, SIG.b], [SIG.b])
            P.dve(lambda e, k=k, sg=sg: e.tensor_tensor(out=H2.t[:, k * 512:(k + 1) * 512], in0=H1.t[:, k * 512:(k + 1) * 512], in1=sg,
                                                        op=ALU.add), [H1.b, SIG.b], [H2.b])
        rms_prep(H2)
        P.dve(lambda e: e.scalar_tensor_tensor(out=YO.t[:], in0=H2.t[:], scalar=RS.t[:, 0:1], in1=G_fin.t[:], op0=ALU.mult,
                                               op1=ALU.mult), [H2.b, RS.b, G_fin.b, WB], [YO.b])
        store(YO, YO.t[:], ydst)

    def decode_attention():
        P.barrier()
        names = ("MRUN", "MNEW", "NMN", "ALP", "LRUN", "LG", "GMX")

        class St:
            pass

        def mk(q):
            st = St()
            for nm, tl in zip(names, Z.SMALL[q]):
                setattr(st, nm, tl)
            st.ACC, st.ACCB = Z.ACC2[q], Z.ACCB2[q]
            st.CTS, st.KTS = Z.CTS[q], Z.KTS[q]
            st.ps, st.pb, st.pt, st.pa = ps[q], PB[q], PT[q], pa[q]
            st.ppt = Tl(ppt.t[:, q * 256:(q + 1) * 256], ppt.b)
            st.pptA = Tl(ppt.t[:, 512 + q * 64:512 + (q + 1) * 64], ppt.b)
            st.sm = Tl(SM.t[0:64, q * 128:(q + 1) * 128], Buf(f"smS{q}"))
            st.k = q
            return st

        def stream(st, seqs):
            k = st.k

            def gather(s_, g8):
                col = s_ * NG8 + g8
                P.dma("pool", lambda e: e.indirect_dma_start(
                    out=Z.PGC[k].t[:].rearrange("p a l -> p (a l)"), out_offset=None, in_=pool_c,
                    in_offset=bass.IndirectOffsetOnAxis(ap=Z.IDX.t[:, col:col + 1], axis=0)), [Z.IDX.b], [Z.PGC[k].b], Z.PGC[k].b)
                P.dma("pool", lambda e: e.indirect_dma_start(
                    out=Z.PGR[k].t[:].rearrange("p a l -> p (a l)"), out_offset=None, in_=pool_r,
                    in_offset=bass.IndirectOffsetOnAxis(ap=Z.IDX.t[:, col:col + 1], axis=0)), [Z.IDX.b], [Z.PGR[k].b], Z.PGR[k].b)

            def stage_S(s_, cT_ap, cT_b, kT_ap, kT_b, Wd):
                psx = st.ps
                P.pe(lambda e: e.matmul(psx.t[0:64, 0:Wd], lhsT=QLT.t[:, s_ * 64:(s_ + 1) * 64], rhs=cT_ap, start=True, stop=False),
                     [QLT.b, cT_b], [psx.b])
                P.pe(lambda e: e.matmul(psx.t[0:64, 0:Wd], lhsT=QRT.t[0:32, s_ * 64:(s_ + 1) * 64], rhs=kT_ap, start=False, stop=True),
                     [QRT.b, kT_b], [psx.b])

            def stage_X(Wd, mask_ap):
                psx = st.ps
                if mask_ap is not None:
                    P.dve(lambda e: e.tensor_tensor(out=st.sm.t[:, 0:Wd], in0=psx.t[0:64, 0:Wd], in1=mask_ap, op=ALU.add),
                          [psx.b, Z.DMASK.b], [st.sm.b])
                    src_ap, src_b = st.sm.t[:, 0:Wd], st.sm.b
                else:
                    src_ap, src_b = psx.t[0:64, 0:Wd], psx.b
                P.dve(lambda e: e.reduce_max(out=st.GMX.t[:], in_=src_ap, axis=AX.X), [src_b], [st.GMX.b])
                P.dve(lambda e: e.tensor_tensor(out=st.MNEW.t[:], in0=st.MRUN.t[:], in1=st.GMX.t[:], op=ALU.max), [st.MRUN.b, st.GMX.b], [st.MNEW.b])
                P.dve(lambda e: e.tensor_tensor(out=st.ALP.t[:], in0=st.MRUN.t[:], in1=st.MNEW.t[:], op=ALU.subtract),
                      [st.MRUN.b, st.MNEW.b], [st.ALP.b])
                P.act(lambda e: e.activation(out=st.ALP.t[:], in_=st.ALP.t[:], func=AF.Exp, scale=SC), [st.ALP.b], [st.ALP.b])
                P.dve(lambda e: e.tensor_scalar(out=st.NMN.t[:], in0=st.MNEW.t[:], scalar1=-SC, scalar2=None, op0=ALU.mult), [st.MNEW.b], [st.NMN.b])
                P.dve(lambda e: e.tensor_copy(out=st.MRUN.t[:], in_=st.MNEW.t[:]), [st.MNEW.b, st.ALP.b], [st.MRUN.b])
                pb = st.pb
                P.act(lambda e: e.activation(out=pb.t[0:64, 0:Wd], in_=src_ap, func=AF.Exp, bias=st.NMN.t[:, 0:1], scale=SC,
                                             accum_out=st.LG.t[:, 0:1]), [src_b, st.NMN.b], [pb.b, st.LG.b])
                P.dve(lambda e: e.scalar_tensor_tensor(out=st.LRUN.t[:], in0=st.LRUN.t[:], scalar=st.ALP.t[:, 0:1], in1=st.LG.t[:], op0=ALU.mult,
                                                       op1=ALU.add), [st.LRUN.b, st.ALP.b, st.LG.b], [st.LRUN.b])

            def stage_PT(nb):
                pb = st.pb
                for j in range(nb):
                    P.pe(lambda e, j=j: e.transpose(out=st.ppt.t[:, j * 64:(j + 1) * 64], in_=pb.t[0:64, j * 128:(j + 1) * 128],
                                                    identity=IDB.t[0:64, 0:64]), [pb.b, IDB.b], [st.ppt.b])
                P.dve(lambda e: e.tensor_copy(out=st.pt.t[:, 0:nb * 64], in_=st.ppt.t[:, 0:nb * 64]), [st.ppt.b], [st.pt.b])

            def stage_PV(nb, vsrc):
                for j in range(nb):
                    vap, vb = vsrc(j)
                    P.pe(lambda e, j=j, vap=vap: e.matmul(st.pa.t[0:64, 0:128], lhsT=st.pt.t[:, j * 64:(j + 1) * 64], rhs=vap,
                                                          start=(j == 0), stop=(j == nb - 1)), [st.pt.b, vb], [st.pa.b])
                P.dve(lambda e: e.scalar_tensor_tensor(out=st.ACC.t[:], in0=st.ACC.t[:], scalar=st.ALP.t[:, 0:1], in1=st.pa.t[0:64, 0:128],
                                                       op0=ALU.mult, op1=ALU.add), [st.ACC.b, st.ALP.b, st.pa.b], [st.ACC.b])

            gather(seqs[0], 0)
            for si, s_ in enumerate(seqs):
                P.dve(lambda e: e.memset(st.MRUN.t[:], -1.0e30), [], [st.MRUN.b])
                P.dve(lambda e: e.memset(st.LRUN.t[:], 0.0), [], [st.LRUN.b])
                P.dve(lambda e: e.memset(st.ACC.t[:], 0.0), [], [st.ACC.b])
                for g8 in range(NG8):
                    P.act(lambda e: e.activation(out=Z.PGCB[k].t[:].rearrange("p a l -> p (a l)"), in_=Z.PGC[k].t[:].rearrange("p a l -> p (a l)"),
                                                 func=AF.Copy), [Z.PGC[k].b], [Z.PGCB[k].b])
                    P.dve(lambda e: e.tensor_copy(out=Z.PGRB[k].t[:].rearrange("p a l -> p (a l)"), in_=Z.PGR[k].t[:].rearrange("p a l -> p (a l)")),
                          [Z.PGR[k].b], [Z.PGRB[k].b])
                    if g8 + 1 < NG8:
                        gather(s_, g8 + 1)
                    elif si + 1 < len(seqs):
                        gather(seqs[si + 1], 0)
                    yield
                    for half in range(2):
                        for j in range(4):
                            a_ = half * 4 + j
                            P.pe(lambda e, a_=a_, j=j: e.transpose(out=ptp.t[:, j * 128:(j + 1) * 128], in_=Z.PGCB[k].t[:, a_, :], identity=ident),
                                 [Z.PGCB[k].b, IDB.b], [ptp.b])
                            P.pe(lambda e, a_=a_, j=j: e.transpose(out=ptp.t[0:32, 512 + j * 128:512 + (j + 1) * 128], in_=Z.PGRB[k].t[:, a_, :],
                                                                   identity=ident), [Z.PGRB[k].b, IDB.b], [ptp.b])
                        P.act(lambda e: e.activation(out=st.CTS.t[:], in_=ptp.t[:, 0:512], func=AF.Copy), [ptp.b], [st.CTS.b])
                        P.act(lambda e: e.activation(out=st.KTS.t[:], in_=ptp.t[0:32, 512:1024], func=AF.Copy), [ptp.b], [st.KTS.b])
                        yield
                        stage_S(s_, st.CTS.t[:], st.CTS.b, st.KTS.t[:], st.KTS.b, 512)
                        yield
                        stage_X(512, None)
                        yield
                        stage_PT(4)
                        yield
                        stage_PV(4, lambda j, half=half: (Z.PGCB[k].t[:, half * 4 + j, :], Z.PGCB[k].b))
                        yield
                stage_S(s_, Z.CKVT_S.t[:], Z.CKVN_S.b, Z.KRT_S.t[:], Z.CKVN_S.b, 128)
                yield
                stage_X(128, Z.DMASK.t[:, s_ * 128:(s_ + 1) * 128])
                yield
                stage_PT(1)
                yield
                stage_PV(1, lambda j: (Z.CKVN_S.t[:], Z.CKVN_S.b))
                P.dve(lambda e: e.reciprocal(out=st.LG.t[:], in_=st.LRUN.t[:]), [st.LRUN.b], [st.LG.b])
                P.dve(lambda e: e.tensor_scalar(out=st.ACCB.t[:], in0=st.ACC.t[:], scalar1=st.LG.t[:, 0:1], scalar2=None, op0=ALU.mult),
                      [st.ACC.b, st.LG.b], [st.ACCB.b])
                P.pe(lambda e: e.transpose(out=st.pptA.t[:], in_=st.ACCB.t[:], identity=IDB.t[0:64, 0:64]), [st.ACCB.b, IDB.b], [st.pptA.b])
                P.dve(lambda e, s_=s_: e.tensor_copy(out=OLT.t[:].rearrange("l (h s t) -> l h s t", h=8, s=16, t=8)[:, :, s_, :],
                                                     in_=st.pptA.t[:].rearrange("l (h t) -> l h t", h=8)), [st.pptA.b], [OLT.b])
                yield

        gens = [stream(mk(0), list(range(0, 8))), stream(mk(1), list(range(8, 16)))]
        alive = [True, True]
        for _ in range(3):
            next(gens[0])
        while any(alive):
            for q in range(2):
                if alive[q]:
                    try:
                        next(gens[q])
                    except StopIteration:
                        alive[q] = False

    P.barrier()
    init_consts()
    for i in range(NS):
        shared(2 * i)
        shared(2 * i + 1)
        own(i, False)
    fs = ST[NKB % 3]
    P.dma("pool", lambda e: e.dma_start(out=stp_o.rearrange("h k v -> k h v"), in_=fs.t[:]), [fs.b], [], fs.b, is_out=True)
    P.barrier()
    es_p.close()
    alloc_sample()
    P.barrier()
    P.dve(lambda e: e.tensor_scalar(out=Z.IDX.t[:], in0=Z.PTL.t[:], scalar1=16.0, scalar2=Z.PM16.t[:, 0:1],
                                    op0=ALU.mult, op1=ALU.add), [Z.PTL.b, Z.PM16.b], [Z.IDX.b])
    own(0, True)

    P.finalize()
    with nc.Block() as block:
        P.emit(block)
    es.close()
    return nc


def _consts():
    import ml_dtypes
    t = np.arange(128)
    c = {}
    su = (t[:, None] <= t[None, :])
    sq = (t[:, None] // 8 == t[None, :] // 8)
    g = np.zeros((128, 768), np.float32)
    g[:, 0:128] = np.where(su, -1.0 / 16, 0.0)
    g[:, 128:256] = np.where(~su, -1.0 / 16, 0.0)
    g[:, 256:384] = su
    g[:, 384:512] = np.where(su & sq, -1.0 / 16, 0.0)
    g[:, 512:640] = np.where((~su) & sq, -1.0 / 16, 0.0)
    g[:, 640:768] = su & sq
    c["gla_c"] = g
    seq = t // 8
    sx = np.zeros((128, 26), np.float32)
    sx[:, 0] = (seq % 2 == 0); sx[:, 1] = (seq % 2 == 1)
    for p in range(8):
        sx[:, 2 + p] = np.where(seq // 2 == p, -1.0 / 16, 0.0)
    for i in range(16):
        sx[:, 10 + i] = (seq == i)
    c["smp_x"] = sx
    m2 = np.zeros((128, 8, 128), np.float32)
    for il in range(2):
        for p in range(8):
            m2[il * 64:(il + 1) * 64, p, :] = (seq == 2 * p + il)[None, :]
    c["m2"] = m2.reshape(128, 1024)
    dm = np.full((64, 16, 128), NEG, np.float32)
    for s in range(16):
        for tq in range(8):
            for h in range(8):
                dm[h * 8 + tq, s, 8 * s:8 * s + tq + 1] = 0.0
    c["dmask"] = dm.reshape(64, 2048)
    c["ident_b"] = np.eye(128, dtype=np.float32)
    c["ident_f"] = np.eye(128, dtype=np.float32)
    c["pm16"] = (t % 16).astype(np.int32).reshape(128, 1)
    return c


def _cs_table(pos):
    inv = (1.0 / (np.float32(10000.0) ** (np.arange(0, 32, 2, dtype=np.float32) / np.float32(32)))).astype(np.float32)
    ang = (pos.astype(np.float32)[:, None] * inv[None, :]).astype(np.float32)
    co, si = np.cos(ang).astype(np.float32), np.sin(ang).astype(np.float32)
    return np.concatenate([co, co, -si, si], axis=1).astype(np.float32)


_NC_CACHE = {}


def kernel(x_prompt, x_sample, p_prompt, p_sample, cache_ckv, cache_krope, state_gla, page_table,
           g_mix_norm, w_in, g_qnorm, w_qup, g_kvnorm, w_uk, w_uv, w_gla_a2, b_gla_a, g_gla_onorm,
           w_out, w_ple_gate, w_ple_proj, g_final):
    f = lambda a: np.ascontiguousarray(np.asarray(a))
    x_prompt, x_sample, p_prompt, p_sample = f(x_prompt), f(x_sample), f(p_prompt), f(p_sample)
    cache_ckv, cache_krope, state_gla, page_table = f(cache_ckv), f(cache_krope), f(state_gla), f(page_table)
    B, S, _ = x_prompt.shape
    BD, TD, _ = x_sample.shape
    NPG = page_table.shape[1]
    NPOOL = cache_ckv.shape[1]
    n_cores = 2 * B
    assert BD == 16 * n_cores and TD == 8 and NPG % 8 == 0 and S % 256 == 0
    NS, NG8 = S // 256, NPG // 8
    past = NPG * cache_ckv.shape[2]
    key = (NS, NG8, NPOOL)
    if key not in _NC_CACHE:
        _NC_CACHE[key] = build(NS, NG8, NPOOL)
    nc = _NC_CACHE[key]

    w_in0 = f(w_in)[0]
    bnd = np.cumsum([0, 256, 160, 512, 256, 256, 512, 16, 512])
    qc, kv, gm, qg, kg, vg, ag, gg = [w_in0[:, bnd[j]:bnd[j + 1]] for j in range(8)]
    w_in_r = f(np.concatenate([kv, kg, vg, ag, qc, gm, qg, gg], axis=1))
    wq = f(w_qup)[0].reshape(256, 8, 96)
    w_qup_r = f(np.concatenate([wq[:, :, :64].reshape(256, 512), wq[:, :, 64:].reshape(256, 256)], axis=1))
    w_ukT = f(np.transpose(f(w_uk)[0], (2, 1, 0)).reshape(64, 1024))
    w_uv2 = f(f(w_uv)[0].reshape(128, 512))
    common = dict(w_in_r=w_in_r, w_qup_r=w_qup_r, w_ukT=w_ukT, w_uv=w_uv2, w_a2=f(w_gla_a2)[0], w_out=f(w_out)[0],
                  w_pg=f(w_ple_gate)[0], w_pp=f(w_ple_proj)[0], g_mix=f(g_mix_norm), g_q=f(g_qnorm), g_kv=f(g_kvnorm),
                  g_on=f(g_gla_onorm), b_a=f(b_gla_a), g_fin=f(g_final).reshape(1, -1),
                  pool_c=cache_ckv[0].reshape(NPOOL * 16, 1024), pool_r=cache_krope[0].reshape(NPOOL * 16, 256))
    common.update(_consts())
    cs_all = _cs_table(np.arange(S))
    cs_smp = _cs_table(past + (np.arange(128) % 8))
    tri = np.where(np.arange(128)[:, None] >= np.arange(128)[None, :], 0.0, NEG).astype(np.float32)
    in_maps = []
    for c in range(n_cores):
        b, r = c // 2, c % 2
        xb = x_prompt[b].reshape(NS, 2, 128, D)
        m = dict(common)
        m["x_all"] = x_prompt[b]
        m["x_own"] = f(xb[:, r].reshape(NS * 128, D))
        m["p_own"] = f(p_prompt[0, b].reshape(NS, 2, 128, 256)[:, r].reshape(NS * 128, 256))
        m["cs_all"] = cs_all
        m["cs_own"] = f(cs_all.reshape(NS, 2, 128, 64)[:, r].reshape(NS * 128, 64))
        m["x_smp"] = f(x_sample[16 * c:16 * c + 16].reshape(128, D))
        m["p_smp"] = f(p_sample[0, 16 * c:16 * c + 16].reshape(128, 256))
        m["cs_smp"] = cs_smp
        mk4 = np.zeros((128, 512), np.float32)
        if r == 0:
            mk4[:, 256:384] = tri; mk4[:, 384:512] = NEG
        else:
            mk4[:, 384:512] = tri
        m["mk4"] = mk4
        par = np.zeros((128, 2), np.float32); par[:, 0] = 1 - r; par[:, 1] = r
        m["par"] = par
        pt = page_table[16 * c:16 * c + 16].reshape(16, NG8, 8)
        ptl = np.repeat(np.transpose(pt, (2, 0, 1)).reshape(8, 16 * NG8), 16, axis=0)
        m["ptl"] = f(ptl.astype(np.int32))
        m["st_in"] = f(state_gla[0, 16 * c:16 * c + 16])
        in_maps.append(m)
    res = run_bass_kernel_spmd(nc, in_maps, core_ids=list(range(n_cores)))
    R = res.results
    y_p = np.zeros((B, S, D), np.float32)
    ckv_p = np.zeros((1, B, S, 128), np.float32); kr_p = np.zeros((1, B, S, 32), np.float32)
    st_p = np.zeros((1, B, 4, 64, 128), np.float32)
    y_s = np.zeros((BD, TD, D), np.float32); ckv_s = np.zeros((1, BD, TD, 128), np.float32)
    kr_s = np.zeros((1, BD, TD, 32), np.float32); st_s = np.zeros((1, BD, 4, 64, 128), np.float32)
    for c in range(n_cores):
        b, r = c // 2, c % 2
        y_p[b].reshape(NS, 2, 128, D)[:, r] = R[c]["y_own"].reshape(NS, 128, D)
        if r == 0:
            ckv_p[0, b] = R[c]["ckv_o"]; kr_p[0, b] = R[c]["kr_o"]; st_p[0, b] = R[c]["stp_o"]
        y_s[16 * c:16 * c + 16] = R[c]["y_smp"].reshape(16, 8, D)
        ckv_s[0, 16 * c:16 * c + 16] = R[c]["ckvs_o"].reshape(16, 8, 128)
        kr_s[0, 16 * c:16 * c + 16] = R[c]["krs_o"].reshape(16, 8, 32)
        st_s[0, 16 * c:16 * c + 16] = R[c]["sts_o"]
    return (y_p, y_s, ckv_p, kr_p, st_p, ckv_s, kr_s, st_s)
```

```python
import contextlib
import numpy as np
import concourse.bass as bass
import concourse.mybir as mybir
from concourse.bass_utils import run_bass_kernel_spmd

F32 = mybir.dt.float32
BF16 = mybir.dt.bfloat16
I32 = mybir.dt.int32
AF = mybir.ActivationFunctionType
ALU = mybir.AluOpType
AX = mybir.AxisListType

D = 1024
NCH = 8
SC = 96.0 ** -0.5
NEG = -30000.0
C_KV, C_KG, C_VG, C_A, C_QC, C_GM, C_QG, C_GG = 0, 160, 416, 928, 944, 1200, 1712, 1968


class Buf:
    __slots__ = ("name", "last_w", "readers", "dsem", "dcount")

    def __init__(self, name):
        self.name = name
        self.last_w = None
        self.readers = []
        self.dsem = None
        self.dcount = 0


class Op:
    __slots__ = ("eng", "fn", "deps", "is_dma", "done", "needed", "waits", "idx")

    def __init__(self, eng, fn, is_dma):
        self.eng = eng
        self.fn = fn
        self.deps = []
        self.is_dma = is_dma
        self.done = None
        self.needed = False
        self.waits = []


class Tl:
    __slots__ = ("t", "b")

    def __init__(self, t, b):
        self.t = t
        self.b = b


class Prog:
    ENGS = ("pe", "act", "dve", "pool", "sp")
    EPOCH = 12000

    def __init__(self, nc, es):
        self.nc = nc
        self.es = es
        self.ops = {e: [] for e in self.ENGS}
        self.order = []
        self.out_dmas = []

    def sem(self, name):
        return self.es.enter_context(self.nc.semaphore(name))

    def tile(self, name, shape, dtype, psum=False):
        if psum:
            t = self.es.enter_context(self.nc.psum_tensor(name, shape, dtype))
        else:
            t = self.es.enter_context(self.nc.sbuf_tensor(name, shape, dtype))
        return Tl(t, Buf(name))

    def _add(self, eng, fn, reads, writes, is_dma, dsem_buf=None):
        op = Op(eng, fn, is_dma)
        deps = []
        for b in reads:
            if b.last_w is not None:
                deps.append((b.last_w, "raw"))
        for b in writes:
            if b.last_w is not None:
                deps.append((b.last_w, "waw"))
            for r in b.readers:
                deps.append((r, "war"))
        for d, kind in deps:
            if d is op:
                continue
            same = (d.eng == eng) and (not d.is_dma) and (not is_dma)
            if same and (kind != "raw" or eng == "pe"):
                continue
            op.deps.append(d)
            d.needed = True
        for b in reads:
            b.readers.append(op)
        for b in writes:
            b.last_w = op
            b.readers = []
        if is_dma:
            if dsem_buf.dsem is None:
                dsem_buf.dsem = self.sem("d_" + dsem_buf.name)
            dsem_buf.dcount += 16
            op.done = (dsem_buf.dsem, dsem_buf.dcount)
        self.ops[eng].append(op)
        self.order.append(op)
        return op

    def pe(self, fn, reads, writes):
        return self._add("pe", fn, reads, writes, False)

    def act(self, fn, reads, writes):
        return self._add("act", fn, reads, writes, False)

    def dve(self, fn, reads, writes):
        return self._add("dve", fn, reads, writes, False)

    def pool(self, fn, reads, writes):
        return self._add("pool", fn, reads, writes, False)

    def dma(self, q, fn, reads, writes, sb, is_out=False):
        op = self._add(q, fn, reads, writes, True, sb)
        if is_out:
            self.out_dmas.append(op)
        return op

    def barrier(self):
        lasts = [self.ops[e][-1] for e in self.ENGS if self.ops[e]]
        dmas = [o for o in self.order if o.is_dma]
        for e in self.ENGS:
            op = Op(e, lambda eng: eng.nop(), False)
            for d in lasts + dmas:
                if d.is_dma or d.eng != e:
                    op.deps.append(d)
                    d.needed = True
            self.ops[e].append(op)
            self.order.append(op)

    def finalize(self):
        esems = {}
        for e in self.ENGS:
            cnt = 0
            ep = 0
            cur = None
            for op in self.ops[e]:
                if op.is_dma or not op.needed:
                    continue
                if cur is None or cnt >= self.EPOCH:
                    cur = self.sem(f"e_{e}_{ep}")
                    ep += 1
                    cnt = 0
                cnt += 1
                op.done = (cur, cnt)
        fin = Op("sp", None, False)
        last = {}
        for o in self.out_dmas:
            s, v = o.done
            k = id(s)
            if k not in last or last[k][1] < v:
                last[k] = (s, v)
        fin.waits = list(last.values())
        waited = {e: {} for e in self.ENGS}
        for op in self.order:
            w = {}
            for d in op.deps:
                s, v = d.done
                k = id(s)
                if k not in w or w[k][1] < v:
                    w[k] = (s, v)
            wm = waited[op.eng]
            for k, (s, v) in w.items():
                if wm.get(k, 0) >= v:
                    continue
                wm[k] = v
                op.waits.append((s, v))
        self.fin = fin

    def emit(self, block):
        nc = self.nc

        def run(e, eng):
            for op in self.ops[e]:
                for s, v in op.waits:
                    eng.wait_ge(s, v)
                ins = op.fn(eng)
                if op.is_dma:
                    ins.then_inc(op.done[0], 16)
                elif op.needed:
                    ins.then_inc(op.done[0], 1)
            if e == "sp":
                for s, v in self.fin.waits:
                    eng.wait_ge(s, v)

        @block.tensor
        def _(eng):
            run("pe", eng)

        @block.scalar
        def _(eng):
            run("act", eng)

        @block.vector
        def _(eng):
            run("dve", eng)

        @block.gpsimd
        def _(eng):
            run("pool", eng)

        @block.sync
        def _(eng):
            run("sp", eng)


def build(NS, NG8, NPOOL):
    SEQL = 256 * NS
    NKB = 2 * NS
    NOWN = NS * 128
    nc = bass.Bass("TRN2", target_bir_lowering=False)
    es = contextlib.ExitStack()
    P = Prog(nc, es)

    def din(name, shape, dt=F32):
        return nc.dram_tensor(name, list(shape), dt, kind="ExternalInput").ap()

    def dout(name, shape, dt=F32):
        return nc.dram_tensor(name, list(shape), dt, kind="ExternalOutput").ap()

    x_all = din("x_all", [SEQL, D]); x_own = din("x_own", [NOWN, D]); p_own = din("p_own", [NOWN, 256])
    x_smp = din("x_smp", [128, D]); p_smp = din("p_smp", [128, 256])
    cs_all = din("cs_all", [SEQL, 64]); cs_own = din("cs_own", [NOWN, 64]); cs_smp = din("cs_smp", [128, 64])
    w_in_d = din("w_in_r", [D, 2480]); w_qup_d = din("w_qup_r", [256, 768]); w_ukT_d = din("w_ukT", [64, 1024])
    w_uv_d = din("w_uv", [128, 512]); w_a2_d = din("w_a2", [16, 256]); w_out_d = din("w_out", [D, D])
    w_pg_d = din("w_pg", [D, D]); w_pp_d = din("w_pp", [256, D])
    g_mix_d = din("g_mix", [1, D]); g_q_d = din("g_q", [1, 256]); g_kv_d = din("g_kv", [1, 128])
    g_on_d = din("g_on", [1, 128]); b_a_d = din("b_a", [1, 256]); g_fin_d = din("g_fin", [1, D])
    mk4_d = din("mk4", [128, 512]); par_d = din("par", [128, 2])
    gc_d = din("gla_c", [128, 6 * 128])
    sx_d = din("smp_x", [128, 2 + 8 + 16])
    m2_d = din("m2", [128, 1024])
    dmask_d = din("dmask", [64, 16 * 128])
    idb_d = din("ident_b", [128, 128]); idf_d = din("ident_f", [128, 128])
    ptl_d = din("ptl", [128, 16 * NG8], I32); pm16_d = din("pm16", [128, 1], I32)
    pool_c = din("pool_c", [NPOOL * 16, 1024]); pool_r = din("pool_r", [NPOOL * 16, 256])
    st_in = din("st_in", [16, 4, 64, 128])

    y_own = dout("y_own", [NOWN, D]); y_smp = dout("y_smp", [128, D])
    ckv_o = dout("ckv_o", [SEQL, 128]); kr_o = dout("kr_o", [SEQL, 32]); stp_o = dout("stp_o", [4, 64, 128])
    ckvs_o = dout("ckvs_o", [128, 128]); krs_o = dout("krs_o", [128, 32]); sts_o = dout("sts_o", [16, 4, 64, 128])

    T = P.tile
    W_in = T("W_in", [128, NCH, 2480], BF16); W_qup = T("W_qup", [128, 2, 768], BF16)
    W_ukT = T("W_ukT", [64, 1024], BF16); W_uv = T("W_uv", [128, 512], BF16); W_a2 = T("W_a2", [16, 256], BF16)
    W_out = T("W_out", [128, NCH, D], BF16); W_pg = T("W_pg", [128, NCH, D], BF16); W_pp = T("W_pp", [128, 2, D], BF16)
    G_mix = T("G_mix", [128, D], F32); G_fin = T("G_fin", [128, D], F32); G_q = T("G_q", [128, 256], F32)
    G_kv = T("G_kv", [128, 128], F32); G_on = T("G_on", [128, 128], F32); B_a = T("B_a", [128, 256], F32)
    PAR = T("PAR", [128, 2], F32); GC = T("GC", [128, 768], F32)
    SX = T("SX", [128, 26], F32)
    IDB = T("IDB", [128, 128], BF16); IDF = T("IDF", [128, 128], F32)
    ONESB = T("ONESB", [128, 1], BF16); ONESF = T("ONESF", [1, 128], F32)
    CMAX = T("CMAX", [128, 1], F32)
    WB = Buf("weights")

    WBQ = {"pool": Buf("weights_pool"), "sp": Buf("weights_sp")}

    def ld(q, dst, src, dap=None):
        P.dma(q, lambda e: e.dma_start(out=(dst.t[:] if dap is None else dap), in_=src), [], [], WBQ[q])

    for c in range(NCH):
        ld("pool", W_in, w_in_d[c * 128:(c + 1) * 128, :], W_in.t[:, c, :])
        ld("pool", W_out, w_out_d[c * 128:(c + 1) * 128, :], W_out.t[:, c, :])
        ld("pool", W_pg, w_pg_d[c * 128:(c + 1) * 128, :], W_pg.t[:, c, :])
    for c in range(2):
        ld("pool", W_qup, w_qup_d[c * 128:(c + 1) * 128, :], W_qup.t[:, c, :])
        ld("pool", W_pp, w_pp_d[c * 128:(c + 1) * 128, :], W_pp.t[:, c, :])
    ld("pool", W_ukT, w_ukT_d); ld("pool", W_uv, w_uv_d); ld("pool", W_a2, w_a2_d)
    ld("pool", IDB, idb_d)
    for dst, src in ((G_mix, g_mix_d), (G_fin, g_fin_d), (G_q, g_q_d), (G_kv, g_kv_d), (G_on, g_on_d), (B_a, b_a_d)):
        ld("sp", dst, src.partition_broadcast(128))
    for dst, src in ((PAR, par_d), (GC, gc_d), (SX, sx_d), (IDF, idf_d)):
        ld("sp", dst, src)
    P.pool(lambda e: e.memset(ONESB.t[:], 1.0), [], [ONESB.b])
    P.pool(lambda e: e.memset(ONESF.t[:], 1.0), [], [ONESF.b])
    CT = T("CT", [128, 128], F32)

    def init_consts():
        P.dve(lambda e: e.tensor_tensor(out=CT.t[:], in0=G_kv.t[:], in1=G_kv.t[:], op=ALU.mult), [G_kv.b, WB], [CT.b])
        P.dve(lambda e: e.reduce_max(out=CMAX.t[:], in_=CT.t[:], axis=AX.X), [CT.b], [CMAX.b])
        P.act(lambda e: e.activation(out=CMAX.t[:], in_=CMAX.t[:], func=AF.Ln, scale=128.0), [CMAX.b], [CMAX.b])
        P.act(lambda e: e.activation(out=CMAX.t[:], in_=CMAX.t[:], func=AF.Exp, scale=0.5), [CMAX.b], [CMAX.b])


    RMAX2 = T("RMAX2", [128, 1], F32)
    P.dve(lambda e: e.memset(RMAX2.t[:], 0.0), [], [RMAX2.b])

    PZ2 = T("pz", [128, 1024], F32, True); PS2 = T("ps", [128, 1024], F32, True)
    pz = [Tl(PZ2.t[:, 0:512], Buf("pz0")), Tl(PZ2.t[:, 512:1024], Buf("pz1"))]
    ps = [Tl(PS2.t[:, 0:512], Buf("ps0")), Tl(PS2.t[:, 512:1024], Buf("ps1"))]
    pa = [T("pa0", [128, 512], F32, True), T("pa1", [128, 512], F32, True)]
    ptp = T("ptp", [128, 1024], BF16, True)
    ppt = T("ppt", [128, 1024], BF16, True)
    ppth = [Tl(ppt.t[:, 0:512], Buf("ppt_a")), Tl(ppt.t[:, 512:1024], Buf("ppt_b"))]

    XO = T("XO", [128, D], F32); PIN = T("PIN", [128, 256], F32)
    CSO = T("CSO", [128, 64], F32)
    JUNK = T("JUNK", [128, D], BF16)
    XN = T("XN", [128, D], BF16); XNT = T("XNT", [128, NCH, 128], BF16)
    SS = T("SS", [128, 1], F32); RS = T("RS", [128, 1], F32)
    SS8 = T("SS8", [128, 8], F32); RS8 = T("RS8", [128, 8], F32)
    CKF = T("CKF", [128, 128], F32); CKB = T("CKB", [128, 128], BF16)
    KRF = T("KRF", [128, 32], F32); KRB = T("KRB", [128, 96], BF16); RT1 = T("RT1", [128, 256], F32); RT2 = T("RT2", [128, 256], F32)
    NR2 = T("NR2", [128, 1], F32)
    AGT = T("AGT", [16, 128], BF16)
    T1 = T("T1", [128, 256], F32); LSB = T("LSB", [128, 256], F32); ERB = T("ERB", [128, 256], F32)
    KD = T("KD", [128, 256], BF16); VB = T("VB", [128, 512], BF16); EBL = T("EBL", [128, 8], F32)
    EBT = T("EBT", [64, 512], F32); ENBT = T("ENBT", [64, 512], F32)
    QET = T("QET", [64, 512], BF16); KET = T("KET", [64, 512], BF16); ATM = T("ATM", [128, 512], BF16)
    SOWNB = T("SOWNB", [64, 4, 128], BF16)
    OG = T("OG", [128, 512], F32)
    SIG = T("SIG", [128, D], F32)
    CAT = T("CAT", [128, D], BF16); CATT = T("CATT", [128, NCH, 128], BF16); H1T = CATT
    QCN = T("QCN", [128, 256], BF16); QCNT = T("QCNT", [128, 2, 128], BF16)
    QNT = T("QNT", [64, 1024], BF16); QLT = T("QLT", [128, 1024], BF16)
    QRB = T("QRB", [128, 768], BF16); QRT = T("QRT", [96, 1024], BF16)
    NL2 = T("NL2", [128, 8], F32); NRQ = T("NRQ", [128, 8], F32); NEGM = T("NEGM", [128, 8], F32)
    R1 = T("R1", [1, 1], F32); RM = T("RM", [128, 1], F32)
    SM = T("SM", [128, 512], F32)
    PB = [T(f"PB{k}", [128, 512], BF16) for k in range(2)]
    PT = [T(f"PT{k}", [128, 512], BF16) for k in range(2)]
    LR = T("LR", [128, 8], F32)
    OLAT = T("OLAT", [128, 8, 128], BF16); OLT = T("OLT", [128, 1024], BF16); QSQ = OLT
    H1 = XO; H1B = JUNK; H2 = XO; YO = SIG
    PBF = T("PBF", [128, 256], BF16); PTT = T("PTT", [128, 2, 128], BF16)
    es_main = P.es
    es_p = contextlib.ExitStack(); P.es = es_p
    NGM = max(1, (NKB + 3) // 4); NCC = (NGM + 2) // 3
    CKVT = T("CKVT", [128, SEQL], BF16); KRT = T("KRT", [96, NCC * 512], BF16); CKVN = T("CKVN", [128, NKB, 128], BF16)
    kvb = [Buf(f"kv{j}") for j in range(NKB)]
    ST = [T(f"ST{k}", [64, 4, 128], F32) for k in range(3)]
    MK4 = T("MK4", [128, 512], BF16)
    XS1 = T("XS", [128, D], F32); XS = [XS1, XS1]
    CSS1 = T("CSS", [128, 64], F32); CSS = [CSS1, CSS1]
    LSUM = T("LSUM", [128, 8, NGM], F32)
    P.es = es_main
    ld("pool", MK4, mk4_d)
    P.dve(lambda e: e.memset(ST[0].t[:], 0.0), [], [ST[0].b])
    Z = type("Z", (), {})()

    def alloc_sample():
        Z.PGC = [T(f"PGC{k}", [128, 8, 128], F32) for k in range(2)]; Z.PGR = [T(f"PGR{k}", [128, 8, 32], F32) for k in range(2)]
        Z.PGCB = [T(f"PGCB{k}", [128, 8, 128], BF16) for k in range(2)]; Z.PGRB = [T(f"PGRB{k}", [128, 8, 32], BF16) for k in range(2)]
        Z.CTS = [T(f"CTS{k}", [128, 512], BF16) for k in range(2)]; Z.KTS = [T(f"KTS{k}", [32, 512], BF16) for k in range(2)]
        Z.SMALL = [[T(f"{nm}_{q}", [64, 1], F32) for nm in ("MRUN", "MNEW", "NMN", "ALP", "LRUN", "LG", "GMX")] for q in range(3)]
        Z.ACC2 = [T(f"ACC_{q}", [64, 128], F32) for q in range(3)]; Z.ACCB2 = [T(f"ACCB_{q}", [64, 128], BF16) for q in range(3)]
        Z.PBS = [PB[0], PB[1]]; Z.PTS = [PT[0], PT[1]]
        Z.CKVT_S = T("CKVT_S", [128, 128], BF16); Z.KRT_S = T("KRT_S", [32, 128], BF16); Z.CKVN_S = T("CKVN_S", [128, 128], BF16)
        Z.DMASK = T("DMASK", [64, 2048], BF16)
        Z.PTL = T("PTL", [128, 16 * NG8], I32); Z.PM16 = T("PM16", [128, 1], I32); Z.IDX = T("IDX", [128, 16 * NG8], I32)
        Z.es_g = contextlib.ExitStack(); P.es = Z.es_g
        s0 = T("S0F", [128, 8, 128], F32); Z.S0F = [s0, s0]; Z.S0B = T("S0B", [128, 8, 128], BF16)
        Z.L2 = T("L2", [128, 2, 64], F32); Z.EBS = T("EBS", [128, 8], F32)
        Z.QEX = T("QEX", [128, 8, 128], BF16); Z.KDX = T("KDX", [128, 16, 64], BF16)
        Z.QEN = T("QEN", [128, 2, 64], BF16)
        Z.SNEW = T("SNEW", [128, 8, 128], F32)
        Z.M2 = T("M2", [128, 1024], BF16)
        P.es = es_main
        ld("pool", Z.M2, m2_d); ld("pool", Z.DMASK, dmask_d); ld("sp", Z.PTL, ptl_d); ld("sp", Z.PM16, pm16_d)

    def alloc_stream2():
        Z.es_g.close()
        Z.PGC.append(T("PGC2", [128, 8, 128], F32)); Z.PGR.append(T("PGR2", [128, 8, 32], F32))
        Z.PGCB.append(T("PGCB2", [128, 8, 128], BF16)); Z.PGRB.append(T("PGRB2", [128, 8, 32], BF16))
        Z.CTS.append(T("CTS2", [128, 512], BF16)); Z.KTS.append(T("KTS2", [32, 512], BF16))
        Z.PBS.append(T("PB2", [128, 512], BF16)); Z.PTS.append(T("PT2", [128, 512], BF16))

    ident = IDB.t[:]

    def transposes(src, dst, nch, cp="act"):
        for c in range(nch):
            P.pe(lambda e, c=c: e.transpose(out=ptp.t[:, c * 128:(c + 1) * 128], in_=src.t[:, c * 128:(c + 1) * 128],
                                            identity=ident), [src.b, IDB.b], [ptp.b])
        f = (lambda e: e.activation(out=dst.t[:].rearrange("p c t -> p (c t)"), in_=ptp.t[:, 0:nch * 128], func=AF.Copy)) \
            if cp == "act" else (lambda e: e.tensor_copy(out=dst.t[:].rearrange("p c t -> p (c t)"), in_=ptp.t[:, 0:nch * 128]))
        (P.act if cp == "act" else P.dve)(f, [ptp.b], [dst.b])

    def linear(out_ps, xT, nch, W, c0, n):
        for c in range(nch):
            P.pe(lambda e, c=c: e.matmul(out_ps.t[:, 0:n], lhsT=xT.t[:, c, :], rhs=W.t[:, c, c0:c0 + n],
                                         start=(c == 0), stop=(c == nch - 1)), [xT.b, W.b, WB], [out_ps.b])

    def linearT(out_ap, out_b, W, c0, m, xT, nch):
        for c in range(nch):
            P.pe(lambda e, c=c: e.matmul(out_ap, lhsT=W.t[:, c, c0:c0 + m], rhs=xT.t[:, c, :],
                                         start=(c == 0), stop=(c == nch - 1)), [xT.b, W.b, WB], [out_b])

    def sumsq(src_ap, src_b, n, out_ap, out_b):
        P.dve(lambda e: e.scalar_tensor_tensor(out=JUNK.t[:, 0:n], in0=src_ap, scalar=1.0, in1=src_ap, op0=ALU.mult,
                                               op1=ALU.mult, accum_out=out_ap), [src_b], [JUNK.b, out_b])

    def powm(out_ap, out_b, in_ap, in_b, scale, bias, power, tmp_ap):
        P.act(lambda e: e.activation(out=tmp_ap, in_=in_ap, func=AF.Ln, bias=bias, scale=scale), [in_b], [out_b])
        P.act(lambda e: e.activation(out=out_ap, in_=tmp_ap, func=AF.Exp, scale=power), [out_b], [out_b])

    def rms_prep(xt):
        sumsq(xt.t[:], xt.b, D, SS.t[:], SS.b)
        powm(RS.t[:], RS.b, SS.t[:], SS.b, 1.0 / D, 1e-6, -0.5, RS.t[:])

    def front(xt):
        rms_prep(xt)
        P.dve(lambda e: e.scalar_tensor_tensor(out=XN.t[:], in0=xt.t[:], scalar=RS.t[:, 0:1], in1=G_mix.t[:],
                                               op0=ALU.mult, op1=ALU.mult), [xt.b, RS.b, G_mix.b, WB], [XN.b])
        transposes(XN, XNT, NCH)

    def rope(src_ap, src_b, nh, cst, outs):
        n = nh * 32
        cs = cst.t[:, 0:32].unsqueeze(1).to_broadcast([128, nh, 32])
        sa = cst.t[:, 32:48].unsqueeze(1).to_broadcast([128, nh, 16])
        sb_ = cst.t[:, 48:64].unsqueeze(1).to_broadcast([128, nh, 16])
        a = RT1.t[:, 0:n].rearrange("p (h r) -> p h r", r=32)
        b = RT2.t[:, 0:n].rearrange("p (h r) -> p h r", r=32)
        P.dve(lambda e: e.tensor_tensor(out=a, in0=src_ap, in1=cs, op=ALU.mult), [src_b, cst.b], [RT1.b])
        P.dve(lambda e: e.tensor_tensor(out=b[:, :, 0:16], in0=src_ap[:, :, 16:32], in1=sa, op=ALU.mult), [src_b, cst.b], [RT2.b])
        P.dve(lambda e: e.tensor_tensor(out=b[:, :, 16:32], in0=src_ap[:, :, 0:16], in1=sb_, op=ALU.mult), [src_b, cst.b], [RT2.b])
        for oap, ob in outs:
            P.dve(lambda e, oap=oap: e.tensor_tensor(out=oap, in0=a, in1=b, op=ALU.add), [RT1.b, RT2.b], [ob])

    def kv_part(cst, ckvt_ap, krt_ap, ckvn_ap, kvbuf, track_rmax, pg=0):
        P.act(lambda e: e.activation(out=CKF.t[:], in_=pz[0].t[:, 0:128], func=AF.Copy), [pz[0].b], [CKF.b])
        sumsq(CKF.t[:], CKF.b, 128, SS8.t[:, 0:1], SS8.b)
        powm(RS8.t[:, 0:1], RS8.b, SS8.t[:, 0:1], SS8.b, 1.0 / 128, 1e-6, -0.5, RS8.t[:, 0:1])
        P.dve(lambda e: e.scalar_tensor_tensor(out=CKF.t[:], in0=CKF.t[:], scalar=RS8.t[:, 0:1], in1=G_kv.t[:],
                                               op0=ALU.mult, op1=ALU.mult), [CKF.b, RS8.b, G_kv.b, WB], [CKF.b])
        P.act(lambda e: e.activation(out=CKB.t[:], in_=CKF.t[:], func=AF.Copy), [CKF.b], [CKB.b])
        P.act(lambda e: e.activation(out=ckvn_ap, in_=CKF.t[:], func=AF.Copy), [CKF.b], [kvbuf])
        rope(pz[0].t[:, 128:160].rearrange("p (h r) -> p h r", r=32), pz[0].b, 1, cst,
             [(KRF.t[:].rearrange("p (h r) -> p h r", r=32), KRF.b)] +
             [(KRB.t[:, 32 * q:32 * q + 32].rearrange("p (h r) -> p h r", r=32), KRB.b) for q in range(3)])
        if track_rmax:
            sumsq(KRF.t[:], KRF.b, 32, NR2.t[:], NR2.b)
            P.dve(lambda e: e.tensor_tensor(out=RMAX2.t[:], in0=RMAX2.t[:], in1=NR2.t[:], op=ALU.max), [RMAX2.b, NR2.b], [RMAX2.b])
        P.pe(lambda e: e.transpose(out=ptp.t[:, 0:128], in_=CKB.t[:], identity=ident), [CKB.b, IDB.b], [ptp.b])
        P.pe(lambda e: e.transpose(out=ptp.t[0:96, 128:256], in_=KRB.t[:], identity=ident), [KRB.b, IDB.b], [ptp.b])
        P.act(lambda e: e.activation(out=ckvt_ap, in_=ptp.t[:, 0:128], func=AF.Copy), [ptp.b], [kvbuf])
        P.act(lambda e: e.activation(out=krt_ap, in_=ptp.t[32 * pg:32 * pg + 32, 128:256], func=AF.Copy), [ptp.b], [kvbuf])

    def gla_nat(triL_ap):
        linearT(ps[0].t[0:16, 0:128], ps[0].b, W_in, C_A, 16, XNT, NCH)
        P.act(lambda e: e.activation(out=AGT.t[:], in_=ps[0].t[0:16, 0:128], func=AF.Copy), [ps[0].b], [AGT.b])
        P.pe(lambda e: e.matmul(ps[1].t[:, 0:256], lhsT=AGT.t[:], rhs=W_a2.t[:], start=True, stop=True),
             [AGT.b, W_a2.b, WB], [ps[1].b])
        P.dve(lambda e: e.tensor_tensor(out=T1.t[:], in0=ps[1].t[:, 0:256], in1=B_a.t[:], op=ALU.add), [ps[1].b, B_a.b, WB], [T1.b])
        P.act(lambda e: e.activation(out=T1.t[:], in_=T1.t[:], func=AF.Exp, scale=-1.0), [T1.b], [T1.b])
        P.act(lambda e: e.activation(out=LSB.t[:], in_=T1.t[:], func=AF.Ln, bias=1.0), [T1.b], [LSB.b])
        P.pe(lambda e: e.matmul(pa[0].t[:, 0:256], lhsT=triL_ap, rhs=LSB.t[:], start=True, stop=True), [GC.b, LSB.b, WB], [pa[0].b])
        P.act(lambda e: e.activation(out=ERB.t[:], in_=pa[0].t[:, 0:256], func=AF.Exp), [pa[0].b], [ERB.b])
        P.dve(lambda e: e.tensor_tensor(out=KD.t[:], in0=pz[0].t[:, C_KG:C_KG + 256], in1=ERB.t[:], op=ALU.mult), [pz[0].b, ERB.b], [KD.b])
        P.act(lambda e: e.activation(out=VB.t[:], in_=pz[1].t[:], func=AF.Copy), [pz[1].b], [VB.b])

    def load_x(q, xt, src, sb=None):
        P.dma(q, lambda e: e.dma_start(out=xt.t[:], in_=src), [], [xt.b], xt.b)

    def store(src_t, src_ap, dst):
        P.dma("pool", lambda e: e.dma_start(out=dst, in_=src_ap), [src_t.b], [], src_t.b, is_out=True)

    TRIU_P, TRIL_P, MASK_P = GC.t[:, 0:128], GC.t[:, 128:256], GC.t[:, 256:384]
    TRIU_S, TRIL_S, MASK_S = GC.t[:, 384:512], GC.t[:, 512:640], GC.t[:, 640:768]

    def shared(blk):
        xt = XS[blk % 2]; cst = CSS[blk % 2]
        load_x("sp", xt, x_all[blk * 128:(blk + 1) * 128, :])
        load_x("sp", cst, cs_all[blk * 128:(blk + 1) * 128, :])
        front(xt)
        linear(pz[0], XNT, NCH, W_in, 0, 416)
        linear(pz[1], XNT, NCH, W_in, C_VG, 512)
        g_ = blk // 4; pg = g_ % 3; kc = (g_ // 3) * 512 + (blk % 4) * 128
        kv_part(cst, CKVT.t[:, blk * 128:(blk + 1) * 128], KRT.t[32 * pg:32 * pg + 32, kc:kc + 128], CKVN.t[:, blk, :], kvb[blk], True, pg)
        store(CKF, CKF.t[:], ckv_o[blk * 128:(blk + 1) * 128, :])
        store(KRF, KRF.t[:], kr_o[blk * 128:(blk + 1) * 128, :])
        gla_nat(TRIL_P)
        for h in range(4):
            P.pe(lambda e, h=h: e.matmul(pa[0].t[0:64, 256 + h:257 + h], lhsT=LSB.t[:, h * 64:(h + 1) * 64], rhs=TRIU_P[:, 127:128],
                                         start=True, stop=True), [LSB.b, GC.b, WB], [pa[0].b])
        P.act(lambda e: e.activation(out=EBL.t[0:64, 0:4], in_=pa[0].t[0:64, 256:260], func=AF.Exp), [pa[0].b], [EBL.b])
        sin, sout = ST[blk % 3], ST[(blk + 1) % 3]
        for h in range(4):
            P.pe(lambda e, h=h: e.matmul(pa[1].t[0:64, h * 128:(h + 1) * 128], lhsT=KD.t[:, h * 64:(h + 1) * 64],
                                         rhs=VB.t[:, h * 128:(h + 1) * 128], start=True, stop=True), [KD.b, VB.b], [pa[1].b])
        for h in range(4):
            P.dve(lambda e, h=h: e.scalar_tensor_tensor(out=sout.t[:, h, :], in0=sin.t[:, h, :], scalar=EBL.t[0:64, h:h + 1],
                                                        in1=pa[1].t[0:64, h * 128:(h + 1) * 128], op0=ALU.mult, op1=ALU.add),
                  [sin.b, EBL.b, pa[1].b], [sout.b])

    def silu_to(dst_ap, dst_b, gate_ps, o_ap, o_bufs, n):
        P.act(lambda e: e.activation(out=SIG.t[:, 0:n], in_=gate_ps.t[:, 0:n], func=AF.Exp, scale=-1.0), [gate_ps.b], [SIG.b])
        P.dve(lambda e: e.tensor_scalar(out=SIG.t[:, 0:n], in0=SIG.t[:, 0:n], scalar1=1.0, scalar2=None, op0=ALU.add), [SIG.b], [SIG.b])
        P.dve(lambda e: e.reciprocal(out=SIG.t[:, 0:n], in_=SIG.t[:, 0:n]), [SIG.b], [SIG.b])
        P.dve(lambda e: e.tensor_tensor(out=SIG.t[:, 0:n], in0=gate_ps.t[:, 0:n], in1=SIG.t[:, 0:n], op=ALU.mult), [gate_ps.b, SIG.b], [SIG.b])
        P.dve(lambda e: e.tensor_tensor(out=dst_ap, in0=o_ap, in1=SIG.t[:, 0:n], op=ALU.mult), o_bufs + [SIG.b], [dst_b])

    def own(i, smp):
        if smp:
            xsrc, psrc, cssrc, ydst = x_smp, p_smp, cs_smp, y_smp
        else:
            sl = slice(i * 128, (i + 1) * 128)
            xsrc, psrc, cssrc, ydst = x_own[sl, :], p_own[sl, :], cs_own[sl, :], y_own[sl, :]
        load_x("sp", XO, xsrc); load_x("sp", PIN, psrc); load_x("sp", CSO, cssrc)
        front(XO)
        triU, triL, maskT = (TRIU_S, TRIL_S, MASK_S) if smp else (TRIU_P, TRIL_P, MASK_P)
        linear(pz[0], XNT, NCH, W_in, 0, 416)
        linear(pz[1], XNT, NCH, W_in, C_VG, 512)
        if smp:
            kv_part(CSO, Z.CKVT_S.t[:], Z.KRT_S.t[:], Z.CKVN_S.t[:], Z.CKVN_S.b, False)
            store(CKF, CKF.t[:], ckvs_o); store(KRF, KRF.t[:], krs_o)
        gla_nat(triL)
        for h in range(4):
            P.pe(lambda e, h=h: e.matmul(pa[1].t[0:64, h * 128:(h + 1) * 128], lhsT=LSB.t[:, h * 64:(h + 1) * 64], rhs=triU,
                                         start=True, stop=True), [LSB.b, GC.b, WB], [pa[1].b])
        P.act(lambda e: e.activation(out=EBT.t[:], in_=pa[1].t[0:64, :], func=AF.Exp), [pa[1].b], [EBT.b])
        P.act(lambda e: e.activation(out=ENBT.t[:], in_=pa[1].t[0:64, :], func=AF.Exp, scale=-1.0), [pa[1].b], [ENBT.b])
        for h in range(4):
            linearT(ps[0].t[0:64, h * 128:(h + 1) * 128], ps[0].b, W_in, C_QG + h * 64, 64, XNT, NCH)
        for h in range(4):
            linearT(ps[1].t[0:64, h * 128:(h + 1) * 128], ps[1].b, W_in, C_KG + h * 64, 64, XNT, NCH)
        P.dve(lambda e: e.scalar_tensor_tensor(out=QET.t[:], in0=ps[0].t[0:64, :], scalar=0.125, in1=EBT.t[:], op0=ALU.mult,
                                               op1=ALU.mult), [ps[0].b, EBT.b], [QET.b])
        P.dve(lambda e: e.tensor_tensor(out=KET.t[:], in0=ps[1].t[0:64, :], in1=ENBT.t[:], op=ALU.mult), [ps[1].b, ENBT.b], [KET.b])
        for h in range(4):
            P.pe(lambda e, h=h: e.matmul(pa[0].t[:, h * 128:(h + 1) * 128], lhsT=KET.t[:, h * 128:(h + 1) * 128],
                                         rhs=QET.t[:, h * 128:(h + 1) * 128], start=True, stop=True), [KET.b, QET.b], [pa[0].b])
        P.dve(lambda e: e.tensor_tensor(out=ATM.t[:].rearrange("p (h t) -> p h t", h=4), in0=pa[0].t[:].rearrange("p (h t) -> p h t", h=4),
                                        in1=maskT.unsqueeze(1).to_broadcast([128, 4, 128]), op=ALU.mult), [pa[0].b, GC.b], [ATM.b])
        if not smp:
            sa, sb_ = ST[(2 * i) % 3], ST[(2 * i + 1) % 3]
            sown = OG.t[0:64, :].rearrange("k (h v) -> k h v", h=4)
            P.dve(lambda e: e.tensor_scalar(out=sown, in0=sa.t[:], scalar1=PAR.t[0:64, 0:1], scalar2=None, op0=ALU.mult),
                  [sa.b, PAR.b, WB], [OG.b])
            P.dve(lambda e: e.scalar_tensor_tensor(out=SOWNB.t[:], in0=sb_.t[:], scalar=PAR.t[0:64, 1:2], in1=sown,
                                                   op0=ALU.mult, op1=ALU.add), [sb_.b, PAR.b, OG.b], [SOWNB.b])
            for h in range(4):
                P.pe(lambda e, h=h: e.matmul(ps[0].t[:, h * 128:(h + 1) * 128], lhsT=ATM.t[:, h * 128:(h + 1) * 128],
                                             rhs=VB.t[:, h * 128:(h + 1) * 128], start=True, stop=False), [ATM.b, VB.b], [ps[0].b])
                P.pe(lambda e, h=h: e.matmul(ps[0].t[:, h * 128:(h + 1) * 128], lhsT=QET.t[:, h * 128:(h + 1) * 128],
                                             rhs=SOWNB.t[:, h, :], start=False, stop=True), [QET.b, SOWNB.b], [ps[0].b])
        else:
            parm = SX.t[:, 0:2]; msel = SX.t[:, 2:10]; kdm = SX.t[:, 10:26]
            for h in range(4):
                s0f = Z.S0F[h % 2]
                for il in range(2):
                    P.dma("sp", lambda e, h=h, il=il, s0f=s0f: e.dma_start(
                        out=s0f.t[il * 64:(il + 1) * 64, :, :],
                        in_=st_in[:, h, :, :].rearrange("(p il) k v -> il k p v", il=2)[il]), [], [s0f.b], s0f.b)
                P.act(lambda e, s0f=s0f: e.activation(out=Z.S0B.t[:].rearrange("p a v -> p (a v)"), in_=s0f.t[:].rearrange("p a v -> p (a v)"),
                                                      func=AF.Copy), [s0f.b], [Z.S0B.b])
                P.pe(lambda e, h=h: e.transpose(out=ppt.t[:, 0:64], in_=QET.t[:, h * 128:(h + 1) * 128], identity=IDB.t[0:64, 0:64]),
                     [QET.b, IDB.b], [ppt.b])
                P.dve(lambda e: e.tensor_copy(out=Z.QEN.t[:], in_=ppt.t[:, 0:64].unsqueeze(1).to_broadcast([128, 2, 64])), [ppt.b], [Z.QEN.b])
                P.pe(lambda e: e.transpose(out=ppt.t[:, 128:256], in_=Z.QEN.t[:].rearrange("p a k -> p (a k)"), identity=ident),
                     [Z.QEN.b, IDB.b], [ppt.b])
                P.dve(lambda e: e.tensor_tensor(out=Z.QEX.t[:], in0=ppt.t[:, 128:256].unsqueeze(1).to_broadcast([128, 8, 128]),
                                                in1=Z.M2.t[:].rearrange("p (a t) -> p a t", a=8), op=ALU.mult), [ppt.b, Z.M2.b], [Z.QEX.b])
                P.pe(lambda e, h=h: e.matmul(ps[0].t[:, h * 128:(h + 1) * 128], lhsT=ATM.t[:, h * 128:(h + 1) * 128],
                                             rhs=VB.t[:, h * 128:(h + 1) * 128], start=True, stop=False), [ATM.b, VB.b], [ps[0].b])
                for p in range(8):
                    P.pe(lambda e, h=h, p=p: e.matmul(ps[0].t[:, h * 128:(h + 1) * 128], lhsT=Z.QEX.t[:, p, :], rhs=Z.S0B.t[:, p, :],
                                                      start=False, stop=(p == 7)), [Z.QEX.b, Z.S0B.b], [ps[0].b])
                P.dve(lambda e, h=h: e.tensor_tensor(out=Z.L2.t[:], in0=LSB.t[:, h * 64:(h + 1) * 64].unsqueeze(1).to_broadcast([128, 2, 64]),
                                                     in1=parm.unsqueeze(2).to_broadcast([128, 2, 64]), op=ALU.mult), [LSB.b, SX.b], [Z.L2.b])
                P.pe(lambda e: e.matmul(pa[1].t[:, 0:8], lhsT=Z.L2.t[:].rearrange("p a k -> p (a k)"), rhs=msel, start=True, stop=True),
                     [Z.L2.b, SX.b], [pa[1].b])
                P.act(lambda e: e.activation(out=Z.EBS.t[:], in_=pa[1].t[:, 0:8], func=AF.Exp), [pa[1].b], [Z.EBS.b])
                P.dve(lambda e, h=h: e.tensor_tensor(out=Z.KDX.t[:], in0=KD.t[:, h * 64:(h + 1) * 64].unsqueeze(1).to_broadcast([128, 16, 64]),
                                                     in1=kdm.unsqueeze(2).to_broadcast([128, 16, 64]), op=ALU.mult), [KD.b, SX.b], [Z.KDX.b])
                for half in range(2):
                    for p4 in range(4):
                        p = half * 4 + p4
                        P.pe(lambda e, h=h, p=p, p4=p4: e.matmul(pa[0].t[:, p4 * 128:(p4 + 1) * 128],
                                                                 lhsT=Z.KDX.t[:, 2 * p:2 * p + 2, :].rearrange("t a k -> t (a k)"),
                                                                 rhs=VB.t[:, h * 128:(h + 1) * 128], start=True, stop=True),
                             [Z.KDX.b, VB.b], [pa[0].b])
                    for p4 in range(4):
                        p = half * 4 + p4
                        P.dve(lambda e, p=p, p4=p4, s0f=s0f: e.scalar_tensor_tensor(
                            out=Z.SNEW.t[:, p, :], in0=s0f.t[:, p, :], scalar=Z.EBS.t[:, p:p + 1], in1=pa[0].t[:, p4 * 128:(p4 + 1) * 128],
                            op0=ALU.mult, op1=ALU.add), [s0f.b, Z.EBS.b, pa[0].b], [Z.SNEW.b])
                for il in range(2):
                    P.dma("pool", lambda e, h=h, il=il: e.dma_start(
                        out=sts_o[:, h, :, :].rearrange("(p il) k v -> il k p v", il=2)[il],
                        in_=Z.SNEW.t[il * 64:(il + 1) * 64, :, :]), [Z.SNEW.b], [], Z.SNEW.b, is_out=True)
        P.act(lambda e: e.activation(out=OG.t[:], in_=ps[0].t[:], func=AF.Copy), [ps[0].b], [OG.b])
        for h in range(4):
            sumsq(OG.t[:, h * 128:(h + 1) * 128], OG.b, 128, SS8.t[:, h:h + 1], SS8.b)
        powm(RS8.t[:, 0:4], RS8.b, SS8.t[:, 0:4], SS8.b, 1.0 / 128, 1e-6, -0.5, RS8.t[:, 0:4])
        for h in range(4):
            P.dve(lambda e, h=h: e.scalar_tensor_tensor(out=OG.t[:, h * 128:(h + 1) * 128], in0=OG.t[:, h * 128:(h + 1) * 128],
                                                        scalar=RS8.t[:, h:h + 1], in1=G_on.t[:], op0=ALU.mult, op1=ALU.mult),
                  [OG.b, RS8.b, G_on.b, WB], [OG.b])
        linear(pz[0], XNT, NCH, W_in, C_GG, 512)
        silu_to(CAT.t[:, 512:1024], CAT.b, pz[0], OG.t[:], [OG.b], 512)

        linear(pz[1], XNT, NCH, W_in, C_QC, 256)
        P.act(lambda e: e.activation(out=RT1.t[:], in_=pz[1].t[:, 0:256], func=AF.Copy), [pz[1].b], [RT1.b])
        sumsq(RT1.t[:], RT1.b, 256, SS.t[:], SS.b)
        powm(RS.t[:], RS.b, SS.t[:], SS.b, 1.0 / 256, 1e-6, -0.5, RS.t[:])
        P.dve(lambda e: e.scalar_tensor_tensor(out=QCN.t[:], in0=RT1.t[:], scalar=RS.t[:, 0:1], in1=G_q.t[:],
                                               op0=ALU.mult, op1=ALU.mult), [RT1.b, RS.b, G_q.b, WB], [QCN.b])
        transposes(QCN, QCNT, 2)
        for h in range(8):
            linearT(ps[h // 4].t[0:64, (h % 4) * 128:(h % 4 + 1) * 128], ps[h // 4].b, W_qup, h * 64, 64, QCNT, 2)
        for k in range(2):
            P.act(lambda e, k=k: e.activation(out=QNT.t[:, k * 512:(k + 1) * 512], in_=ps[k].t[0:64, :], func=AF.Copy), [ps[k].b], [QNT.b])
        for h in range(8):
            P.pe(lambda e, h=h: e.matmul(pz[h // 4].t[:, (h % 4) * 128:(h % 4 + 1) * 128], lhsT=W_ukT.t[:, h * 128:(h + 1) * 128],
                                         rhs=QNT.t[:, h * 128:(h + 1) * 128], start=True, stop=True), [W_ukT.b, WB, QNT.b], [pz[h // 4].b])
        for k in range(2):
            if smp:
                o = QLT.t[:].rearrange("l (s h t) -> l h s t", s=16, h=8, t=8)[:, 4 * k:4 * k + 4, :, :]
                i_ = pz[k].t[:].rearrange("l (h s t) -> l h s t", h=4, s=16, t=8)
            else:
                o = QLT.t[:, k * 512:(k + 1) * 512]; i_ = pz[k].t[:]
            P.act(lambda e, o=o, i_=i_: e.activation(out=o, in_=i_, func=AF.Copy), [pz[k].b], [QLT.b])
        linear(ps[0], QCNT, 2, W_qup, 512, 256)
        rope(ps[0].t[:, 0:256].rearrange("p (h r) -> p h r", r=32), ps[0].b, 8, CSO,
             [(QRB.t[:].rearrange("p (h q r) -> p h q r", q=3, r=32)[:, :, q, :], QRB.b) for q in range(3)])
        for h in range(8):
            P.pe(lambda e, h=h: e.transpose(out=ptp.t[0:96, h * 128:(h + 1) * 128], in_=QRB.t[:, h * 96:(h + 1) * 96], identity=ident),
                 [QRB.b, IDB.b], [ptp.b])
        if smp:
            o = QRT.t[:].rearrange("l (s h t) -> l h s t", s=16, h=8, t=8)
            i_ = ptp.t[0:96, :].rearrange("l (h s t) -> l h s t", h=8, s=16, t=8)
        else:
            o = QRT.t[:]; i_ = ptp.t[0:96, :]
        P.act(lambda e: e.activation(out=o, in_=i_, func=AF.Copy), [ptp.b], [QRT.b])

        if not smp:
            P.dve(lambda e: e.tensor_tensor(out=QSQ.t[:], in0=QLT.t[:], in1=QLT.t[:], op=ALU.mult), [QLT.b], [QSQ.b])
            for h in range(8):
                P.pe(lambda e, h=h: e.matmul(pa[0].t[:, h:h + 1], lhsT=QSQ.t[:, h * 128:(h + 1) * 128], rhs=ONESB.t[:],
                                             start=True, stop=True), [QSQ.b, ONESB.b], [pa[0].b])
            qr0 = QRB.t[:].rearrange("p (h q r) -> p h q r", q=3, r=32)[:, :, 0, :]
            P.dve(lambda e: e.tensor_tensor(out=RT1.t[:].rearrange("p (h r) -> p h r", r=32), in0=qr0, in1=qr0, op=ALU.mult), [QRB.b], [RT1.b])
            P.dve(lambda e: e.tensor_reduce(out=NRQ.t[:], in_=RT1.t[:].rearrange("p (h r) -> p h r", r=32), axis=AX.X, op=ALU.add),
                  [RT1.b], [NRQ.b])
            P.pe(lambda e: e.transpose(out=pa[1].t[0:1, 0:128], in_=RMAX2.t[:], identity=IDF.t[:]), [RMAX2.b, IDF.b, WB], [pa[1].b])
            P.dve(lambda e: e.reduce_max(out=R1.t[:], in_=pa[1].t[0:1, 0:128], axis=AX.X), [pa[1].b], [R1.b])
            P.pe(lambda e: e.matmul(pa[1].t[:, 128:129], lhsT=ONESF.t[:], rhs=R1.t[:], start=True, stop=True), [ONESF.b, R1.b], [pa[1].b])
            powm(RM.t[:], RM.b, pa[1].t[:, 128:129], pa[1].b, 1.0, 1e-20, 0.5, RM.t[:])
            powm(NL2.t[:], NL2.b, pa[0].t[:, 0:8], pa[0].b, 1.0, 1e-20, 0.5, NL2.t[:])
            powm(NRQ.t[:], NRQ.b, NRQ.t[:], NRQ.b, 1.0, 1e-20, 0.5, NRQ.t[:])
            P.dve(lambda e: e.tensor_scalar(out=NL2.t[:], in0=NL2.t[:], scalar1=CMAX.t[:, 0:1], scalar2=None, op0=ALU.mult),
                  [NL2.b, CMAX.b], [NL2.b])
            P.dve(lambda e: e.scalar_tensor_tensor(out=NEGM.t[:], in0=NRQ.t[:], scalar=RM.t[:, 0:1], in1=NL2.t[:], op0=ALU.mult,
                                                   op1=ALU.add), [NRQ.b, RM.b, NL2.b], [NEGM.b])
            P.dve(lambda e: e.tensor_scalar(out=NEGM.t[:], in0=NEGM.t[:], scalar1=-SC, scalar2=None, op0=ALU.mult), [NEGM.b], [NEGM.b])
            nkb = 2 * i + 2
            ng = (nkb + 3) // 4
            steps = [(h, g) for h in range(8) for g in range(ng)]
            nst = len(steps)

            def geo(t):
                h, g = steps[t]
                nb = min(4, nkb - 4 * g)
                return h, g, nb, nb * 128

            def st_S(t):
                h, g, nb, Wd = geo(t)
                psx = ps[t % 2]; c0 = 4 * g * 128
                kbufs = [kvb[4 * g + j] for j in range(nb)]
                P.pe(lambda e: e.matmul(psx.t[:, 0:Wd], lhsT=QLT.t[:, h * 128:(h + 1) * 128], rhs=CKVT.t[:, c0:c0 + Wd],
                                        start=True, stop=False), [QLT.b] + kbufs, [psx.b])
                pg = g % 3; kc = (g // 3) * 512
                P.pe(lambda e: e.matmul(psx.t[:, 0:Wd], lhsT=QRT.t[32 * pg:32 * pg + 32, h * 128:(h + 1) * 128],
                                        rhs=KRT.t[32 * pg:32 * pg + 32, kc:kc + Wd], start=False, stop=True), [QRT.b] + kbufs, [psx.b])

            def st_X(t):
                h, g, nb, Wd = geo(t)
                psx = ps[t % 2]; pb = PB[t % 2]
                if g == ng - 1:
                    P.dve(lambda e: e.tensor_tensor(out=SM.t[:, 0:Wd], in0=psx.t[:, 0:Wd], in1=MK4.t[:, 512 - Wd:512], op=ALU.add),
                          [psx.b, MK4.b], [SM.b])
                    src_ap, src_b = SM.t[:, 0:Wd], SM.b
                else:
                    src_ap, src_b = psx.t[:, 0:Wd], psx.b
                P.act(lambda e: e.activation(out=pb.t[:, 0:Wd], in_=src_ap, func=AF.Exp, bias=NEGM.t[:, h:h + 1], scale=SC,
                                             accum_out=LSUM.t[:, h, g:g + 1]), [src_b, NEGM.b], [pb.b, LSUM.b])

            def st_T(t):
                h, g, nb, Wd = geo(t)
                pb = PB[t % 2]; pt = PT[t % 2]; pph = ppth[t % 2]
                for j in range(nb):
                    P.pe(lambda e, j=j: e.transpose(out=pph.t[:, j * 128:(j + 1) * 128], in_=pb.t[:, j * 128:(j + 1) * 128], identity=ident),
                         [pb.b, IDB.b], [pph.b])
                P.dve(lambda e: e.tensor_copy(out=pt.t[:, 0:Wd], in_=pph.t[:, 0:Wd]), [pph.b], [pt.b])

            def st_PV(t):
                h, g, nb, Wd = geo(t)
                pt = PT[t % 2]; bank = pa[h // 4]
                for j in range(nb):
                    first = (h % 4 == 0 and g == 0 and j == 0)
                    lastm = (h % 4 == 3 and g == ng - 1 and j == nb - 1)
                    P.pe(lambda e, j=j, first=first, lastm=lastm: e.matmul(
                        bank.t[:, (h % 4) * 128:(h % 4 + 1) * 128], lhsT=pt.t[:, j * 128:(j + 1) * 128], rhs=CKVN.t[:, 4 * g + j, :],
                        start=first, stop=lastm, skip_group_check=True), [pt.b, kvb[4 * g + j]], [bank.b])

            st_S(0)
            for t in range(nst):
                if t + 1 < nst:
                    st_S(t + 1)
                st_X(t)
                st_T(t)
                if t >= 1:
                    st_PV(t - 1)
            st_PV(nst - 1)
            P.dve(lambda e: e.tensor_reduce(out=LR.t[:], in_=LSUM.t[:, :, 0:ng], axis=AX.X, op=ALU.add), [LSUM.b], [LR.b])
            P.dve(lambda e: e.reciprocal(out=LR.t[:], in_=LR.t[:]), [LR.b], [LR.b])
            for h in range(8):
                P.dve(lambda e, h=h: e.tensor_scalar(out=OLAT.t[:, h, :], in0=pa[h // 4].t[:, (h % 4) * 128:(h % 4 + 1) * 128],
                                                     scalar1=LR.t[:, h:h + 1], scalar2=None, op0=ALU.mult), [pa[h // 4].b, LR.b], [OLAT.b])
            for h in range(8):
                P.pe(lambda e, h=h: e.transpose(out=ptp.t[:, h * 128:(h + 1) * 128], in_=OLAT.t[:, h, :], identity=ident),
                     [OLAT.b, IDB.b], [ptp.b])
            P.act(lambda e: e.activation(out=OLT.t[:], in_=ptp.t[:], func=AF.Copy), [ptp.b], [OLT.b])
        else:
            decode_attention()

        for h in range(8):
            P.pe(lambda e, h=h: e.matmul(pz[0].t[:, h * 64:(h + 1) * 64], lhsT=OLT.t[:, h * 128:(h + 1) * 128], rhs=W_uv.t[:, h * 64:(h + 1) * 64],
                                         start=True, stop=True), [OLT.b, W_uv.b, WB], [pz[0].b])
        P.act(lambda e: e.activation(out=OG.t[:], in_=pz[0].t[:], func=AF.Copy), [pz[0].b], [OG.b])
        linear(pz[1], XNT, NCH, W_in, C_GM, 512)
        silu_to(CAT.t[:, 0:512], CAT.b, pz[1], OG.t[:], [OG.b], 512)
        transposes(CAT, CATT, NCH)
        for k in range(2):
            linear(pz[k], CATT, NCH, W_out, k * 512, 512)
            P.dve(lambda e, k=k: e.tensor_tensor(out=H1.t[:, k * 512:(k + 1) * 512], in0=pz[k].t[:], in1=XO.t[:, k * 512:(k + 1) * 512],
                                                 op=ALU.add), [pz[k].b, XO.b], [H1.b])
        P.act(lambda e: e.activation(out=H1B.t[:], in_=H1.t[:], func=AF.Copy), [H1.b], [H1B.b])
        transposes(H1B, H1T, NCH)
        P.act(lambda e: e.activation(out=PBF.t[:], in_=PIN.t[:], func=AF.Copy), [PIN.b], [PBF.b])
        transposes(PBF, PTT, 2)
        for k in range(2):
            linear(pz[k], H1T, NCH, W_pg, k * 512, 512)
            sg = SIG.t[:, k * 512:(k + 1) * 512]
            P.act(lambda e, k=k, sg=sg: e.activation(out=sg, in_=pz[k].t[:], func=AF.Exp, scale=-1.0), [pz[k].b], [SIG.b])
            P.dve(lambda e, sg=sg: e.tensor_scalar(out=sg, in0=sg, scalar1=1.0, scalar2=None, op0=ALU.add), [SIG.b], [SIG.b])
            P.dve(lambda e, sg=sg: e.reciprocal(out=sg, in_=sg), [SIG.b], [SIG.b])
            linear(ps[k], PTT, 2, W_pp, k * 512, 512)
            P.dve(lambda e, k=k, sg=sg: e.tensor_tensor(out=sg, in0=ps[k].t[:], in1=sg, op=ALU.mult), [ps[k].b, SIG.b], [SIG.b])
            P.dve(lambda e, k=k, sg=sg: e.tensor_tensor(out=H2.t[:, k * 512:(k + 1) * 512], in0=H1.t[:, k * 512:(k + 1) * 512], in1=sg,
                                                        op=ALU.add), [H1.b, SIG.b], [H2.b])
        rms_prep(H2)
        P.dve(lambda e: e.scalar_tensor_tensor(out=YO.t[:], in0=H2.t[:], scalar=RS.t[:, 0:1], in1=G_fin.t[:], op0=ALU.mult,
                                               op1=ALU.mult), [H2.b, RS.b, G_fin.b, WB], [YO.b])
        store(YO, YO.t[:], ydst)

    def decode_attention():
        P.barrier()
        alloc_stream2()
        names = ("MRUN", "MNEW", "NMN", "ALP", "LRUN", "LG", "GMX")
        s_ps = [ps[0], ps[1], pz[0]]
        s_pa = [pa[0], pa[1], pz[1]]

        class St:
            pass

        def mk(q):
            st = St()
            for nm, tl in zip(names, Z.SMALL[q]):
                setattr(st, nm, tl)
            st.ACC, st.ACCB = Z.ACC2[q], Z.ACCB2[q]
            st.CTS, st.KTS = Z.CTS[q], Z.KTS[q]
            st.ps, st.pb, st.pt, st.pa = s_ps[q], Z.PBS[q], Z.PTS[q], s_pa[q]
            st.ppt = Tl(ppt.t[:, q * 256:(q + 1) * 256], ppt.b)
            st.pptA = Tl(ppt.t[:, 768 + q * 64:768 + (q + 1) * 64], ppt.b)
            st.sm = Tl(SM.t[0:64, q * 128:(q + 1) * 128], Buf(f"smS{q}"))
            st.k = q
            return st

        def stream(st, seqs):
            k = st.k

            def gather(s_, g8):
                col = s_ * NG8 + g8
                P.dma("pool", lambda e: e.indirect_dma_start(
                    out=Z.PGC[k].t[:].rearrange("p a l -> p (a l)"), out_offset=None, in_=pool_c,
                    in_offset=bass.IndirectOffsetOnAxis(ap=Z.IDX.t[:, col:col + 1], axis=0)), [Z.IDX.b], [Z.PGC[k].b], Z.PGC[k].b)
                P.dma("pool", lambda e: e.indirect_dma_start(
                    out=Z.PGR[k].t[:].rearrange("p a l -> p (a l)"), out_offset=None, in_=pool_r,
                    in_offset=bass.IndirectOffsetOnAxis(ap=Z.IDX.t[:, col:col + 1], axis=0)), [Z.IDX.b], [Z.PGR[k].b], Z.PGR[k].b)

            def stage_S(s_, cT_ap, cT_b, kT_ap, kT_b, Wd):
                psx = st.ps
                P.pe(lambda e: e.matmul(psx.t[0:64, 0:Wd], lhsT=QLT.t[:, s_ * 64:(s_ + 1) * 64], rhs=cT_ap, start=True, stop=False),
                     [QLT.b, cT_b], [psx.b])
                P.pe(lambda e: e.matmul(psx.t[0:64, 0:Wd], lhsT=QRT.t[0:32, s_ * 64:(s_ + 1) * 64], rhs=kT_ap, start=False, stop=True),
                     [QRT.b, kT_b], [psx.b])

            def stage_X(Wd, mask_ap):
                psx = st.ps
                if mask_ap is not None:
                    P.dve(lambda e: e.tensor_tensor(out=st.sm.t[:, 0:Wd], in0=psx.t[0:64, 0:Wd], in1=mask_ap, op=ALU.add),
                          [psx.b, Z.DMASK.b], [st.sm.b])
                    src_ap, src_b = st.sm.t[:, 0:Wd], st.sm.b
                else:
                    src_ap, src_b = psx.t[0:64, 0:Wd], psx.b
                P.dve(lambda e: e.reduce_max(out=st.GMX.t[:], in_=src_ap, axis=AX.X), [src_b], [st.GMX.b])
                P.dve(lambda e: e.tensor_tensor(out=st.MNEW.t[:], in0=st.MRUN.t[:], in1=st.GMX.t[:], op=ALU.max), [st.MRUN.b, st.GMX.b], [st.MNEW.b])
                P.dve(lambda e: e.tensor_tensor(out=st.ALP.t[:], in0=st.MRUN.t[:], in1=st.MNEW.t[:], op=ALU.subtract),
                      [st.MRUN.b, st.MNEW.b], [st.ALP.b])
                P.act(lambda e: e.activation(out=st.ALP.t[:], in_=st.ALP.t[:], func=AF.Exp, scale=SC), [st.ALP.b], [st.ALP.b])
                P.dve(lambda e: e.tensor_scalar(out=st.NMN.t[:], in0=st.MNEW.t[:], scalar1=-SC, scalar2=None, op0=ALU.mult), [st.MNEW.b], [st.NMN.b])
                P.dve(lambda e: e.tensor_copy(out=st.MRUN.t[:], in_=st.MNEW.t[:]), [st.MNEW.b, st.ALP.b], [st.MRUN.b])
                pb = st.pb
                P.act(lambda e: e.activation(out=pb.t[0:64, 0:Wd], in_=src_ap, func=AF.Exp, bias=st.NMN.t[:, 0:1], scale=SC,
                                             accum_out=st.LG.t[:, 0:1]), [src_b, st.NMN.b], [pb.b, st.LG.b])
                P.dve(lambda e: e.scalar_tensor_tensor(out=st.LRUN.t[:], in0=st.LRUN.t[:], scalar=st.ALP.t[:, 0:1], in1=st.LG.t[:], op0=ALU.mult,
                                                       op1=ALU.add), [st.LRUN.b, st.ALP.b, st.LG.b], [st.LRUN.b])

            def stage_PT(nb):
                pb = st.pb
                for j in range(nb):
                    P.pe(lambda e, j=j: e.transpose(out=st.ppt.t[:, j * 64:(j + 1) * 64], in_=pb.t[0:64, j * 128:(j + 1) * 128],
                                                    identity=IDB.t[0:64, 0:64]), [pb.b, IDB.b], [st.ppt.b])
                P.dve(lambda e: e.tensor_copy(out=st.pt.t[:, 0:nb * 64], in_=st.ppt.t[:, 0:nb * 64]), [st.ppt.b], [st.pt.b])

            def stage_PV(nb, vsrc):
                for j in range(nb):
                    vap, vb = vsrc(j)
                    P.pe(lambda e, j=j, vap=vap: e.matmul(st.pa.t[0:64, 0:128], lhsT=st.pt.t[:, j * 64:(j + 1) * 64], rhs=vap,
                                                          start=(j == 0), stop=(j == nb - 1)), [st.pt.b, vb], [st.pa.b])
                P.dve(lambda e: e.scalar_tensor_tensor(out=st.ACC.t[:], in0=st.ACC.t[:], scalar=st.ALP.t[:, 0:1], in1=st.pa.t[0:64, 0:128],
                                                       op0=ALU.mult, op1=ALU.add), [st.ACC.b, st.ALP.b, st.pa.b], [st.ACC.b])

            gather(seqs[0], 0)
            for si, s_ in enumerate(seqs):
                P.dve(lambda e: e.memset(st.MRUN.t[:], -1.0e30), [], [st.MRUN.b])
                P.dve(lambda e: e.memset(st.LRUN.t[:], 0.0), [], [st.LRUN.b])
                P.dve(lambda e: e.memset(st.ACC.t[:], 0.0), [], [st.ACC.b])
                for g8 in range(NG8):
                    P.act(lambda e: e.activation(out=Z.PGCB[k].t[:].rearrange("p a l -> p (a l)"), in_=Z.PGC[k].t[:].rearrange("p a l -> p (a l)"),
                                                 func=AF.Copy), [Z.PGC[k].b], [Z.PGCB[k].b])
                    P.dve(lambda e: e.tensor_copy(out=Z.PGRB[k].t[:].rearrange("p a l -> p (a l)"), in_=Z.PGR[k].t[:].rearrange("p a l -> p (a l)")),
                          [Z.PGR[k].b], [Z.PGRB[k].b])
                    if g8 + 1 < NG8:
                        gather(s_, g8 + 1)
                    elif si + 1 < len(seqs):
                        gather(seqs[si + 1], 0)
                    yield
                    for half in range(2):
                        for j in range(4):
                            a_ = half * 4 + j
                            P.pe(lambda e, a_=a_, j=j: e.transpose(out=ptp.t[:, j * 128:(j + 1) * 128], in_=Z.PGCB[k].t[:, a_, :], identity=ident),
                                 [Z.PGCB[k].b, IDB.b], [ptp.b])
                            P.pe(lambda e, a_=a_, j=j: e.transpose(out=ptp.t[0:32, 512 + j * 128:512 + (j + 1) * 128], in_=Z.PGRB[k].t[:, a_, :],
                                                                   identity=ident), [Z.PGRB[k].b, IDB.b], [ptp.b])
                        P.act(lambda e: e.activation(out=st.CTS.t[:], in_=ptp.t[:, 0:512], func=AF.Copy), [ptp.b], [st.CTS.b])
                        P.act(lambda e: e.activation(out=st.KTS.t[:], in_=ptp.t[0:32, 512:1024], func=AF.Copy), [ptp.b], [st.KTS.b])
                        yield
                        stage_S(s_, st.CTS.t[:], st.CTS.b, st.KTS.t[:], st.KTS.b, 512)
                        yield
                        stage_X(512, None)
                        yield
                        stage_PT(4)
                        yield
                        stage_PV(4, lambda j, half=half: (Z.PGCB[k].t[:, half * 4 + j, :], Z.PGCB[k].b))
                        yield
                stage_S(s_, Z.CKVT_S.t[:], Z.CKVN_S.b, Z.KRT_S.t[:], Z.CKVN_S.b, 128)
                yield
                stage_X(128, Z.DMASK.t[:, s_ * 128:(s_ + 1) * 128])
                yield
                stage_PT(1)
                yield
                stage_PV(1, lambda j: (Z.CKVN_S.t[:], Z.CKVN_S.b))
                P.dve(lambda e: e.reciprocal(out=st.LG.t[:], in_=st.LRUN.t[:]), [st.LRUN.b], [st.LG.b])
                P.dve(lambda e: e.tensor_scalar(out=st.ACCB.t[:], in0=st.ACC.t[:], scalar1=st.LG.t[:, 0:1], scalar2=None, op0=ALU.mult),
                      [st.ACC.b, st.LG.b], [st.ACCB.b])
                P.pe(lambda e: e.transpose(out=st.pptA.t[:], in_=st.ACCB.t[:], identity=IDB.t[0:64, 0:64]), [st.ACCB.b, IDB.b], [st.pptA.b])
                P.dve(lambda e, s_=s_: e.tensor_copy(out=OLT.t[:].rearrange("l (h s t) -> l h s t", h=8, s=16, t=8)[:, :, s_, :],
                                                     in_=st.pptA.t[:].rearrange("l (h t) -> l h t", h=8)), [st.pptA.b], [OLT.b])
                yield

        gens = [stream(mk(0), list(range(0, 6))), stream(mk(1), list(range(6, 11))), stream(mk(2), list(range(11, 16)))]
        alive = [True, True, True]
        for _ in range(4):
            next(gens[0])
        for _ in range(2):
            next(gens[1])
        while any(alive):
            for q in range(3):
                if alive[q]:
                    try:
                        next(gens[q])
                    except StopIteration:
                        alive[q] = False

    P.barrier()
    init_consts()
    for i in range(NS):
        shared(2 * i)
        shared(2 * i + 1)
        own(i, False)
    fs = ST[NKB % 3]
    P.dma("pool", lambda e: e.dma_start(out=stp_o.rearrange("h k v -> k h v"), in_=fs.t[:]), [fs.b], [], fs.b, is_out=True)
    P.barrier()
    es_p.close()
    alloc_sample()
    P.barrier()
    P.dve(lambda e: e.tensor_scalar(out=Z.IDX.t[:], in0=Z.PTL.t[:], scalar1=16.0, scalar2=Z.PM16.t[:, 0:1],
                                    op0=ALU.mult, op1=ALU.add), [Z.PTL.b, Z.PM16.b], [Z.IDX.b])
    own(0, True)

    P.finalize()
    with nc.Block() as block:
        P.emit(block)
    es.close()
    return nc


def _consts():
    import ml_dtypes
    t = np.arange(128)
    c = {}
    su = (t[:, None] <= t[None, :])
    sq = (t[:, None] // 8 == t[None, :] // 8)
    g = np.zeros((128, 768), np.float32)
    g[:, 0:128] = np.where(su, -1.0 / 16, 0.0)
    g[:, 128:256] = np.where(~su, -1.0 / 16, 0.0)
    g[:, 256:384] = su
    g[:, 384:512] = np.where(su & sq, -1.0 / 16, 0.0)
    g[:, 512:640] = np.where((~su) & sq, -1.0 / 16, 0.0)
    g[:, 640:768] = su & sq
    c["gla_c"] = g
    seq = t // 8
    sx = np.zeros((128, 26), np.float32)
    sx[:, 0] = (seq % 2 == 0); sx[:, 1] = (seq % 2 == 1)
    for p in range(8):
        sx[:, 2 + p] = np.where(seq // 2 == p, -1.0 / 16, 0.0)
    for i in range(16):
        sx[:, 10 + i] = (seq == i)
    c["smp_x"] = sx
    m2 = np.zeros((128, 8, 128), np.float32)
    for il in range(2):
        for p in range(8):
            m2[il * 64:(il + 1) * 64, p, :] = (seq == 2 * p + il)[None, :]
    c["m2"] = m2.reshape(128, 1024)
    dm = np.full((64, 16, 128), NEG, np.float32)
    for s in range(16):
        for tq in range(8):
            for h in range(8):
                dm[h * 8 + tq, s, 8 * s:8 * s + tq + 1] = 0.0
    c["dmask"] = dm.reshape(64, 2048)
    c["ident_b"] = np.eye(128, dtype=np.float32)
    c["ident_f"] = np.eye(128, dtype=np.float32)
    c["pm16"] = (t % 16).astype(np.int32).reshape(128, 1)
    return c


def _cs_table(pos):
    inv = (1.0 / (np.float32(10000.0) ** (np.arange(0, 32, 2, dtype=np.float32) / np.float32(32)))).astype(np.float32)
    ang = (pos.astype(np.float32)[:, None] * inv[None, :]).astype(np.float32)
    co, si = np.cos(ang).astype(np.float32), np.sin(ang).astype(np.float32)
    return np.concatenate([co, co, -si, si], axis=1).astype(np.float32)


_NC_CACHE = {}


def kernel(x_prompt, x_sample, p_prompt, p_sample, cache_ckv, cache_krope, state_gla, page_table,
           g_mix_norm, w_in, g_qnorm, w_qup, g_kvnorm, w_uk, w_uv, w_gla_a2, b_gla_a, g_gla_onorm,
           w_out, w_ple_gate, w_ple_proj, g_final):
    f = lambda a: np.ascontiguousarray(np.asarray(a))
    x_prompt, x_sample, p_prompt, p_sample = f(x_prompt), f(x_sample), f(p_prompt), f(p_sample)
    cache_ckv, cache_krope, state_gla, page_table = f(cache_ckv), f(cache_krope), f(state_gla), f(page_table)
    B, S, _ = x_prompt.shape
    BD, TD, _ = x_sample.shape
    NPG = page_table.shape[1]
    NPOOL = cache_ckv.shape[1]
    n_cores = 2 * B
    assert BD == 16 * n_cores and TD == 8 and NPG % 8 == 0 and S % 256 == 0
    NS, NG8 = S // 256, NPG // 8
    past = NPG * cache_ckv.shape[2]
    key = (NS, NG8, NPOOL)
    if key not in _NC_CACHE:
        _NC_CACHE[key] = build(NS, NG8, NPOOL)
    nc = _NC_CACHE[key]

    w_in0 = f(w_in)[0]
    bnd = np.cumsum([0, 256, 160, 512, 256, 256, 512, 16, 512])
    qc, kv, gm, qg, kg, vg, ag, gg = [w_in0[:, bnd[j]:bnd[j + 1]] for j in range(8)]
    w_in_r = f(np.concatenate([kv, kg, vg, ag, qc, gm, qg, gg], axis=1))
    wq = f(w_qup)[0].reshape(256, 8, 96)
    w_qup_r = f(np.concatenate([wq[:, :, :64].reshape(256, 512), wq[:, :, 64:].reshape(256, 256)], axis=1))
    w_ukT = f(np.transpose(f(w_uk)[0], (2, 1, 0)).reshape(64, 1024))
    w_uv2 = f(f(w_uv)[0].reshape(128, 512))
    common = dict(w_in_r=w_in_r, w_qup_r=w_qup_r, w_ukT=w_ukT, w_uv=w_uv2, w_a2=f(w_gla_a2)[0], w_out=f(w_out)[0],
                  w_pg=f(w_ple_gate)[0], w_pp=f(w_ple_proj)[0], g_mix=f(g_mix_norm), g_q=f(g_qnorm), g_kv=f(g_kvnorm),
                  g_on=f(g_gla_onorm), b_a=f(b_gla_a), g_fin=f(g_final).reshape(1, -1),
                  pool_c=cache_ckv[0].reshape(NPOOL * 16, 1024), pool_r=cache_krope[0].reshape(NPOOL * 16, 256))
    common.update(_consts())
    cs_all = _cs_table(np.arange(S))
    cs_smp = _cs_table(past + (np.arange(128) % 8))
    tri = np.where(np.arange(128)[:, None] >= np.arange(128)[None, :], 0.0, NEG).astype(np.float32)
    in_maps = []
    for c in range(n_cores):
        b, r = c // 2, c % 2
        xb = x_prompt[b].reshape(NS, 2, 128, D)
        m = dict(common)
        m["x_all"] = x_prompt[b]
        m["x_own"] = f(xb[:, r].reshape(NS * 128, D))
        m["p_own"] = f(p_prompt[0, b].reshape(NS, 2, 128, 256)[:, r].reshape(NS * 128, 256))
        m["cs_all"] = cs_all
        m["cs_own"] = f(cs_all.reshape(NS, 2, 128, 64)[:, r].reshape(NS * 128, 64))
        m["x_smp"] = f(x_sample[16 * c:16 * c + 16].reshape(128, D))
        m["p_smp"] = f(p_sample[0, 16 * c:16 * c + 16].reshape(128, 256))
        m["cs_smp"] = cs_smp
        mk4 = np.zeros((128, 512), np.float32)
        if r == 0:
            mk4[:, 256:384] = tri; mk4[:, 384:512] = NEG
        else:
            mk4[:, 384:512] = tri
        m["mk4"] = mk4
        par = np.zeros((128, 2), np.float32); par[:, 0] = 1 - r; par[:, 1] = r
        m["par"] = par
        pt = page_table[16 * c:16 * c + 16].reshape(16, NG8, 8)
        ptl = np.repeat(np.transpose(pt, (2, 0, 1)).reshape(8, 16 * NG8), 16, axis=0)
        m["ptl"] = f(ptl.astype(np.int32))
        m["st_in"] = f(state_gla[0, 16 * c:16 * c + 16])
        in_maps.append(m)
    res = run_bass_kernel_spmd(nc, in_maps, core_ids=list(range(n_cores)))
    R = res.results
    y_p = np.zeros((B, S, D), np.float32)
    ckv_p = np.zeros((1, B, S, 128), np.float32); kr_p = np.zeros((1, B, S, 32), np.float32)
    st_p = np.zeros((1, B, 4, 64, 128), np.float32)
    y_s = np.zeros((BD, TD, D), np.float32); ckv_s = np.zeros((1, BD, TD, 128), np.float32)
    kr_s = np.zeros((1, BD, TD, 32), np.float32); st_s = np.zeros((1, BD, 4, 64, 128), np.float32)
    for c in range(n_cores):
        b, r = c // 2, c % 2
        y_p[b].reshape(NS, 2, 128, D)[:, r] = R[c]["y_own"].reshape(NS, 128, D)
        if r == 0:
            ckv_p[0, b] = R[c]["ckv_o"]; kr_p[0, b] = R[c]["kr_o"]; st_p[0, b] = R[c]["stp_o"]
        y_s[16 * c:16 * c + 16] = R[c]["y_smp"].reshape(16, 8, D)
        ckv_s[0, 16 * c:16 * c + 16] = R[c]["ckvs_o"].reshape(16, 8, 128)
        kr_s[0, 16 * c:16 * c + 16] = R[c]["krs_o"].reshape(16, 8, 32)
        st_s[0, 16 * c:16 * c + 16] = R[c]["sts_o"]
    return (y_p, y_s, ckv_p, kr_p, st_p, ckv_s, kr_s, st_s)
```
